# Optimizing a Trainium2 kernel written in Bass

```python
import math
import jax
import jax.numpy as jnp
from jax import lax
import numpy as np

D_MODEL = 1024
BATCH = 8
SEQ = 4096
DEPTH = 4

HEAD_DIM = 64
N_MIX_HEADS = D_MODEL // HEAD_DIM
N_GROUPS = 4
HEADS_PER_GROUP = N_MIX_HEADS // N_GROUPS
GROUP_WIDTH = HEADS_PER_GROUP * HEAD_DIM
Q_BLOCK = 128
NEG = -1e30

REL_BUCKETS = 32
REL_MAX_DIST = 128

IDX_HEADS = 8
IDX_DIM = 64
DSA_TOPK = 256

CMP_LEN = 32
CMP_STRIDE = 16
SLC_LEN = 64
SLC_TOPN = 16
WIN = 512

MOBA_BLOCK = 256
MOBA_TOPK = 3
MOBA_Q_BLOCK = 32

DIFF_HALF = HEAD_DIM // 2

MEM_LEN = 256
CROSS_HEADS = 4
CROSS_DIM = D_MODEL // CROSS_HEADS
D_FF = 4 * D_MODEL

DEEPNORM_ALPHA = (2 * DEPTH) ** 0.25
DEEPNORM_BETA = (8 * DEPTH) ** -0.25

G = GROUP_WIDTH
SPLIT_SIZES = (
    G, G, G, IDX_HEADS * IDX_DIM, IDX_DIM, IDX_HEADS,
    G, HEAD_DIM, HEAD_DIM, HEAD_DIM, HEAD_DIM, HEAD_DIM, HEAD_DIM,
    3 * HEADS_PER_GROUP,
    G, G, G,
    G, G, G,
)
IN_COLS = sum(SPLIT_SIZES)

kernel_name = 'hybrid_dsa_nsa_moba_diff_trunk'


def split_cols(h):
    offs = np.cumsum(SPLIT_SIZES)[:-1].tolist()
    return jnp.split(h, offs, axis=-1)


def heads(t, n):
    return t.reshape(t.shape[0], t.shape[1], n, -1)


def gather_rows(src, idx):
    return jax.vmap(lambda s, i: s[i])(src, idx)


def layer_norm(x, g, b, eps=1e-5):
    xf = x.astype(jnp.float32)
    mu = xf.mean(-1, keepdims=True)
    var = jnp.square(xf - mu).mean(-1, keepdims=True)
    return ((xf - mu) * lax.rsqrt(var + eps) * g + b).astype(x.dtype)


def rms_norm(x, g, eps=1e-6):
    xf = x.astype(jnp.float32)
    return (xf * lax.rsqrt(jnp.mean(xf * xf, -1, keepdims=True) + eps) * g).astype(x.dtype)


def masked_softmax(logits, mask):
    logits = jnp.where(mask, logits.astype(jnp.float32), NEG)
    return jax.nn.softmax(logits, axis=-1) * mask


def rel_bucket(dist):
    n = jnp.maximum(dist, 0)
    max_exact = REL_BUCKETS // 2
    nf = jnp.maximum(n, max_exact).astype(jnp.float32)
    large = max_exact + (jnp.log(nf / max_exact) / math.log(REL_MAX_DIST / max_exact)
                         * (REL_BUCKETS - max_exact)).astype(jnp.int32)
    large = jnp.minimum(large, REL_BUCKETS - 1)
    return jnp.where(n < max_exact, n, large)


def sweep(fn, L, blk):
    starts = jnp.arange(L // blk, dtype=jnp.int32) * blk
    out = lax.map(fn, starts)
    nb, B, _, H, d = out.shape
    return out.transpose(1, 0, 2, 3, 4).reshape(B, L, H * d)


def dsa_attention(q, k, v, q_idx, k_idx, w_idx, rel_tab):
    B, L, H, Dh = q.shape
    topk = min(DSA_TOPK, L // 4)
    key_pos = jnp.arange(L, dtype=jnp.int32)

    def block(q0):
        qb = lax.dynamic_slice_in_dim(q, q0, Q_BLOCK, axis=1)
        qib = lax.dynamic_slice_in_dim(q_idx, q0, Q_BLOCK, axis=1)
        wib = lax.dynamic_slice_in_dim(w_idx, q0, Q_BLOCK, axis=1)
        qpos = q0 + jnp.arange(Q_BLOCK, dtype=jnp.int32)
        rel = jax.nn.relu(jnp.einsum('bqhd,bsd->bhqs', qib, k_idx) * IDX_DIM ** -0.5)
        score = jnp.einsum('bqh,bhqs->bqs', wib, rel).astype(jnp.float32) * IDX_HEADS ** -0.5
        score = jnp.where(key_pos[None, None, :] <= qpos[None, :, None], score, -jnp.inf)
        _, sel = lax.top_k(score, topk)
        kg = gather_rows(k, sel)
        vg = gather_rows(v, sel)
        dist = qpos[None, :, None] - sel
        bias = rel_tab[rel_bucket(dist)].transpose(0, 3, 1, 2)
        logits = jnp.einsum('bqhd,bqkhd->bhqk', qb, kg) * Dh ** -0.5 + bias
        p = masked_softmax(logits, (dist >= 0)[:, None])
        return jnp.einsum('bhqk,bqkhd->bqhd', p.astype(v.dtype), vg)

    return sweep(block, L, Q_BLOCK)


def nsa_attention(q, k_cmp_raw, v_cmp_raw, k_slc, v_slc, k_win, v_win, gate_logits,
                  pos_k, pos_v, w1_k, w2_k, w1_v, w2_v, rel_tab):
    B, L, H, Dh = q.shape
    n_cmp = (L - CMP_LEN) // CMP_STRIDE + 1
    cmp_start = np.arange(n_cmp) * CMP_STRIDE
    cmp_idx = cmp_start[:, None] + np.arange(CMP_LEN)[None, :]
    cmp_end = jnp.asarray(cmp_start + CMP_LEN - 1, jnp.int32)

    def compress(t, pos, w1, w2):
        blocks = (t[:, cmp_idx] + pos).reshape(B, n_cmp, CMP_LEN * Dh)
        return jax.nn.gelu(blocks @ w1) @ w2

    kc = compress(k_cmp_raw, pos_k, w1_k, w2_k)
    vc = compress(v_cmp_raw, pos_v, w1_v, w2_v)
    n_slc = L // SLC_LEN
    topn = min(SLC_TOPN, n_slc)
    slc_start = np.arange(n_slc) * SLC_LEN
    overlap = jnp.asarray(((cmp_start[:, None] <= slc_start[None, :] + SLC_LEN - 1)
                           & (cmp_start[:, None] + CMP_LEN - 1 >= slc_start[None, :])).astype(np.float32))
    slc_start_j = jnp.asarray(slc_start, jnp.int32)
    slc_ids = jnp.arange(n_slc, dtype=jnp.int32)
    slc_offs = jnp.arange(SLC_LEN, dtype=jnp.int32)
    k_win_pad = jnp.pad(k_win, ((0, 0), (WIN, 0), (0, 0)))
    v_win_pad = jnp.pad(v_win, ((0, 0), (WIN, 0), (0, 0)))
    win_offs = jnp.arange(WIN + Q_BLOCK, dtype=jnp.int32) - WIN
    gates = jax.nn.sigmoid(gate_logits.astype(jnp.float32)).astype(q.dtype).reshape(B, L, H, 3)
    scale = Dh ** -0.5

    def block(q0):
        qb = lax.dynamic_slice_in_dim(q, q0, Q_BLOCK, axis=1)
        gb = lax.dynamic_slice_in_dim(gates, q0, Q_BLOCK, axis=1)
        qpos = q0 + jnp.arange(Q_BLOCK, dtype=jnp.int32)
        lc = jnp.einsum('bqhd,bnd->bhqn', qb, kc) * scale
        pc = masked_softmax(lc, cmp_end[None, :] <= qpos[:, None])
        o_cmp = jnp.einsum('bhqn,bnd->bqhd', pc.astype(vc.dtype), vc)
        imp = jnp.einsum('bhqn,nj->bqj', pc, overlap)
        cur = qpos // SLC_LEN
        forced = ((slc_ids[None, :] == 0) | (slc_ids[None, :] == cur[:, None])
                  | (slc_ids[None, :] == cur[:, None] - 1))
        imp = jnp.where(forced, jnp.inf, imp)
        imp = jnp.where(slc_start_j[None, :] <= qpos[:, None], imp, -jnp.inf)
        _, sel = lax.top_k(imp, topn)
        tok = (sel[..., None] * SLC_LEN + slc_offs).reshape(B, Q_BLOCK, topn * SLC_LEN)
        ksg = gather_rows(k_slc, tok)
        vsg = gather_rows(v_slc, tok)
        dist_s = qpos[None, :, None] - tok
        ls = (jnp.einsum('bqhd,bqnd->bhqn', qb, ksg) * scale
              + rel_tab[rel_bucket(dist_s)].transpose(0, 3, 1, 2))
        ps = masked_softmax(ls, (dist_s >= 0)[:, None])
        o_slc = jnp.einsum('bhqn,bqnd->bqhd', ps.astype(vsg.dtype), vsg)
        kwb = lax.dynamic_slice_in_dim(k_win_pad, q0, WIN + Q_BLOCK, axis=1)
        vwb = lax.dynamic_slice_in_dim(v_win_pad, q0, WIN + Q_BLOCK, axis=1)
        kpos = q0 + win_offs
        dist_w = qpos[:, None] - kpos[None, :]
        mask_w = (dist_w >= 0) & (dist_w < WIN) & (kpos[None, :] >= 0)
        lw = (jnp.einsum('bqhd,bkd->bhqk', qb, kwb) * scale
              + rel_tab[rel_bucket(dist_w)].transpose(2, 0, 1)[None])
        pw = masked_softmax(lw, mask_w)
        o_win = jnp.einsum('bhqk,bkd->bqhd', pw.astype(vwb.dtype), vwb)
        return gb[..., 0:1] * o_cmp + gb[..., 1:2] * o_slc + gb[..., 2:3] * o_win

    return sweep(block, L, Q_BLOCK)


def moba_attention(q, k, v, rel_tab):
    B, L, H, Dh = q.shape
    nb = -(-L // MOBA_BLOCK)
    pad = nb * MOBA_BLOCK - L
    k_pad = jnp.pad(k, ((0, 0), (0, pad), (0, 0), (0, 0)))
    v_pad = jnp.pad(v, ((0, 0), (0, pad), (0, 0), (0, 0)))
    k_blk = k_pad.reshape(B, nb, MOBA_BLOCK, H, Dh)
    k_mean = k_blk.mean(axis=2)
    k_bh = k_blk.transpose(0, 3, 1, 2, 4)
    v_bh = v_pad.reshape(B, nb, MOBA_BLOCK, H, Dh).transpose(0, 3, 1, 2, 4)
    topk = min(MOBA_TOPK, nb - 1)
    offs = jnp.arange(MOBA_BLOCK, dtype=jnp.int32)
    blk_ids = jnp.arange(nb, dtype=jnp.int32)
    h_ids = jnp.arange(H)[None, :, None, None, None]
    tab_t = rel_tab.T
    scale = Dh ** -0.5
    gather2 = jax.vmap(jax.vmap(lambda s, i: s[i]))

    def block(q0):
        qb = lax.dynamic_slice_in_dim(q, q0, MOBA_Q_BLOCK, axis=1)
        qpos = q0 + jnp.arange(MOBA_Q_BLOCK, dtype=jnp.int32)
        cur = q0 // MOBA_BLOCK
        ko = lax.dynamic_slice_in_dim(k_pad, cur * MOBA_BLOCK, MOBA_BLOCK, axis=1)
        vo = lax.dynamic_slice_in_dim(v_pad, cur * MOBA_BLOCK, MOBA_BLOCK, axis=1)
        dist_o = qpos[:, None] - (cur * MOBA_BLOCK + offs)[None, :]
        lo = (jnp.einsum('bqhd,bkhd->bhqk', qb, ko) * scale
              + rel_tab[rel_bucket(dist_o)].transpose(2, 0, 1)[None])
        mask_o = jnp.broadcast_to(dist_o >= 0, lo.shape)
        if topk == 0:
            p = masked_softmax(lo, mask_o).astype(v.dtype)
            return jnp.einsum('bhqk,bkhd->bqhd', p, vo)
        gate = jnp.einsum('bqhd,bnhd->bhqn', qb, k_mean).astype(jnp.float32)
        gate = jnp.where(blk_ids < cur, gate, -jnp.inf)
        gval, sel = lax.top_k(gate, topk)
        kg = gather2(k_bh, sel)
        vg = gather2(v_bh, sel)
        dist_p = qpos[None, None, :, None, None] - (sel[..., None] * MOBA_BLOCK + offs)
        lp = (jnp.einsum('bqhd,bhqjkd->bhqjk', qb, kg) * scale
              + tab_t[h_ids, rel_bucket(dist_p)])
        mask_p = jnp.broadcast_to(jnp.isfinite(gval)[..., None], lp.shape)
        n_p = topk * MOBA_BLOCK
        logits = jnp.concatenate([lo, lp.reshape(B, H, MOBA_Q_BLOCK, n_p)], axis=-1)
        mask = jnp.concatenate([mask_o, mask_p.reshape(B, H, MOBA_Q_BLOCK, n_p)], axis=-1)
        p = masked_softmax(logits, mask).astype(v.dtype)
        po = p[..., :MOBA_BLOCK]
        pp = p[..., MOBA_BLOCK:].reshape(B, H, MOBA_Q_BLOCK, topk, MOBA_BLOCK)
        return (jnp.einsum('bhqk,bkhd->bqhd', po, vo)
                + jnp.einsum('bhqjk,bhqjkd->bqhd', pp, vg))

    return sweep(block, L, MOBA_Q_BLOCK)


def diff_attention(q, k, v, lam, lam_init, norm_g, rel_tab):
    B, L, H, Dh = v.shape
    q2 = q.reshape(B, L, H, 2, DIFF_HALF)
    k2 = k.reshape(B, L, H, 2, DIFF_HALF)
    key_pos = jnp.arange(L, dtype=jnp.int32)

    def block(q0):
        qb = lax.dynamic_slice_in_dim(q2, q0, Q_BLOCK, axis=1)
        qpos = q0 + jnp.arange(Q_BLOCK, dtype=jnp.int32)
        dist = qpos[:, None] - key_pos[None, :]
        bias = rel_tab[rel_bucket(dist)].transpose(2, 0, 1)
        logits = (jnp.einsum('bqhcd,bkhcd->bhcqk', qb, k2) * DIFF_HALF ** -0.5
                  + bias[None, :, None])
        p = masked_softmax(logits, dist >= 0)
        a = p[:, :, 0] - lam * p[:, :, 1]
        return jnp.einsum('bhqk,bkhd->bqhd', a.astype(v.dtype), v)

    o = sweep(block, L, Q_BLOCK).reshape(B, L, H, Dh)
    o = rms_norm(o, norm_g) * (1.0 - lam_init)
    return o.reshape(B, L, H * Dh)


def cross_attention(x, mem, wq, wk, wv, wo):
    B, L, _ = x.shape
    q = (x @ wq).reshape(B, L, CROSS_HEADS, CROSS_DIM)
    k = (mem @ wk).reshape(B, -1, CROSS_HEADS, CROSS_DIM)
    v = (mem @ wv).reshape(B, -1, CROSS_HEADS, CROSS_DIM)
    logits = jnp.einsum('bqhd,bmhd->bhqm', q, k).astype(jnp.float32) * CROSS_DIM ** -0.5
    p = jax.nn.softmax(logits, axis=-1).astype(v.dtype)
    o = jnp.einsum('bhqm,bmhd->bqhd', p, v).reshape(B, L, D_MODEL)
    return o @ wo


def squared_relu_mlp(x, w1, w2):
    return jnp.square(jax.nn.relu(x @ w1)) @ w2


def setup_inputs(seed: int = 0) -> dict:
    key = jax.random.key(seed)
    ks = jax.random.split(key, 32)

    def nrm(k, shape, scale):
        return jax.random.normal(k, shape, jnp.float32) * scale

    D = D_MODEL
    return {
        'x': nrm(ks[0], (BATCH, SEQ, D), 1.0),
        'mem': nrm(ks[1], (BATCH, MEM_LEN, D), 1.0),
        'rel_bias': nrm(ks[2], (REL_BUCKETS, N_MIX_HEADS), 0.2),
        'w_in': nrm(ks[3], (DEPTH, D, IN_COLS), D ** -0.5),
        'w_out': nrm(ks[4], (DEPTH, D, D), D ** -0.5 * DEEPNORM_BETA),
        'nsa_pos_k': nrm(ks[5], (DEPTH, CMP_LEN, HEAD_DIM), 0.1),
        'nsa_pos_v': nrm(ks[6], (DEPTH, CMP_LEN, HEAD_DIM), 0.1),
        'nsa_w1_k': nrm(ks[7], (DEPTH, CMP_LEN * HEAD_DIM, HEAD_DIM), (CMP_LEN * HEAD_DIM) ** -0.5),
        'nsa_w2_k': nrm(ks[8], (DEPTH, HEAD_DIM, HEAD_DIM), HEAD_DIM ** -0.5),
        'nsa_w1_v': nrm(ks[9], (DEPTH, CMP_LEN * HEAD_DIM, HEAD_DIM), (CMP_LEN * HEAD_DIM) ** -0.5),
        'nsa_w2_v': nrm(ks[10], (DEPTH, HEAD_DIM, HEAD_DIM), HEAD_DIM ** -0.5),
        'diff_lq1': nrm(ks[11], (DEPTH, DIFF_HALF), 0.1),
        'diff_lk1': nrm(ks[12], (DEPTH, DIFF_HALF), 0.1),
        'diff_lq2': nrm(ks[13], (DEPTH, DIFF_HALF), 0.1),
        'diff_lk2': nrm(ks[14], (DEPTH, DIFF_HALF), 0.1),
        'diff_g': 1.0 + nrm(ks[15], (DEPTH, HEAD_DIM), 0.01),
        'ln1_g': 1.0 + nrm(ks[16], (DEPTH, D), 0.01),
        'ln1_b': nrm(ks[17], (DEPTH, D), 0.01),
        'xq': nrm(ks[18], (DEPTH, D, D), D ** -0.5),
        'xk': nrm(ks[19], (DEPTH, D, D), D ** -0.5),
        'xv': nrm(ks[20], (DEPTH, D, D), D ** -0.5),
        'xo': nrm(ks[21], (DEPTH, D, D), D ** -0.5 * DEEPNORM_BETA),
        'ln2_g': 1.0 + nrm(ks[22], (DEPTH, D), 0.01),
        'ln2_b': nrm(ks[23], (DEPTH, D), 0.01),
        'mlp_w1': nrm(ks[24], (DEPTH, D, D_FF), D ** -0.5),
        'mlp_w2': nrm(ks[25], (DEPTH, D_FF, D), D_FF ** -0.5 * DEEPNORM_BETA),
        'ln3_g': 1.0 + nrm(ks[26], (DEPTH, D), 0.01),
        'ln3_b': nrm(ks[27], (DEPTH, D), 0.01),
    }


def reference(x, mem, rel_bias, w_in, w_out, nsa_pos_k, nsa_pos_v, nsa_w1_k, nsa_w2_k,
              nsa_w1_v, nsa_w2_v, diff_lq1, diff_lk1, diff_lq2, diff_lk2, diff_g,
              ln1_g, ln1_b, xq, xk, xv, xo, ln2_g, ln2_b, mlp_w1, mlp_w2, ln3_g, ln3_b):
    HG = HEADS_PER_GROUP
    tabs = [rel_bias[:, g * HG:(g + 1) * HG] for g in range(N_GROUPS)]
    for l in range(DEPTH):
        h = x @ w_in[l]
        (a_q, a_k, a_v, a_qi, a_ki, a_w,
         b_q, b_kc, b_vc, b_ks, b_vs, b_kw, b_vw, b_g,
         c_q, c_k, c_v, d_q, d_k, d_v) = split_cols(h)
        o_a = dsa_attention(heads(a_q, HG), heads(a_k, HG), heads(a_v, HG),
                            heads(a_qi, IDX_HEADS), a_ki, a_w, tabs[0])
        o_b = nsa_attention(heads(b_q, HG), b_kc, b_vc, b_ks, b_vs, b_kw, b_vw, b_g,
                            nsa_pos_k[l], nsa_pos_v[l], nsa_w1_k[l], nsa_w2_k[l],
                            nsa_w1_v[l], nsa_w2_v[l], tabs[1])
        o_c = moba_attention(heads(c_q, HG), heads(c_k, HG), heads(c_v, HG), tabs[2])
        lam_init = 0.8 - 0.6 * math.exp(-0.3 * l)
        lam = (jnp.exp(jnp.sum(diff_lq1[l].astype(jnp.float32) * diff_lk1[l]))
               - jnp.exp(jnp.sum(diff_lq2[l].astype(jnp.float32) * diff_lk2[l])) + lam_init)
        o_d = diff_attention(heads(d_q, HG), heads(d_k, HG), heads(d_v, HG), lam, lam_init,
                             diff_g[l], tabs[3])
        mix = jnp.concatenate([o_a, o_b, o_c, o_d], axis=-1).astype(x.dtype) @ w_out[l]
        x = layer_norm(DEEPNORM_ALPHA * x + mix, ln1_g[l], ln1_b[l])
        x = layer_norm(DEEPNORM_ALPHA * x + cross_attention(x, mem, xq[l], xk[l], xv[l], xo[l]),
                       ln2_g[l], ln2_b[l])
        x = layer_norm(DEEPNORM_ALPHA * x + squared_relu_mlp(x, mlp_w1[l], mlp_w2[l]),
                       ln3_g[l], ln3_b[l])
    return x
```

```python
import math
from contextlib import ExitStack
import numpy as np
import concourse.bass as bass
import concourse.mybir as mybir
from concourse.bass_utils import run_bass_kernel_spmd

F32 = mybir.dt.float32
BF16 = mybir.dt.bfloat16
AF = mybir.ActivationFunctionType
ALU = mybir.AluOpType
AX = mybir.AxisListType

D = 1024
DEPTH_FULL = 4
ALPHA = (2 * DEPTH_FULL) ** 0.25
NEGM = -30000.0
REL_BUCKETS = 32

O_AQ, O_AK, O_AV, O_AQI, O_AKI, O_AW = 0, 256, 512, 768, 1280, 1344
O_BQ, O_BKC, O_BVC, O_BKS, O_BVS, O_BKW, O_BVW, O_BG = 1352, 1608, 1672, 1736, 1800, 1864, 1928, 1992
O_CQ, O_CK, O_CV, O_DQ, O_DK, O_DV = 2004, 2260, 2516, 2772, 3028, 3284


def _r(a, n):
    return list(range(a, a + n))


def _diffcols(o):
    cols = []
    for ch in range(3):
        for p in range(4):
            mi = min(ch * 3 + min(p, 2), 7)
            cols += _r(o + mi * 32, 32)
    return cols


FM_COLS = (_r(O_AQ, 256) + _r(O_AK, 256) + _r(O_AQI, 512) + _r(O_AKI, 64) * 2
           + _r(O_BQ, 256) + _r(O_BKC, 64) + _r(O_BVC, 64) + _r(O_BKS, 64) * 2 + _r(O_BKW, 64) * 2
           + _r(O_CQ, 256) + _r(O_CK, 256) + _diffcols(O_DQ) + _diffcols(O_DK))
NFM = len(FM_COLS) // 128
C_AQ, C_AK, C_QI, C_KI, C_BQ, C_KCVC, C_KS, C_KW, C_CQ, C_CK, C_DQ, C_DK = 0, 2, 4, 8, 9, 11, 12, 13, 14, 16, 18, 21
TM_COLS = (_r(O_AV, 256) + _r(O_CV, 256)
           + _r(O_DV, 256) + _r(O_BVS, 64) + _r(O_BVW, 64) + _r(O_AW, 8) + _r(O_BG, 12))
NTMA, NTMB = 512, 404
NWIN = len(FM_COLS) + len(TM_COLS)


def rel_bucket_np(n):
    n = np.maximum(n, 0)
    me = REL_BUCKETS // 2
    nf = np.maximum(n, me).astype(np.float32)
    large = me + (np.log(nf / np.float32(me)) / np.float32(math.log(128 / me)) * (REL_BUCKETS - me)).astype(np.int32)
    large = np.minimum(large, REL_BUCKETS - 1)
    return np.where(n < me, n, large)


class Res:
    __slots__ = ("w", "r", "ps")

    def __init__(self):
        self.w = None
        self.r = {}
        self.ps = False


class Tile:
    __slots__ = ("t", "res", "sub")

    def __init__(self, t):
        self.t = t
        self.res = Res()
        self.sub = None

    def __getitem__(self, idx):
        return self.t[idx]


class Sched:
    def __init__(self, nc, stack):
        self.nc = nc
        self.engs = {"pe": nc.tensor, "act": nc.scalar, "dve": nc.vector, "pool": nc.gpsimd, "sp": nc.sync}
        self.sem = {k: stack.enter_context(nc.semaphore("s_" + k)) for k in self.engs}
        self.cnt = {k: 0 for k in self.engs}
        self.seen = {k: {} for k in self.engs}
        self.R = 10
        self.dq = {q: dict(sems=[stack.enter_context(nc.semaphore("d_%s%d" % (q, i))) for i in range(self.R)], n=0)
                   for q in ("sp", "pool", "poolc")}
        self.engs["poolc"] = nc.gpsimd
        self.seen["poolc"] = self.seen["pool"]
        self.ninst = 0

    def _wait(self, e, tok):
        key, h, v = tok
        if self.seen[e].get(key, 0) >= v:
            return
        self.engs[e].wait_ge(h, v)
        self.seen[e][key] = v

    def _deps(self, e, reads, writes):
        for b in reads:
            if b.w is not None:
                if not (e == "pe" and b.w[0] == "pe"):
                    self._wait(e, b.w)
        for b in writes:
            if b.w is not None:
                if not (e == "pe" and b.w[0] == "pe"):
                    self._wait(e, b.w)
            for t in list(b.r.values()):
                if not (e == "pe" and t[0] == "pe"):
                    self._wait(e, t)

    def _commit(self, tok, reads, writes):
        for b in writes:
            b.w = tok
            b.r = {}
        for b in reads:
            b.r[tok[0]] = tok

    def op(self, e, fn, reads=(), writes=()):
        reads = [x.res if isinstance(x, Tile) else x for x in reads]
        writes = [x.res if isinstance(x, Tile) else x for x in writes]
        if e != "pe":
            writes = writes + [x for x in reads if x.ps and x not in writes]
        self._deps(e, reads, writes)
        ins = fn(self.engs[e])
        self.cnt[e] += 1
        ins.then_inc(self.sem[e], 1)
        self._commit((e, self.sem[e], self.cnt[e]), reads, writes)
        self.ninst += 1

    def dma(self, q, out, in_, reads=(), writes=()):
        reads = [x.res if isinstance(x, Tile) else x for x in reads]
        writes = [x.res if isinstance(x, Tile) else x for x in writes]
        d = self.dq[q]
        i = d["n"] % self.R
        k = d["n"] // self.R
        d["n"] += 1
        key = "d_%s%d" % (q, i)
        h = d["sems"][i]
        if k > 0:
            self._wait(q, (key, h, 16 * k))
        self._deps(q, reads, writes)
        self.engs[q].dma_start(out=out, in_=in_).then_inc(h, 16)
        self._commit((key, h, 16 * (k + 1)), reads, writes)
        self.ninst += 1

    def barrier(self):
        for q, d in self.dq.items():
            if q == "poolc":
                continue
            n = d["n"]
            for i in range(self.R):
                if n > i:
                    last = ((n - 1 - i) // self.R) + 1
                    self._wait("sp", ("d_%s%d" % (q, i), d["sems"][i], 16 * last))
        for e in ("pe", "act", "dve", "pool"):
            if self.cnt[e] > 0:
                self._wait("sp", (e, self.sem[e], self.cnt[e]))
        self.nc.sync.nop().then_inc(self.sem["sp"], 1)
        self.cnt["sp"] += 1
        tok = ("sp", self.sem["sp"], self.cnt["sp"])
        for e in ("pe", "act", "dve", "pool"):
            self._wait(e, tok)


class K:
    def __init__(self, L, depth, debug=False):
        self.L, self.depth, self.debug = L, depth, debug
        self.NT, self.NG = L // 128, L // 512
        self.nc = bass.Bass("TRN2", target_bir_lowering=False)
        self.stack = ExitStack()
        self.S = Sched(self.nc, self.stack)
        self.dbg_names = []
        self.banks = []
        for i in range(8):
            t = self.stack.enter_context(self.nc.psum_tensor("bank%d" % i, [128, 512], F32))
            self.banks.append(Tile(t))
            self.banks[-1].res.ps = True
        self.bank_i = 0
        self.ntile = 0

    def din(self, name, shape, dt=F32):
        return self.nc.dram_tensor(name, list(shape), dt, kind="ExternalInput").ap()

    def dscr(self, name, shape, dt, dbg=False):
        kind = "ExternalOutput" if (self.debug and dbg) else "Internal"
        if self.debug and dbg:
            self.dbg_names.append(name)
        return self.nc.dram_tensor(name, list(shape), dt, kind=kind).ap()

    def tile(self, st, shape, dt, name=None):
        self.ntile += 1
        t = st.enter_context(self.nc.sbuf_tensor("%s_%d" % (name or "t", self.ntile), list(shape), dt))
        return Tile(t)

    def bank(self, lo=0, hi=8):
        n = hi - lo
        b = self.banks[lo + (self.bank_i % n)]
        self.bank_i += 1
        return b

    def mm(self, bank, out, lhsT, rhs, start, reads):
        self.S.op("pe", lambda e: e.matmul(out, lhsT=lhsT, rhs=rhs, start=start, stop=True, skip_group_check=True),
                  reads=reads, writes=[bank])

    def tr(self, bank, out, in_, ident, reads):
        self.S.op("pe", lambda e: e.transpose(out, in_, ident), reads=reads, writes=[bank])

    def act(self, out, in_, func, reads, writes, bias=None, scale=None, accum_out=None):
        kw = {}
        if bias is not None:
            kw["bias"] = bias
        if scale is not None:
            kw["scale"] = scale
        if accum_out is not None:
            kw["accum_out"] = accum_out
        self.S.op("act", lambda e: e.activation(out=out, in_=in_, func=func, **kw), reads=reads, writes=writes)

    def ts(self, out, in0, s1, op0, reads, writes, s2=None, op1=None, accum_out=None, eng="dve"):
        kw = {}
        if op1 is not None:
            kw["op1"] = op1
        if accum_out is not None:
            kw["accum_out"] = accum_out
        self.S.op(eng, lambda e: e.tensor_scalar(out=out, in0=in0, scalar1=s1, scalar2=s2, op0=op0, **kw),
                  reads=reads, writes=writes)

    def tt(self, out, in0, in1, op, reads, writes, eng="dve"):
        self.S.op(eng, lambda e: e.tensor_tensor(out=out, in0=in0, in1=in1, op=op), reads=reads, writes=writes)

    def stt(self, out, in0, scalar, in1, op0, op1, reads, writes, accum_out=None):
        kw = {}
        if accum_out is not None:
            kw["accum_out"] = accum_out
        self.S.op("dve", lambda e: e.scalar_tensor_tensor(out=out, in0=in0, scalar=scalar, in1=in1, op0=op0, op1=op1, **kw),
                  reads=reads, writes=writes)

    def copy(self, eng, out, in_, reads, writes):
        if eng == "act":
            self.S.op("act", lambda e: e.activation(out=out, in_=in_, func=AF.Copy), reads=reads, writes=writes)
        else:
            self.S.op(eng, lambda e: e.tensor_copy(out=out, in_=in_), reads=reads, writes=writes)

    def memset(self, out, val, writes, eng="dve"):
        self.S.op(eng, lambda e: e.memset(out, val), reads=[], writes=writes)

    def load(self, out, in_, tile, q="sp"):
        self.S.dma(q, out, in_, reads=[], writes=[tile])

    def store(self, out, in_, tile, q="sp"):
        self.S.dma(q, out, in_, reads=[tile], writes=[])


def build(L=4096, depth=4, debug=False, stop_after=None):
    k = K(L, depth, debug)
    nc, S = k.nc, k.S
    NT, NG = k.NT, k.NG
    x_in = k.din("x", [L, D])
    mem_in = k.din("mem", [256, D])
    w_in = k.din("w_in_p", [depth, D, NWIN])
    w_out = k.din("w_out", [depth, D, D])
    xq = k.din("xq", [depth, D, D])
    xk = k.din("xk", [depth, D, D])
    xv = k.din("xv", [depth, D, D])
    xo = k.din("xo", [depth, D, D])
    w1 = k.din("mlp_w1", [depth, D, 4 * D])
    w2 = k.din("mlp_w2", [depth, 4 * D, D])
    lngb = k.din("lngb", [depth, 128, 6, D])
    biasG = k.din("biasG", [128, 16, 2, 128])
    maskM = k.din("maskM", [128, 3, 128])
    cbias = k.din("cbias", [128, 16])
    identf = k.din("identf", [128, 128])
    ee64 = k.din("ee64", [64, L])
    ee256 = k.din("ee256", [16, L])
    cmpneg = k.din("cmpneg", [256, L])
    ovl = k.din("ovl", [256, 63])
    nsa_add = k.din("nsa_add", [L, 63])
    nsa_w1 = k.din("nsa_w1", [depth, 128, 32, 64])
    nsa_w2 = k.din("nsa_w2", [depth, 64, 192])
    nsa_posT = k.din("nsa_posT", [depth, 128, 32])
    diff_l = k.din("diff_l", [depth, 128, 4, 32])
    diff_g = k.din("diff_g", [depth, 128, 64])
    moba_add = k.din("moba_add", [128, 16, 4, 16])
    causneg = k.din("causneg", [128, 128])
    out_d = nc.dram_tensor("out", [L, D], F32, kind="ExternalOutput").ap()

    xres = [k.dscr("xresA", [L, D], F32), k.dscr("xresB", [L, D], F32)]
    xT = k.dscr("xT", [8, 128, L], BF16, dbg=True)
    hT = k.dscr("hT", [NFM, 128, L], BF16, dbg=True)
    vaug_a = k.dscr("vaug_a", [L, 260], BF16, dbg=True)
    vaug_c = k.dscr("vaug_c", [L, 260], BF16)
    vaug_d = k.dscr("vaug_d", [L, 260], BF16)
    vaug_b = k.dscr("vaug_b", [L, 130], BF16)
    smallf = k.dscr("smallf", [L, 20], F32, dbg=True)
    negT_d = k.dscr("negT", [NG, NT, 128, 512], BF16)
    oT = k.dscr("oT", [8, 128, L], BF16, dbg=True)
    memT = k.dscr("memT", [8, 128, 256], BF16)
    hmlp = k.dscr("hmlp", [32, 128, L], BF16)

    wspecs = [("w_in_fm", lambda l: w_in[l, :, 0:NFM * 128], D, NFM * 128), ("w_in_tm", lambda l: w_in[l, :, NFM * 128:NWIN], D, NTMA + NTMB),
              ("w_out", lambda l: w_out[l], D, D), ("xk", lambda l: xk[l], D, D), ("xv", lambda l: xv[l], D, D),
              ("xq", lambda l: xq[l], D, D), ("xo", lambda l: xo[l], D, D), ("w1", lambda l: w1[l], D, 4 * D), ("w2", lambda l: w2[l], 4 * D, D)]
    wbf = {}

    def convert_layer(l):
        for nme, src, rows, cols in wspecs:
            dst = k.dscr("wb_%s_%d" % (nme, l), [rows, cols], BF16)
            rl = []
            for r0 in range(0, rows, 256):
                rr = Res()
                S.dma("poolc", dst[r0:r0 + 256, :], src(l)[r0:r0 + 256, :], reads=[], writes=[rr])
                rl.append(rr)
            wbf[(nme, l)] = (dst, rl)

    convert_layer(0)

    gst = k.stack
    ident = k.tile(gst, [128, 128], F32, "ident")
    identb = k.tile(gst, [128, 128], BF16, "identb")
    onesb = k.tile(gst, [128, 128], BF16, "onesb")
    eps5 = k.tile(gst, [128, 1], F32, "eps5")
    eps6 = k.tile(gst, [128, 1], F32, "eps6")
    cb = k.tile(gst, [128, 16], F32, "cb")
    Bm = k.tile(gst, [128, 16, 2, 128], BF16, "Bm")
    B4 = k.tile(gst, [128, 128], BF16, "B4")

    k.load(ident[:], identf, ident)
    k.load(cb[:], cbias, cb)
    k.copy("dve", identb[:], ident[:], [ident], [identb])
    k.memset(onesb[:], 1.0, [onesb])
    k.memset(eps5[:], 1e-5, [eps5])
    k.memset(eps6[:], 1e-6, [eps6])
    with ExitStack() as st:
        bg = k.tile(st, [128, 16, 2, 128], F32, "bg")
        mk = k.tile(st, [128, 3, 128], F32, "mk")
        k.load(bg[:], biasG, bg)
        k.load(mk[:], maskM, mk)
        for h in range(16):
            inv = 1.0 / (32 ** -0.5) if h >= 12 else 1.0 / (64 ** -0.5)
            for d in range(2):
                k.ts(bg[:, h, d, :], bg[:, h, d, :], cb[:, h:h + 1], ALU.subtract, [bg, cb], [bg], s2=inv, op1=ALU.mult)
            k.tt(Bm[:, h, 0, :], bg[:, h, 0, :], mk[:, 0, :], ALU.add, [bg, mk], [Bm])
            k.copy("dve", Bm[:, h, 1, :], bg[:, h, 1, :], [bg], [Bm])
        k.copy("dve", B4[:], mk[:, 1, :], [mk], [B4])
        mt = k.tile(st, [128, 2, D], F32, "mt")
        mts = k.tile(st, [128, 8, 256], BF16, "mts")
        k.load(mt[:], mem_in.rearrange("(c p) d -> p c d", p=128), mt)
        for c in range(8):
            b = k.bank()
            for mc in range(2):
                k.tr(b, b[:, mc * 128:(mc + 1) * 128], mt[:, mc, c * 128:(c + 1) * 128], ident[:], [mt, ident])
            k.copy("act", mts[:, c, :], b[:, 0:256], [b], [mts])
        k.store(memT.rearrange("c p m -> p c m"), mts[:], mts)
        S.barrier()
    if stop_after == "setup":
        return k

    def load_w(st, wd, KC, M, name, split="col"):
        W = k.tile(st, [128, KC, M], BF16, name)
        wd, wres = wd
        if split == "row":
            W.sub = [Res() for _ in range(KC)]
            for kc in range(KC):
                S.dma("sp", W[:, kc, :], wd[kc * 128:(kc + 1) * 128, :], reads=[wres[(kc * 128) // 256]], writes=[W.sub[kc]])
        else:
            nb = (M + 511) // 512
            W.sub = [Res() for _ in range(nb)]
            for cb in range(nb):
                c0, c1 = cb * 512, min(M, (cb + 1) * 512)
                S.dma("sp", W[:, :, c0:c1], wd[:, c0:c1].rearrange("(kc p) n -> p kc n", p=128), reads=wres, writes=[W.sub[cb]])
        return W

    def phase_fm(actT, KC, ncols, wd, nchunks, outd, relu2=False):
        GW = min(512, ncols)
        with ExitStack() as st:
            W = load_w(st, wd, KC, nchunks * 128, "Wfm")
            at = [k.tile(st, [128, KC, GW], BF16, "at") for _ in range(2)]
            CB = 4
            stg = [k.tile(st, [128, CB, GW], BF16, "stg") for _ in range(3)]
            rt = [k.tile(st, [128, GW], F32, "rt") for _ in range(2)] if relu2 else None
            si = 0
            ev = 0
            ngr = ncols // GW

            def ld(g):
                k.load(at[g % 2][:], actT[:, :, g * GW:(g + 1) * GW].rearrange("c p n -> p c n"), at[g % 2])
            ld(0)
            for g in range(ngr):
                a = at[g % 2]
                if g + 1 < ngr:
                    ld(g + 1)
                for c0 in range(0, nchunks, CB):
                    sg = stg[si % 3]
                    si += 1
                    nb = min(CB, nchunks - c0)
                    for cc in range(nb):
                        c = c0 + cc
                        b = k.bank()
                        for kc in range(KC):
                            k.mm(b, b[:, 0:GW], W[:, kc, c * 128:(c + 1) * 128], a[:, kc, :], kc == 0, [W.sub[c // 4], a])
                        if relu2:
                            r = rt[ev % 2]
                            k.act(r[:], b[:, 0:GW], AF.Relu, [b], [r])
                            k.tt(sg[:, cc, :], r[:], r[:], ALU.mult, [r], [sg], eng="pool")
                        else:
                            k.copy("act" if ev % 2 == 0 else "dve", sg[:, cc, :], b[:, 0:GW], [b], [sg])
                        ev += 1
                    k.store(outd[c0:c0 + nb, :, g * GW:(g + 1) * GW].rearrange("c p n -> p c n"), sg[:, 0:nb, :], sg)
            S.barrier()

    def phase_tm(l):
        with ExitStack() as st:
            W = load_w(st, wbf[("w_in_tm", l)], 8, NTMA + NTMB, "Wtm")
            at = [k.tile(st, [128, 8, 512], BF16, "at") for _ in range(2)]
            sa = [k.tile(st, [128, 4, 4, 65], BF16, "sa") for _ in range(2)]
            sc = [k.tile(st, [128, 4, 4, 65], BF16, "sc") for _ in range(2)]
            sd = [k.tile(st, [128, 4, 4, 65], BF16, "sd") for _ in range(2)]
            sb = [k.tile(st, [128, 4, 2, 65], BF16, "sb") for _ in range(2)]
            sf = [k.tile(st, [128, 4, 20], F32, "sf") for _ in range(2)]
            import os
            tmdbg = int(os.environ.get("TMDBG", "0"))
            for tl in sa + sc + sd + sb:
                if not (tmdbg & 64):
                    k.memset(tl[:], 1.0, [tl])
            def ld(g):
                k.load(at[g % 2][:], xT[:, :, g * 512:(g + 1) * 512].rearrange("c p n -> p c n"), at[g % 2])
            ld(0)
            for g in range(NG):
                a = at[g % 2]
                if g + 1 < NG:
                    ld(g + 1)
                A, C, Dd, B, Fs = sa[g % 2], sc[g % 2], sd[g % 2], sb[g % 2], sf[g % 2]
                for r in range(4):
                    b = k.bank()
                    for kc in range(8):
                        k.mm(b, b[:, 0:512], a[:, kc, r * 128:(r + 1) * 128], W[:, kc, 0:512], kc == 0, [a, W.sub[0]])
                    if not (tmdbg & 8):
                        k.copy("act", A[:, r, :, 0:64], b[:, 0:256].rearrange("p (h d) -> p h d", d=64), [b], [A])
                    if not (tmdbg & 128):
                        k.copy("act", C[:, r, :, 0:64], b[:, 256:512].rearrange("p (h d) -> p h d", d=64), [b], [C])
                    b = k.bank()
                    for kc in range(8):
                        k.mm(b, b[:, 0:NTMB], a[:, kc, r * 128:(r + 1) * 128], W[:, kc, 512:512 + NTMB], kc == 0, [a, W.sub[1]])
                    if not (tmdbg & 16):
                        k.copy("dve", Dd[:, r, :, 0:64], b[:, 0:256].rearrange("p (h d) -> p h d", d=64), [b], [Dd])
                    if not (tmdbg & 256):
                        k.copy("dve", B[:, r, :, 0:64], b[:, 256:384].rearrange("p (h d) -> p h d", d=64), [b], [B])
                    if not (tmdbg & 32):
                        k.copy("dve", Fs[:, r, :], b[:, 384:404], [b], [Fs])
                sl = slice(g * 512, (g + 1) * 512)
                import os
                tmdbg = int(os.environ.get("TMDBG", "0"))
                if tmdbg & 1:
                    continue
                k.store(vaug_a[sl, :].rearrange("(r p) c -> p r c", p=128), A[:].rearrange("p r h c -> p r (h c)"), A)
                k.store(vaug_c[sl, :].rearrange("(r p) c -> p r c", p=128), C[:].rearrange("p r h c -> p r (h c)"), C)
                k.store(vaug_d[sl, :].rearrange("(r p) c -> p r c", p=128), Dd[:].rearrange("p r h c -> p r (h c)"), Dd)
                if tmdbg & 2:
                    continue
                k.store(vaug_b[sl, :].rearrange("(r p) c -> p r c", p=128), B[:].rearrange("p r h c -> p r (h c)"), B)
                if tmdbg & 4:
                    continue
                k.store(smallf[sl, :].rearrange("(r p) c -> p r c", p=128), Fs[:], Fs)
            S.barrier()

    def phase_outln(actT, KC, wd, gbl, j0, xin, xout):
        GW = 256
        with ExitStack() as st:
            W = load_w(st, wd, KC, D, "Wo", split="row")
            gb = k.tile(st, [128, 2, D], F32, "gb")
            k.load(gb[:], gbl[:, j0:j0 + 2, :], gb)
            at = [k.tile(st, [128, KC, GW], BF16, "at") for _ in range(2)]
            xr = [k.tile(st, [128, 2, D], F32, "xr") for _ in range(2)]
            zt = [k.tile(st, [128, D], F32, "z") for _ in range(2)]
            xn = [k.tile(st, [128, D], F32, "xn") for _ in range(2)]
            xts = [k.tile(st, [128, 8, GW], BF16, "xts") for _ in range(2)]
            sm = [k.tile(st, [128, 24], F32, "sm") for _ in range(2)]
            ti = 0
            def ld(g):
                k.load(at[g % 2][:], actT[:, :, g * GW:(g + 1) * GW].rearrange("c p n -> p c n"), at[g % 2])
                k.load(xr[g % 2][:], xin[g * GW:(g + 1) * GW, :].rearrange("(r p) d -> p r d", p=128), xr[g % 2])
            ld(0)
            for g in range(L // GW):
                a = at[g % 2]
                X = xr[g % 2]
                XT = xts[g % 2]
                if g + 1 < L // GW:
                    ld(g + 1)
                for r in range(2):
                    z, xo_, s = zt[ti % 2], xn[ti % 2], sm[ti % 2]
                    ti += 1
                    for nh in range(2):
                        b = k.bank()
                        for kc in range(KC):
                            k.mm(b, b[:, :], a[:, kc, r * 128:(r + 1) * 128], W[:, kc, nh * 512:(nh + 1) * 512], kc == 0, [a, W.sub[kc]])
                        k.stt(z[:, nh * 512:(nh + 1) * 512], X[:, r, nh * 512:(nh + 1) * 512], ALPHA, b[:, :],
                              ALU.mult, ALU.add, [X, b], [z])
                    S.op("dve", lambda e: e.bn_stats(s[:, 0:6], z[:, 0:512]), [z.res], [s.res])
                    S.op("dve", lambda e: e.bn_stats(s[:, 6:12], z[:, 512:1024]), [z.res], [s.res])
                    S.op("dve", lambda e: e.bn_aggr(s[:, 12:14], s[:, 0:12]), [s.res], [s.res])
                    k.act(s[:, 14:15], s[:, 13:14], AF.Ln, [s, eps5], [s], bias=eps5[:, 0:1])
                    k.act(s[:, 15:16], s[:, 14:15], AF.Exp, [s], [s], scale=-0.5)
                    k.ts(s[:, 16:17], s[:, 12:13], s[:, 15:16], ALU.mult, [s], [s], s2=-1.0, op1=ALU.mult)
                    k.act(xo_[:], z[:], AF.Identity, [z, s], [xo_], bias=s[:, 16:17], scale=s[:, 15:16])
                    k.tt(xo_[:], xo_[:], gb[:, 0, :], ALU.mult, [xo_, gb], [xo_], eng="pool")
                    k.tt(X[:, r, :], xo_[:], gb[:, 1, :], ALU.add, [xo_, gb], [X])
                    for c0 in range(0, 8, 4):
                        b = k.bank()
                        for cc in range(4):
                            c = c0 + cc
                            k.tr(b, b[:, cc * 128:(cc + 1) * 128], X[:, r, c * 128:(c + 1) * 128], ident[:], [X, ident])
                        k.copy("act", XT[:, c0:c0 + 4, r * 128:(r + 1) * 128],
                               b[:, :].rearrange("p (c n) -> p c n", n=128), [b], [XT])
                k.store(xout[g * GW:(g + 1) * GW, :].rearrange("(r p) d -> p r d", p=128), X[:], X)
                k.store(xT[:, :, g * GW:(g + 1) * GW].rearrange("c p n -> p c n"), XT[:], XT)
            S.barrier()

    ACC0 = 0

    def attn_group(g, maps, jlist, NV, pts, out_cb):
        accs = [k.banks[ACC0 + r] for r in range(4)]
        started = [False] * 4
        steps = [(j, rlo, rhi, mi, m) for (j, rlo, rhi) in jlist for mi, m in enumerate(maps)]
        ptof = {}

        def part_a(idxs):
            bks = {}
            for idx in idxs:
                j, rlo, rhi, mi, m = steps[idx]
                if m.get("pre") is not None:
                    m["pre"](j)
            for idx in idxs:
                j, rlo, rhi, mi, m = steps[idx]
                c0, c1 = rlo * 128, rhi * 128
                b = k.bank(4, 8)
                bks[idx] = b
                k.mm(b, b[:, c0:c1], m["kt"](j), m["qt"][:, c0:c1], True, m["reads"])
            for idx in idxs:
                j, rlo, rhi, mi, m = steps[idx]
                c0, c1 = rlo * 128, rhi * 128
                b = bks[idx]
                for r in range(rlo, rhi):
                    d = 4 * g + r - j
                    if d in m["bm"]:
                        bmt, bmap = m["bm"][d]
                        k.mm(b, b[:, r * 128:(r + 1) * 128], identb[:], bmap, False, [identb, bmt])
                mmx = m["mm"](j) if m.get("mm") else None
                if mmx is not None:
                    lh, rh, rds = mmx
                    k.mm(b, b[:, c0:c1], lh, rh[:, c0:c1], False, rds)
            for idx in idxs:
                j, rlo, rhi, mi, m = steps[idx]
                c0, c1 = rlo * 128, rhi * 128
                b = bks[idx]
                pt = pts[k.pt_i % len(pts)]
                k.pt_i += 1
                if m.get("cb") is not None:
                    k.act(pt[:, c0:c1], b[:, c0:c1], AF.Exp, [b] + m["cbr"], [pt], bias=m["cb"], scale=m["scale"])
                else:
                    k.act(pt[:, c0:c1], b[:, c0:c1], AF.Exp, [b], [pt], scale=m["scale"])
                if m.get("pmul") is not None:
                    mk_ap, mk_rd = m["pmul"](j)
                    k.tt(pt[:, c0:c1], pt[:, c0:c1], mk_ap[:, c0:c1], ALU.mult, [pt] + mk_rd, [pt], eng="pool")
                ptof[idx] = pt

        def part_b(idxs):
            for idx in idxs:
                j, rlo, rhi, mi, m = steps[idx]
                pt = ptof.pop(idx)
                for r in range(rlo, rhi):
                    k.mm(accs[r], accs[r][:, mi * NV:(mi + 1) * NV], pt[:, r * 128:(r + 1) * 128], m["v"](j),
                         not started[r], [pt] + m["vreads"])
                    started[r] = True

        sss = [list(range(i, min(i + 2, len(steps)))) for i in range(0, len(steps), 2)]
        LA = 1
        for si in range(len(sss) + LA):
            if si < len(sss):
                part_a(sss[si])
            if si - LA >= 0:
                part_b(sss[si - LA])
        for r in range(4):
            out_cb(r, accs[r])

    k.pt_i = 0

    def causal_jlist(g):
        return [(j, max(0, j - 4 * g), 4) for j in range(0, 4 * g + 4)]

    def win_jlist(g):
        res = []
        for j in range(max(0, 4 * g - 4), 4 * g + 4):
            rlo = max(0, j - 4 * g)
            rhi = min(4, j + 4 - 4 * g + 1)
            if rhi > rlo:
                res.append((j, rlo, rhi))
        return res

    def finish_tile_out(st_tiles, o32, r, g, ostg, row0):
        b = k.bank(4, 8)
        for c in range(2):
            k.tr(b, b[:, c * 128:(c + 1) * 128], o32[:, c * 128:(c + 1) * 128], ident[:], [o32, ident])
        k.copy("act", ostg[:, :, r * 128:(r + 1) * 128], b[:, 0:256].rearrange("p (c n) -> p c n", n=128), [b], [ostg])

    def phase_dsa_select(l):
        NIT = 20
        with ExitStack() as st:
            qi = k.tile(st, [128, 4, L], BF16, "qi")
            ki = k.tile(st, [128, L], BF16, "ki")
            k.load(qi[:], hT[C_QI:C_QI + 4, :, :].rearrange("c p n -> p c n"), qi)
            k.load(ki[:], hT[C_KI, :, :], ki)
            cneg = k.tile(st, [128, 128], F32, "cneg")
            k.load(cneg[:], causneg, cneg)
            wall = k.tile(st, [128, NT, 8], F32, "wall")
            k.load(wall[:], smallf[:, 0:8].rearrange("(i p) c -> p i c", p=128), wall)
            wabs = k.tile(st, [128, NT, 8], F32, "wabs")
            wsgn = k.tile(st, [128, NT, 8], F32, "wsgn")
            k.ts(wsgn[:], wall[:], 0.0, ALU.is_ge, [wall], [wsgn], s2=2.0, op1=ALU.mult)
            k.ts(wsgn[:], wsgn[:], -1.0, ALU.add, [wsgn], [wsgn])
            k.tt(wabs[:], wall[:], wsgn[:], ALU.mult, [wall, wsgn], [wabs])
            NTL = 4
            accs = [k.tile(st, [128, L], F32, "acc") for _ in range(NTL)]
            junk = {"dve": k.tile(st, [128, L], BF16, "junkd"), "act": k.tile(st, [128, L], BF16, "junka")}
            tmps = [k.tile(st, [128, 512], F32, "tmp") for _ in range(6)]
            sms = [k.tile(st, [128, 8], F32, "sm") for _ in range(NTL)]
            rks = [k.tile(st, [128, 2, NIT], F32, "rk") for _ in range(NTL)]
            ckrow = k.tile(st, [128, NIT], F32, "ckrow")
            for it in range(NIT):
                k.memset(ckrow[:, it:it + 1], 2.0 ** -(it + 1), [ckrow])
            negst = k.tile(st, [128, NT, 512], BF16, "negst")
            k.memset(negst[:], 1.0, [negst])
            tmi = 0
            for g in range(NG):
                tiles = [i for i in range(4 * g, 4 * g + 4) if i >= 2]
                if True:
                    pair = tiles
                    ceng = ["dve" if (pi % 2 == 0) else "act" for pi in range(len(pair))]
                    for pi, i in enumerate(pair):
                        acc = accs[pi]
                        Lk = 128 * (i + 1)
                        for kb in range(0, Lk, 512):
                            n = min(512, Lk - kb)
                            for h in range(8):
                                pb = (h % 2) * 64
                                b = k.bank()
                                k.mm(b, b[:, 0:n], qi[pb:pb + 64, h // 2, i * 128:(i + 1) * 128], ki[pb:pb + 64, kb:kb + n], True, [qi, ki])
                                if h == 0:
                                    k.ts(acc[:, kb:kb + n], b[:, 0:n], 0.0, ALU.max, [b, wall], [acc], s2=wall[:, i, 0:1], op1=ALU.mult)
                                else:
                                    t = tmps[tmi % 6]
                                    tmi += 1
                                    k.act(t[:, 0:n], b[:, 0:n], AF.Relu, [b, wabs], [t], scale=wabs[:, i, h:h + 1])
                                    if h >= 5:
                                        k.ts(t[:, 0:n], t[:, 0:n], wsgn[:, i, h:h + 1], ALU.mult, [t, wsgn], [t], eng="pool")
                                        k.tt(acc[:, kb:kb + n], acc[:, kb:kb + n], t[:, 0:n], ALU.add, [acc, t], [acc], eng="pool")
                                    else:
                                        k.stt(acc[:, kb:kb + n], t[:, 0:n], wsgn[:, i, h:h + 1], acc[:, kb:kb + n], ALU.mult, ALU.add, [t, wsgn, acc], [acc])
                        k.tt(acc[:, i * 128:Lk], acc[:, i * 128:Lk], cneg[:], ALU.add, [acc, cneg], [acc])
                        s, rk = sms[pi], rks[pi]
                        S.op("dve", lambda e: e.tensor_reduce(out=s[:, 0:1], in_=acc[:, 0:Lk], axis=AX.X, op=ALU.max), [acc.res], [s.res])
                        S.op("dve", lambda e: e.tensor_reduce(out=s[:, 1:2], in_=acc[:, 0:i * 128], axis=AX.X, op=ALU.min), [acc.res], [s.res])
                        k.tt(s[:, 2:3], s[:, 0:1], s[:, 1:2], ALU.subtract, [s], [s])
                        k.ts(rk[:, 0, :], ckrow[:], s[:, 2:3], ALU.mult, [ckrow, s], [rk])
                        k.ts(rk[:, 1, :], rk[:, 0, :], 2.0, ALU.mult, [rk], [rk])
                        k.tt(s[:, 3:4], s[:, 1:2], rk[:, 0, 0:1], ALU.add, [s, rk], [s])
                    for it in range(NIT):
                        for pi, i in enumerate(pair):
                            acc, s = accs[pi], sms[pi]
                            Lk = 128 * (i + 1)
                            if ceng[pi] == "dve":
                                jk = junk["dve"]
                                k.ts(jk[:, 0:Lk], acc[:, 0:Lk], s[:, 3:4], ALU.is_ge, [acc, s], [jk, s], s2=0.0, op1=ALU.add, accum_out=s[:, 4:5])
                            else:
                                jk = junk["act"]
                                k.act(jk[:, 0:Lk], acc[:, 0:Lk], AF.Sign, [acc, s], [jk, s], bias=s[:, 3:4], scale=-1.0, accum_out=s[:, 4:5])
                        for pi, i in enumerate(pair):
                            s, rk = sms[pi], rks[pi]
                            Lk = 128 * (i + 1)
                            last = (it == NIT - 1)
                            if ceng[pi] == "dve":
                                cmpv, opge, oplt = 255.5, ALU.is_ge, ALU.is_lt
                            else:
                                cmpv, opge, oplt = float(Lk - 511), ALU.is_le, ALU.is_gt
                            if not last:
                                k.stt(s[:, 5:6], s[:, 4:5], cmpv, rk[:, 1, it + 1:it + 2], opge, ALU.mult, [s, rk], [s])
                                k.stt(s[:, 3:4], s[:, 5:6], rk[:, 0, it + 1:it + 2], s[:, 3:4], ALU.subtract, ALU.add, [s, rk], [s])
                            else:
                                k.stt(s[:, 5:6], s[:, 4:5], cmpv, rk[:, 0, it:it + 1], oplt, ALU.mult, [s, rk], [s])
                                k.tt(s[:, 1:2], s[:, 3:4], s[:, 5:6], ALU.subtract, [s], [s])
                    for pi, i in enumerate(pair):
                        acc, s = accs[pi], sms[pi]
                        Lk = 128 * (i + 1)
                        r = i - 4 * g
                        k.ts(acc[:, 0:Lk], acc[:, 0:Lk], s[:, 1:2], ALU.is_ge, [acc, s], [acc])
                        for j0 in range(0, i + 1, 4):
                            nb = min(4, i + 1 - j0)
                            b = k.bank()
                            for jj in range(nb):
                                k.tr(b, b[:, jj * 128:(jj + 1) * 128], acc[:, (j0 + jj) * 128:(j0 + jj + 1) * 128], ident[:], [acc, ident])
                            k.copy("act", negst[:, j0:j0 + nb, r * 128:(r + 1) * 128],
                                   b[:, 0:nb * 128].rearrange("p (c n) -> p c n", n=128), [b], [negst])
                nj = 4 * g + 4
                k.store(negT_d[g, 0:nj, :, :].rearrange("j p n -> p j n"), negst[:, 0:nj, :], negst)
            S.barrier()

    def phase_dsa_attn(l):
        with ExitStack() as st:
            qt = k.tile(st, [128, 2, L], BF16, "qt")
            kt = k.tile(st, [128, 2, L], BF16, "kt")
            va = k.tile(st, [128, NT, 260], BF16, "va")
            k.load(qt[:], hT[C_AQ:C_AQ + 2, :, :].rearrange("c p n -> p c n"), qt)
            k.load(kt[:], hT[C_AK:C_AK + 2, :, :].rearrange("c p n -> p c n"), kt)
            k.load(va[:], vaug_a.rearrange("(i p) c -> p i c", p=128), va)
            ngs = [k.tile(st, [128, 4, 512], BF16, "ng") for _ in range(3)]
            pts = [k.tile(st, [128, 512], BF16, "pt") for _ in range(6)]
            o32s = [k.tile(st, [128, 256], F32, "o32") for _ in range(2)]
            rds = [k.tile(st, [128, 4], F32, "rd") for _ in range(2)]
            ostg = [k.tile(st, [128, 2, 512], BF16, "ostg") for _ in range(2)]
            ngi = [0]
            oi = [0]
            for g in range(NG):
                ngcache = {}

                def get_ng(j, g=g, ngcache=ngcache):
                    jb = j // 4
                    if jb not in ngcache:
                        t = ngs[ngi[0] % 3]
                        ngi[0] += 1
                        k.load(t[:], negT_d[g, jb * 4:jb * 4 + 4, :, :].rearrange("j p n -> p j n"), t)
                        ngcache[jb] = t
                    return ngcache[jb]

                maps = []
                for h in range(4):
                    pb = (h % 2) * 64
                    c = h // 2

                    def mmf(j):
                        t = get_ng(j)
                        return (t[:, j % 4, :], [t])
                    maps.append(dict(
                        kt=(lambda j, pb=pb, c=c: kt[pb:pb + 64, c, j * 128:(j + 1) * 128]),
                        qt=qt[pb:pb + 64, c, g * 512:(g + 1) * 512], scale=0.125,
                        cb=cb[:, h:h + 1], cbr=[cb], bm={0: (Bm, Bm[:, h, 0, :]), 1: (Bm, Bm[:, h, 1, :])},
                        mm=None, pmul=mmf, v=(lambda j, h=h: va[:, j, h * 65:(h + 1) * 65]), reads=[kt, qt], vreads=[va]))
                OS = ostg[g % 2]

                def out_cb(r, acc, g=g, OS=OS):
                    o32 = o32s[oi[0] % 2]
                    rd = rds[oi[0] % 2]
                    oi[0] += 1
                    av = acc[:, 0:260].rearrange("p (h c) -> p h c", c=65)
                    S.op("dve", lambda e: e.reciprocal(rd[:, :], av[:, :, 64]), [acc.res], [rd.res])
                    for h in range(4):
                        k.ts(o32[:, h * 64:(h + 1) * 64], av[:, h, 0:64], rd[:, h:h + 1], ALU.mult, [acc, rd], [o32])
                    finish_tile_out(None, o32, r, g, OS, 0)
                attn_group(g, maps, causal_jlist(g), 65, pts, out_cb)
                k.store(oT[0:2, :, g * 512:(g + 1) * 512].rearrange("c p n -> p c n"), OS[:], OS)
            S.barrier()

    def phase_moba(l):
        with ExitStack() as st:
            qt = k.tile(st, [128, 2, L], BF16, "qt")
            kt = k.tile(st, [128, 2, L], BF16, "kt")
            va = k.tile(st, [128, NT, 260], BF16, "va")
            k.load(qt[:], hT[C_CQ:C_CQ + 2, :, :].rearrange("c p n -> p c n"), qt)
            k.load(kt[:], hT[C_CK:C_CK + 2, :, :].rearrange("c p n -> p c n"), kt)
            k.load(va[:], vaug_c.rearrange("(i p) c -> p i c", p=128), va)
            e256 = k.tile(st, [16, L], BF16, "e256")
            k.load(e256[:], ee256, e256, q="pool")
            madd = k.tile(st, [128, 16, 4, 16], F32, "madd")
            k.load(madd[:], moba_add, madd)
            nb = L // 256
            kmf = k.tile(st, [128, 2, 16], F32, "kmf")
            km = k.tile(st, [128, 2, 16], BF16, "km")
            k.memset(kmf[:], 0.0, [kmf])
            for c in range(2):
                S.op("dve", lambda e: e.tensor_reduce(out=kmf[:, c, 0:nb], in_=kt[:, c, :].rearrange("p (n s) -> p n s", s=256),
                                                       axis=AX.X, op=ALU.add), [kt.res], [kmf.res])
            k.ts(km[:], kmf[:], 1.0 / 256.0, ALU.mult, [kmf], [km])
            pts = [k.tile(st, [128, 512], BF16, "pt") for _ in range(6)]
            o32s = [k.tile(st, [128, 256], F32, "o32") for _ in range(2)]
            rds = [k.tile(st, [128, 4], F32, "rd") for _ in range(2)]
            ostg = [k.tile(st, [128, 2, 512], BF16, "ostg") for _ in range(2)]
            gms = [k.tile(st, [128, 4, 16], F32, "gm") for _ in range(2)]
            m8s = [k.tile(st, [128, 4, 8], F32, "m8") for _ in range(2)]
            bmT = [k.tile(st, [16, 4, 512], BF16, "bmT") for _ in range(2)]
            oi = [0]
            gi = 0
            for g in range(NG):
                BT = bmT[g % 2]
                for r in range(4):
                    i = 4 * g + r
                    cur = i // 2
                    gm, m8 = gms[gi % 2], m8s[gi % 2]
                    gi += 1
                    bA, bB = k.bank(4, 8), k.bank(4, 8)
                    for h in range(4):
                        pb = (h % 2) * 64
                        c = h // 2
                        b = bA if pb == 0 else bB
                        k.mm(b, b[:, (h // 2) * 16:(h // 2) * 16 + 16], qt[pb:pb + 64, c, i * 128:(i + 1) * 128], km[pb:pb + 64, c, :], h < 2, [qt, km])
                    for h in range(4):
                        b = bA if h % 2 == 0 else bB
                        k.tt(gm[:, h, :], b[:, (h // 2) * 16:(h // 2) * 16 + 16], madd[:, cur, h, :], ALU.add, [b, madd], [gm])
                    for h in range(4):
                        S.op("dve", lambda e, h=h: e.max(out=m8[:, h, :], in_=gm[:, h, :]), [gm.res], [m8.res])
                    k.ts(m8[:, :, 2], m8[:, :, 2], -1e29, ALU.max, [m8], [m8])
                    for h in range(4):
                        k.ts(gm[:, h, :], gm[:, h, :], m8[:, h, 2:3], ALU.is_lt, [gm, m8], [gm], s2=NEGM, op1=ALU.mult)
                    k.memset(gm[:, :, cur:cur + 1], 0.0, [gm])
                    b = k.bank(4, 8)
                    for h in range(4):
                        k.tr(b, b[0:16, h * 128:(h + 1) * 128], gm[:, h, :], ident[:], [gm, ident])
                    k.copy("act", BT[:, :, r * 128:(r + 1) * 128], b[0:16, :].rearrange("p (h n) -> p h n", n=128), [b], [BT])
                maps = []
                for h in range(4):
                    pb = (h % 2) * 64
                    c = h // 2
                    hh = 8 + h
                    maps.append(dict(
                        kt=(lambda j, pb=pb, c=c: kt[pb:pb + 64, c, j * 128:(j + 1) * 128]),
                        qt=qt[pb:pb + 64, c, g * 512:(g + 1) * 512], scale=0.125,
                        cb=cb[:, hh:hh + 1], cbr=[cb], bm={0: (Bm, Bm[:, hh, 0, :]), 1: (Bm, Bm[:, hh, 1, :])},
                        mm=(lambda j, h=h, BT=BT: (e256[:, j * 128:(j + 1) * 128], BT[:, h, :], [e256, BT])),
                        v=(lambda j, h=h: va[:, j, h * 65:(h + 1) * 65]), reads=[kt, qt], vreads=[va]))
                OS = ostg[g % 2]

                def out_cb(r, acc, g=g, OS=OS):
                    o32 = o32s[oi[0] % 2]
                    rd = rds[oi[0] % 2]
                    oi[0] += 1
                    av = acc[:, 0:260].rearrange("p (h c) -> p h c", c=65)
                    S.op("dve", lambda e: e.reciprocal(rd[:, :], av[:, :, 64]), [acc.res], [rd.res])
                    for h in range(4):
                        k.ts(o32[:, h * 64:(h + 1) * 64], av[:, h, 0:64], rd[:, h:h + 1], ALU.mult, [acc, rd], [o32])
                    finish_tile_out(None, o32, r, g, OS, 0)
                attn_group(g, maps, causal_jlist(g), 65, pts, out_cb)
                k.store(oT[4:6, :, g * 512:(g + 1) * 512].rearrange("c p n -> p c n"), OS[:], OS)
            S.barrier()

    def phase_diff(l):
        lam_init = 0.8 - 0.6 * math.exp(-0.3 * l)
        with ExitStack() as st:
            qt = k.tile(st, [128, 3, L], BF16, "qt")
            kt = k.tile(st, [128, 3, L], BF16, "kt")
            va = k.tile(st, [128, NT, 260], BF16, "va")
            k.load(qt[:], hT[C_DQ:C_DQ + 3, :, :].rearrange("c p n -> p c n"), qt)
            k.load(kt[:], hT[C_DK:C_DK + 3, :, :].rearrange("c p n -> p c n"), kt)
            k.load(va[:], vaug_d.rearrange("(i p) c -> p i c", p=128), va)
            dl = k.tile(st, [128, 4, 32], F32, "dl")
            gd = k.tile(st, [128, 64], F32, "gd")
            k.load(dl[:], diff_l[l], dl)
            k.load(gd[:], diff_g[l], gd)
            lm = k.tile(st, [128, 8], F32, "lm")
            pr = k.tile(st, [128, 2, 32], F32, "pr")
            k.tt(pr[:, 0, :], dl[:, 0, :], dl[:, 1, :], ALU.mult, [dl], [pr])
            k.tt(pr[:, 1, :], dl[:, 2, :], dl[:, 3, :], ALU.mult, [dl], [pr])
            S.op("dve", lambda e: e.tensor_reduce(out=lm[:, 0:2], in_=pr[:], axis=AX.X, op=ALU.add), [pr.res], [lm.res])
            k.act(lm[:, 2:4], lm[:, 0:2], AF.Exp, [lm], [lm])
            k.tt(lm[:, 4:5], lm[:, 2:3], lm[:, 3:4], ALU.subtract, [lm], [lm])
            k.ts(lm[:, 5:6], lm[:, 4:5], lam_init, ALU.add, [lm], [lm], s2=-1.0, op1=ALU.mult)
            k.ts(gd[:], gd[:], 1.0 - lam_init, ALU.mult, [gd], [gd])
            pts = [k.tile(st, [128, 512], BF16, "pt") for _ in range(6)]
            o32s = [k.tile(st, [128, 4, 256], F32, "o32") for _ in range(2)]
            rds = [k.tile(st, [128, 16], F32, "rd") for _ in range(2)]
            t1s = [k.tile(st, [128, 64], F32, "t1") for _ in range(2)]
            jks = [k.tile(st, [128, 64], F32, "jk") for _ in range(2)]
            ostg = [k.tile(st, [128, 2, 512], BF16, "ostg") for _ in range(2)]
            oi = [0]
            sc = 32 ** -0.5
            for g in range(NG):
                OS = ostg[g % 2]
                O32 = o32s[g % 2]
                for hp in range(2):
                    maps = []
                    for hl in range(2):
                        h = hp * 2 + hl
                        hh = 12 + h
                        for m_ in range(2):
                            mi_ = 2 * h + m_
                            pb = (mi_ % 3) * 32
                            c = mi_ // 3
                            maps.append(dict(
                                kt=(lambda j, pb=pb, c=c: kt[pb:pb + 32, c, j * 128:(j + 1) * 128]),
                                qt=qt[pb:pb + 32, c, g * 512:(g + 1) * 512], scale=sc,
                                cb=cb[:, hh:hh + 1], cbr=[cb], bm={0: (Bm, Bm[:, hh, 0, :]), 1: (Bm, Bm[:, hh, 1, :])},
                                mm=None, v=(lambda j, h=h: va[:, j, h * 65:(h + 1) * 65]), reads=[kt, qt], vreads=[va]))

                    def out_cb(r, acc, hp=hp, g=g, O32=O32, OS=OS):
                        rd = rds[oi[0] % 2]
                        t1 = t1s[oi[0] % 2]
                        jk = jks[oi[0] % 2]
                        oi[0] += 1
                        av = acc[:, 0:260].rearrange("p (h c) -> p h c", c=65)
                        S.op("dve", lambda e: e.reciprocal(rd[:, 0:4], av[:, :, 64]), [acc.res], [rd.res])
                        for hl in range(2):
                            h = hp * 2 + hl
                            od = O32[:, r, h * 64:(h + 1) * 64]
                            k.tt(rd[:, 4 + hl:5 + hl], rd[:, 2 * hl + 1:2 * hl + 2], lm[:, 5:6], ALU.mult, [rd, lm], [rd])
                            k.ts(t1[:], av[:, 2 * hl, 0:64], rd[:, 2 * hl:2 * hl + 1], ALU.mult, [acc, rd], [t1])
                            k.stt(od, av[:, 2 * hl + 1, 0:64], rd[:, 4 + hl:5 + hl], t1[:], ALU.mult, ALU.add, [acc, rd, t1], [O32])
                            k.stt(jk[:], od, 1.0, od, ALU.mult, ALU.mult, [O32], [jk, rd], accum_out=rd[:, 6 + hl:7 + hl])
                            k.act(rd[:, 8 + hl:9 + hl], rd[:, 6 + hl:7 + hl], AF.Ln, [rd, eps6], [rd], bias=eps6[:, 0:1], scale=1.0 / 64.0)
                            k.act(rd[:, 10 + hl:11 + hl], rd[:, 8 + hl:9 + hl], AF.Exp, [rd], [rd], scale=-0.5)
                            k.stt(od, od, rd[:, 10 + hl:11 + hl], gd[:], ALU.mult, ALU.mult, [O32, rd, gd], [O32])
                        if hp == 1:
                            b = k.bank(4, 8)
                            for c in range(2):
                                k.tr(b, b[:, c * 128:(c + 1) * 128], O32[:, r, c * 128:(c + 1) * 128], ident[:], [O32, ident])
                            k.copy("act", OS[:, :, r * 128:(r + 1) * 128], b[:, 0:256].rearrange("p (c n) -> p c n", n=128), [b], [OS])
                    attn_group(g, maps, causal_jlist(g), 65, pts, out_cb)
                k.store(oT[6:8, :, g * 512:(g + 1) * 512].rearrange("c p n -> p c n"), OS[:], OS)
            S.barrier()

    def phase_nsa(l):
        with ExitStack() as st:
            qt = k.tile(st, [128, 2, L], BF16, "qt")
            raw = k.tile(st, [128, L], BF16, "raw")
            ks = k.tile(st, [128, L], BF16, "ks")
            kw = k.tile(st, [128, L], BF16, "kw")
            vb = k.tile(st, [128, NT, 130], BF16, "vb")
            k.load(qt[:], hT[C_BQ:C_BQ + 2, :, :].rearrange("c p n -> p c n"), qt)
            k.load(raw[:], hT[C_KCVC, :, :], raw)
            k.load(ks[:], hT[C_KS, :, :], ks)
            k.load(kw[:], hT[C_KW, :, :], kw)
            k.load(vb[:], vaug_b.rearrange("(i p) c -> p i c", p=128), vb)
            e64 = k.tile(st, [128, L], BF16, "e64")
            k.memset(e64[:], 0.0, [e64])
            k.load(e64[0:64, :], ee64, e64, q="pool")
            cng = k.tile(st, [128, 2, L], BF16, "cng")
            k.load(cng[:], cmpneg.rearrange("(c p) n -> p c n", p=128), cng, q="pool")
            w1t = k.tile(st, [128, 32, 64], BF16, "w1t")
            k.load(w1t[:], nsa_w1[l], w1t, q="pool")
            w2t = k.tile(st, [64, 192], BF16, "w2t")
            k.load(w2t[:], nsa_w2[l], w2t, q="pool")
            posT = k.tile(st, [128, 32], BF16, "posT")
            k.load(posT[:], nsa_posT[l], posT, q="pool")
            vca = k.tile(st, [128, 2, 128], BF16, "vca")
            k.memset(vca[:], 1.0, [vca])
            k.load(vca[:, :, 65:128], ovl.rearrange("(c p) j -> p c j", p=128), vca, q="pool")
            kcT = k.tile(st, [128, 256], BF16, "kcT")
            gates = k.tile(st, [128, NT, 12], F32, "gates")
            k.load(gates[:], smallf[:, 8:20].rearrange("(i p) c -> p i c", p=128), gates)
            k.act(gates[:], gates[:], AF.Exp, [gates], [gates], scale=-1.0)
            k.ts(gates[:], gates[:], 1.0, ALU.add, [gates], [gates])
            S.op("dve", lambda e: e.reciprocal(gates[:], gates[:]), [gates.res], [gates.res])
            nadd = k.tile(st, [128, NT, 63], F32, "nadd")
            k.load(nadd[:], nsa_add.rearrange("(i p) c -> p i c", p=128), nadd)
            ncmp = (L - 32) // 16 + 1
            gT = [k.tile(st, [64, 256], BF16, "gT") for _ in range(2)]
            u = k.tile(st, [64, 256], F32, "u")
            u2 = k.tile(st, [64, 256], F32, "u2")
            cc_ = k.tile(st, [64, 1], F32, "cc")
            for kv in range(2):
                pb = kv * 64
                b = k.bank()
                for j in range(32):
                    rhs = raw[pb:pb + 64, j:j + 16 * (ncmp - 1) + 1:16]
                    k.mm(b, b[0:64, 0:ncmp], w1t[pb:pb + 64, j, :], rhs, j == 0, [w1t, raw])
                for j in range(32):
                    k.mm(b, b[0:64, 256:257], w1t[pb:pb + 64, j, :], posT[pb:pb + 64, j:j + 1], False, [w1t, posT])
                k.copy("dve", cc_[:], b[0:64, 256:257], [b], [cc_])
                k.memset(u[:], 0.0, [u])
                k.act(u[:, 0:ncmp], b[0:64, 0:ncmp], AF.Identity, [b, cc_], [u], bias=cc_[:, 0:1])
                k.tt(u2[:], u[:], u[:], ALU.mult, [u], [u2])
                k.ts(u2[:], u2[:], 0.044715, ALU.mult, [u2], [u2], s2=1.0, op1=ALU.add)
                k.tt(u2[:], u2[:], u[:], ALU.mult, [u2, u], [u2])
                k.act(u2[:], u2[:], AF.Tanh, [u2], [u2], scale=0.7978845608028654)
                k.ts(u2[:], u2[:], 1.0, ALU.add, [u2], [u2], s2=0.5, op1=ALU.mult)
                k.tt(gT[kv][:], u2[:], u[:], ALU.mult, [u2, u], [gT[kv]])
            b = k.bank()
            k.mm(b, b[:, 0:256], w2t[:, 0:128], gT[0][:], True, [w2t, gT[0]])
            k.copy("dve", kcT[:], b[:, 0:256], [b], [kcT])
            for c in range(2):
                b = k.bank()
                k.mm(b, b[:, 0:64], gT[1][:, c * 128:(c + 1) * 128], w2t[:, 128:192], True, [w2t, gT[1]])
                k.copy("dve", vca[:, c, 0:64], b[:, 0:64], [b], [vca])
            import os
            nsadbg = int(os.environ.get("NSADBG", "0"))
            if nsadbg == 1:
                S.barrier()
                return
            pts = [k.tile(st, [128, 512], BF16, "pt") for _ in range(6)]
            oacc = [k.tile(st, [128, 4, 256], F32, "oacc") for _ in range(2)]
            rds = [k.tile(st, [128, 12], F32, "rd") for _ in range(2)]
            imps = [k.tile(st, [128, 64], F32, "imp") for _ in range(2)]
            imp2 = [k.tile(st, [128, 64], F32, "imp2") for _ in range(2)]
            m8s = [k.tile(st, [128, 16], F32, "m8") for _ in range(2)]
            bmn = [k.tile(st, [128, 64], F32, "bmn") for _ in range(2)]
            for t_ in bmn:
                k.memset(t_[:], 1.0, [t_])
            mks = [k.tile(st, [128, 512], BF16, "mk") for _ in range(3)]
            mki = [0]
            bmT = [k.tile(st, [128, 512], BF16, "bmT") for _ in range(2)]
            for t_ in bmT:
                k.memset(t_[:], 0.0, [t_])
            ostg = [k.tile(st, [128, 2, 512], BF16, "ostg") for _ in range(2)]
            oi = [0]
            for g in range(NG):
                OA = oacc[g % 2]
                BT = bmT[g % 2]
                OS = ostg[g % 2]
                maps = []
                for h in range(4):
                    pb = (h % 2) * 64
                    c = h // 2
                    maps.append(dict(
                        kt=(lambda j, pb=pb: kcT[pb:pb + 64, j * 128:(j + 1) * 128]),
                        qt=qt[pb:pb + 64, c, g * 512:(g + 1) * 512], scale=0.125, cb=None, bm={},
                        mm=(lambda j, g=g: (identb[:], cng[:, j, g * 512:(g + 1) * 512], [identb, cng])),
                        v=(lambda j: vca[:, j, :]), reads=[kcT, qt], vreads=[vca]))

                def cb_cmp(r, acc, g=g, OA=OA, BT=BT):
                    i = 4 * g + r
                    rd = rds[oi[0] % 2]
                    imp, i2, m8, bn = imps[oi[0] % 2], imp2[oi[0] % 2], m8s[oi[0] % 2], bmn[oi[0] % 2]
                    oi[0] += 1
                    av = acc[:, :].rearrange("p (h c) -> p h c", c=128)
                    k.ts(rd[:, 0:4], av[:, :, 64], 1e-30, ALU.max, [acc], [rd])
                    S.op("dve", lambda e: e.reciprocal(rd[:, 0:4], rd[:, 0:4]), [rd.res], [rd.res])
                    k.tt(rd[:, 4:8], rd[:, 0:4], gates[:, i, 0:12:3], ALU.mult, [rd, gates], [rd])
                    for h in range(4):
                        k.ts(OA[:, r, h * 64:(h + 1) * 64], av[:, h, 0:64], rd[:, 4 + h:5 + h], ALU.mult, [acc, rd], [OA])
                    k.ts(imp[:, 0:63], av[:, 0, 65:128], rd[:, 0:1], ALU.mult, [acc, rd], [imp])
                    for h in range(1, 4):
                        k.stt(imp[:, 0:63], av[:, h, 65:128], rd[:, h:h + 1], imp[:, 0:63], ALU.mult, ALU.add, [acc, rd, imp], [imp])
                    k.tt(imp[:, 0:63], imp[:, 0:63], nadd[:, i, :], ALU.add, [imp, nadd], [imp])
                    S.op("dve", lambda e: e.max(out=m8[:, 0:8], in_=imp[:, 0:63]), [imp.res], [m8.res])
                    S.op("dve", lambda e: e.match_replace(out=i2[:, 0:63], in_to_replace=m8[:, 0:8], in_values=imp[:, 0:63], imm_value=-3e6),
                         [imp.res, m8.res], [i2.res])
                    S.op("dve", lambda e: e.max(out=m8[:, 8:16], in_=i2[:, 0:63]), [i2.res], [m8.res])
                    k.ts(bn[:, 1:64], imp[:, 0:63], m8[:, 14:15], ALU.is_ge, [imp, m8], [bn])
                    b = k.bank(4, 8)
                    k.tr(b, b[0:64, 0:128], bn[:, :], ident[:], [bn, ident])
                    k.copy("act", BT[0:64, r * 128:(r + 1) * 128], b[0:64, 0:128], [b], [BT])
                jl = [(j, 0, 4) for j in range(2) if 16 * 128 * j + 31 <= 512 * g + 511]
                attn_group(g, maps, jl, 128, pts, cb_cmp)

                mkc = {}

                def get_mk(j, BT=BT, mkc=mkc):
                    if j not in mkc:
                        t = mks[mki[0] % 3]
                        mki[0] += 1
                        b = k.bank(4, 8)
                        k.mm(b, b[:, :], e64[:, j * 128:(j + 1) * 128], BT[:, :], True, [e64, BT])
                        k.copy("dve", t[:, :], b[:, :], [b], [t])
                        mkc[j] = t
                    return mkc[j]

                for br in range(2):
                    if nsadbg == 2 or (nsadbg in (3, 4) and br == 1) or (nsadbg == 5 and br == 0):
                        continue
                    maps = []
                    for h in range(4):
                        pb = (h % 2) * 64
                        c = h // 2
                        hh = 4 + h
                        kk = ks if br == 0 else kw
                        bmd = {0: (Bm, Bm[:, hh, 0, :]), 1: (Bm, Bm[:, hh, 1, :])}
                        if br == 1:
                            bmd[4] = (B4, B4[:])
                        maps.append(dict(
                            kt=(lambda j, pb=pb, kk=kk: kk[pb:pb + 64, j * 128:(j + 1) * 128]),
                            qt=qt[pb:pb + 64, c, g * 512:(g + 1) * 512], scale=0.125,
                            cb=cb[:, hh:hh + 1], cbr=[cb], bm=bmd,
                            mm=None, pre=(get_mk if br == 0 else None),
                            pmul=((lambda j: (get_mk(j)[:, :], [get_mk(j)])) if br == 0 else None),
                            v=(lambda j, br=br: vb[:, j, br * 65:(br + 1) * 65]), reads=[kk, qt], vreads=[vb]))

                    def cb_br(r, acc, g=g, br=br, OA=OA, OS=OS):
                        i = 4 * g + r
                        rd = rds[oi[0] % 2]
                        oi[0] += 1
                        av = acc[:, 0:260].rearrange("p (h c) -> p h c", c=65)
                        S.op("dve", lambda e: e.reciprocal(rd[:, 0:4], av[:, :, 64]), [acc.res], [rd.res])
                        k.tt(rd[:, 4:8], rd[:, 0:4], gates[:, i, 1 + br:12:3], ALU.mult, [rd, gates], [rd])
                        for h in range(4):
                            k.stt(OA[:, r, h * 64:(h + 1) * 64], av[:, h, 0:64], rd[:, 4 + h:5 + h], OA[:, r, h * 64:(h + 1) * 64],
                                  ALU.mult, ALU.add, [acc, rd, OA], [OA])
                        if br == 1:
                            b = k.bank(4, 8)
                            for c in range(2):
                                k.tr(b, b[:, c * 128:(c + 1) * 128], OA[:, r, c * 128:(c + 1) * 128], ident[:], [OA, ident])
                            k.copy("act", OS[:, :, r * 128:(r + 1) * 128], b[:, 0:256].rearrange("p (c n) -> p c n", n=128), [b], [OS])
                    attn_group(g, maps, causal_jlist(g) if br == 0 else win_jlist(g), 65, pts, cb_br)
                k.store(oT[2:4, :, g * 512:(g + 1) * 512].rearrange("c p n -> p c n"), OS[:], OS)
            S.barrier()

    def phase_cross(l):
        with ExitStack() as st:
            Wk = load_w(st, wbf[("xk", l)], 8, D, "Wk")
            Wv = load_w(st, wbf[("xv", l)], 8, D, "Wv", split="row")
            Wq = load_w(st, wbf[("xq", l)], 8, D, "Wq")
            mT = k.tile(st, [128, 8, 256], BF16, "mT")
            k.load(mT[:], memT.rearrange("c p m -> p c m"), mT)
            kxT = k.tile(st, [128, 8, 256], BF16, "kxT")
            vx = k.tile(st, [128, 2, D], BF16, "vx")
            for c in range(8):
                b = k.bank()
                for kc in range(8):
                    k.mm(b, b[:, 0:256], Wk[:, kc, c * 128:(c + 1) * 128], mT[:, kc, :], kc == 0, [Wk.sub[c // 4], mT])
                k.copy("act", kxT[:, c, :], b[:, 0:256], [b], [kxT])
            for mc in range(2):
                for nh in range(2):
                    b = k.bank()
                    for kc in range(8):
                        k.mm(b, b[:, :], mT[:, kc, mc * 128:(mc + 1) * 128], Wv[:, kc, nh * 512:(nh + 1) * 512], kc == 0, [Wv.sub[kc], mT])
                    k.copy("act", vx[:, mc, nh * 512:(nh + 1) * 512], b[:, :], [b], [vx])
            at = [k.tile(st, [128, 8, 512], BF16, "at") for _ in range(2)]
            qx = [k.tile(st, [128, 8, 512], BF16, "qx") for _ in range(2)]
            Es = [k.tile(st, [128, 2, 512], BF16, "E") for _ in range(2)]
            rdn = [k.tile(st, [128, 512], F32, "rdn") for _ in range(2)]
            ostg = [k.tile(st, [128, 8, 512], BF16, "ostg") for _ in range(2)]
            sc = 256 ** -0.5
            ei = 0
            def ld(g):
                k.load(at[g % 2][:], xT[:, :, g * 512:(g + 1) * 512].rearrange("c p n -> p c n"), at[g % 2])
            ld(0)
            for g in range(NG):
                a, Q, OS = at[g % 2], qx[g % 2], ostg[g % 2]
                if g + 1 < NG:
                    ld(g + 1)
                for c in range(8):
                    b = k.bank()
                    for kc in range(8):
                        k.mm(b, b[:, :], Wq[:, kc, c * 128:(c + 1) * 128], a[:, kc, :], kc == 0, [Wq.sub[c // 4], a])
                    k.copy("act" if c % 2 == 0 else "dve", Q[:, c, :], b[:, :], [b], [Q])
                for h in range(4):
                    E, rd = Es[ei % 2], rdn[ei % 2]
                    ei += 1
                    for mc in range(2):
                        b = k.bank()
                        for dc in range(2):
                            k.mm(b, b[:, :], kxT[:, 2 * h + dc, mc * 128:(mc + 1) * 128], Q[:, 2 * h + dc, :], dc == 0, [kxT, Q])
                        k.act(E[:, mc, :], b[:, :], AF.Exp, [b], [E], scale=sc)
                    b = k.bank()
                    for mc in range(2):
                        k.mm(b, b[:, :], onesb[:], E[:, mc, :], mc == 0, [onesb, E])
                    S.op("dve", lambda e: e.reciprocal(rd[:], b[:, :]), [b.res], [rd.res])
                    for dc in range(2):
                        b = k.bank()
                        for mc in range(2):
                            k.mm(b, b[:, :], vx[:, mc, (2 * h + dc) * 128:(2 * h + dc + 1) * 128], E[:, mc, :], mc == 0, [vx, E])
                        k.tt(OS[:, 2 * h + dc, :], b[:, :], rd[:], ALU.mult, [b, rd], [OS])
                k.store(oT[:, :, g * 512:(g + 1) * 512].rearrange("c p n -> p c n"), OS[:], OS)
            S.barrier()

    with ExitStack() as st:
        xr = [k.tile(st, [128, 2, D], F32, "xr") for _ in range(2)]
        xts = [k.tile(st, [128, 8, 256], BF16, "xts") for _ in range(2)]
        for g in range(L // 256):
            X, XT = xr[g % 2], xts[g % 2]
            k.load(X[:], x_in[g * 256:(g + 1) * 256, :].rearrange("(r p) d -> p r d", p=128), X)
            for r in range(2):
                for c0 in range(0, 8, 4):
                    b = k.bank()
                    for cc in range(4):
                        c = c0 + cc
                        k.tr(b, b[:, cc * 128:(cc + 1) * 128], X[:, r, c * 128:(c + 1) * 128], ident[:], [X, ident])
                    k.copy("act" if c0 == 0 else "dve", XT[:, c0:c0 + 4, r * 128:(r + 1) * 128],
                           b[:, :].rearrange("p (c n) -> p c n", n=128), [b], [XT])
            k.store(xT[:, :, g * 256:(g + 1) * 256].rearrange("c p n -> p c n"), XT[:], XT)
        S.barrier()

    if stop_after == "xt0":
        return k
    cur_x = x_in
    pp = 0
    done = False
    for l in range(depth):
        last = (l == depth - 1)
        phase_fm(xT, 8, L, wbf[("w_in_fm", l)], NFM, hT)
        if stop_after == "fm":
            break
        phase_tm(l)
        if stop_after == "proj":
            break
        import os
        only = os.environ.get("ONLY", "")
        if l + 1 < depth:
            convert_layer(l + 1)
        if only in ("", "dsa_sel", "dsa"):
            phase_dsa_select(l)
        if only in ("", "dsa"):
            phase_dsa_attn(l)
        if only in ("", "nsa"):
            phase_nsa(l)
        if only in ("", "moba"):
            phase_moba(l)
        if only in ("", "diff"):
            phase_diff(l)
        if stop_after == "mix":
            break
        nxt = xres[pp]
        phase_outln(oT, 8, wbf[("w_out", l)], lngb[l], 0, cur_x, nxt)
        cur_x = nxt
        pp ^= 1
        phase_cross(l)
        nxt = xres[pp]
        phase_outln(oT, 8, wbf[("xo", l)], lngb[l], 2, cur_x, nxt)
        cur_x = nxt
        pp ^= 1
        phase_fm(xT, 8, L, wbf[("w1", l)], 32, hmlp, relu2=True)
        nxt = out_d if last else xres[pp]
        phase_outln(hmlp, 32, wbf[("w2", l)], lngb[l], 4, cur_x, nxt)
        cur_x = nxt
        pp ^= 1
    if stop_after is not None:
        pass
    S.barrier()
    k.ninst = S.ninst
    return k


def host_consts(L, rel_bias):
    c = {}
    s = np.arange(128)[:, None]
    t = np.arange(128)[None, :]
    G = np.zeros((128, 16, 2, 128), np.float32)
    for d in range(2):
        bk = rel_bucket_np(t - s + 128 * d)
        G[:, :, d, :] = np.transpose(rel_bias[bk], (0, 2, 1))
    c["biasG"] = G
    M = np.zeros((128, 3, 128), np.float32)
    M[:, 0, :] = np.where(t - s >= 0, 0.0, NEGM)
    M[:, 1, :] = np.where(s > t, 0.0, NEGM)
    c["maskM"] = M
    c["cbias"] = np.ascontiguousarray(np.broadcast_to(rel_bias[31][None, :], (128, 16))).astype(np.float32)
    c["identf"] = np.eye(128, dtype=np.float32)
    pos = np.arange(L)
    c["ee64"] = (pos[None, :] // 64 == np.arange(64)[:, None]).astype(np.float32)
    c["ee256"] = (pos[None, :] // 256 == np.arange(16)[:, None]).astype(np.float32)
    n = np.arange(256)[:, None]
    cm = np.where((16 * n + 31 <= pos[None, :]) & (n < (L - 32) // 16 + 1), 0.0, NEGM).astype(np.float32)
    c["cmpneg"] = cm
    cs = np.arange(256) * 16
    ss = np.arange(1, 64) * 64
    ov = ((cs[:, None] <= ss[None, :] + 63) & (cs[:, None] + 31 >= ss[None, :])).astype(np.float32)
    ov[(L - 32) // 16 + 1:, :] = 0.0
    c["ovl"] = ov
    j = np.arange(1, 64)[None, :]
    cur = (pos // 64)[:, None]
    add = np.zeros((L, 63), np.float32)
    add = np.where((j == cur) | (j == cur - 1), 1e6 + 16.0 * j, add)
    add = np.where(j > cur, -1e6 - 16.0 * j, add)
    c["nsa_add"] = add.astype(np.float32)
    ma = np.zeros((128, 16, 4, 16), np.float32)
    for cu in range(16):
        ma[:, cu, :, cu:] = -1e30
    c["moba_add"] = ma
    tt_ = np.arange(128)[:, None]
    s_ = np.arange(128)[None, :]
    c["causneg"] = np.where(s_ <= tt_, 0.0, -1e30).astype(np.float32)
    return c


def host_weights(inp, depth):
    w = {}
    w["w_in_p"] = np.ascontiguousarray(inp["w_in"][:depth][:, :, FM_COLS + TM_COLS])
    for nme in ("w_out", "xq", "xk", "xv", "xo", "mlp_w1", "mlp_w2"):
        w[nme] = np.ascontiguousarray(inp[nme][:depth])
    gb = np.stack([inp["ln1_g"], inp["ln1_b"], inp["ln2_g"], inp["ln2_b"], inp["ln3_g"], inp["ln3_b"]], axis=1)[:depth]
    w["lngb"] = np.ascontiguousarray(np.broadcast_to(gb[:, None], (depth, 128, 6, D))).astype(np.float32)
    w1k = inp["nsa_w1_k"][:depth].reshape(depth, 32, 64, 64).transpose(0, 2, 1, 3)
    w1v = inp["nsa_w1_v"][:depth].reshape(depth, 32, 64, 64).transpose(0, 2, 1, 3)
    w["nsa_w1"] = np.ascontiguousarray(np.concatenate([w1k, w1v], axis=1))
    w["nsa_w2"] = np.ascontiguousarray(np.concatenate([inp["nsa_w2_k"][:depth], inp["nsa_w2_k"][:depth], inp["nsa_w2_v"][:depth]], axis=2))
    w["nsa_posT"] = np.ascontiguousarray(np.concatenate([inp["nsa_pos_k"][:depth].transpose(0, 2, 1),
                                                         inp["nsa_pos_v"][:depth].transpose(0, 2, 1)], axis=1))
    dl = np.stack([inp["diff_lq1"], inp["diff_lk1"], inp["diff_lq2"], inp["diff_lk2"]], axis=1)[:depth]
    w["diff_l"] = np.ascontiguousarray(np.broadcast_to(dl[:, None], (depth, 128, 4, 32))).astype(np.float32)
    w["diff_g"] = np.ascontiguousarray(np.broadcast_to(inp["diff_g"][:depth][:, None], (depth, 128, 64))).astype(np.float32)
    return w


_CACHE = {}


def kernel(**inputs):
    inp = {k_: np.asarray(v, dtype=np.float32) for k_, v in inputs.items()}
    L, depth = 4096, 4
    if "prog" not in _CACHE:
        _CACHE["prog"] = build(L, depth)
    kb = _CACHE["prog"]
    shared = {}
    shared.update(host_consts(L, inp["rel_bias"]))
    shared.update(host_weights(inp, depth))
    in_maps = []
    for b in range(8):
        m = dict(shared)
        m["x"] = np.ascontiguousarray(inp["x"][b])
        m["mem"] = np.ascontiguousarray(inp["mem"][b])
        in_maps.append(m)
    res = run_bass_kernel_spmd(kb.nc, in_maps, core_ids=list(range(8)))
    return np.stack([np.asarray(r["out"], dtype=np.float32) for r in res.results], axis=0)
```

```python
import math
from contextlib import ExitStack
import numpy as np
import concourse.bass as bass
import concourse.mybir as mybir
from concourse.bass_utils import run_bass_kernel_spmd

F32 = mybir.dt.float32
BF16 = mybir.dt.bfloat16
AF = mybir.ActivationFunctionType
ALU = mybir.AluOpType
AX = mybir.AxisListType

D = 1024
DEPTH_FULL = 4
ALPHA = (2 * DEPTH_FULL) ** 0.25
NEGM = -30000.0
REL_BUCKETS = 32

O_AQ, O_AK, O_AV, O_AQI, O_AKI, O_AW = 0, 256, 512, 768, 1280, 1344
O_BQ, O_BKC, O_BVC, O_BKS, O_BVS, O_BKW, O_BVW, O_BG = 1352, 1608, 1672, 1736, 1800, 1864, 1928, 1992
O_CQ, O_CK, O_CV, O_DQ, O_DK, O_DV = 2004, 2260, 2516, 2772, 3028, 3284


def _r(a, n):
    return list(range(a, a + n))


def _diffcols(o):
    cols = []
    for ch in range(3):
        for p in range(4):
            mi = min(ch * 3 + min(p, 2), 7)
            cols += _r(o + mi * 32, 32)
    return cols


FM_COLS = (_r(O_AQ, 256) + _r(O_AK, 256) + _r(O_AQI, 512) + _r(O_AKI, 64) * 2
           + _r(O_BQ, 256) + _r(O_BKC, 64) + _r(O_BVC, 64) + _r(O_BKS, 64) * 2 + _r(O_BKW, 64) * 2
           + _r(O_CQ, 256) + _r(O_CK, 256) + _diffcols(O_DQ) + _diffcols(O_DK))
NFM = len(FM_COLS) // 128
C_AQ, C_AK, C_QI, C_KI, C_BQ, C_KCVC, C_KS, C_KW, C_CQ, C_CK, C_DQ, C_DK = 0, 2, 4, 8, 9, 11, 12, 13, 14, 16, 18, 21
TM_COLS = (_r(O_AV, 256) + _r(O_CV, 256)
           + _r(O_DV, 256) + _r(O_BVS, 64) + _r(O_BVW, 64) + _r(O_AW, 8) + _r(O_BG, 12))
NTMA, NTMB = 512, 404
NWIN = len(FM_COLS) + len(TM_COLS)


def rel_bucket_np(n):
    n = np.maximum(n, 0)
    me = REL_BUCKETS // 2
    nf = np.maximum(n, me).astype(np.float32)
    large = me + (np.log(nf / np.float32(me)) / np.float32(math.log(128 / me)) * (REL_BUCKETS - me)).astype(np.int32)
    large = np.minimum(large, REL_BUCKETS - 1)
    return np.where(n < me, n, large)


class Res:
    __slots__ = ("w", "r", "ps")

    def __init__(self):
        self.w = None
        self.r = {}
        self.ps = False


class Tile:
    __slots__ = ("t", "res", "sub")

    def __init__(self, t):
        self.t = t
        self.res = Res()
        self.sub = None

    def __getitem__(self, idx):
        return self.t[idx]


class Sched:
    def __init__(self, nc, stack):
        self.nc = nc
        self.engs = {"pe": nc.tensor, "act": nc.scalar, "dve": nc.vector, "pool": nc.gpsimd, "sp": nc.sync}
        self.sem = {k: stack.enter_context(nc.semaphore("s_" + k)) for k in self.engs}
        self.cnt = {k: 0 for k in self.engs}
        self.seen = {k: {} for k in self.engs}
        self.R = 10
        self.dq = {q: dict(sems=[stack.enter_context(nc.semaphore("d_%s%d" % (q, i))) for i in range(self.R)], n=0)
                   for q in ("sp", "pool", "poolc")}
        self.engs["poolc"] = nc.gpsimd
        self.seen["poolc"] = self.seen["pool"]
        self.ninst = 0

    def _wait(self, e, tok):
        key, h, v = tok
        if self.seen[e].get(key, 0) >= v:
            return
        self.engs[e].wait_ge(h, v)
        self.seen[e][key] = v

    def _deps(self, e, reads, writes):
        for b in reads:
            if b.w is not None:
                if not (e == "pe" and b.w[0] == "pe"):
                    self._wait(e, b.w)
        for b in writes:
            if b.w is not None:
                if not (e == "pe" and b.w[0] == "pe"):
                    self._wait(e, b.w)
            for t in list(b.r.values()):
                if not (e == "pe" and t[0] == "pe"):
                    self._wait(e, t)

    def _commit(self, tok, reads, writes):
        for b in writes:
            b.w = tok
            b.r = {}
        for b in reads:
            b.r[tok[0]] = tok

    def op(self, e, fn, reads=(), writes=()):
        reads = [x.res if isinstance(x, Tile) else x for x in reads]
        writes = [x.res if isinstance(x, Tile) else x for x in writes]
        if e != "pe":
            writes = writes + [x for x in reads if x.ps and x not in writes]
        self._deps(e, reads, writes)
        ins = fn(self.engs[e])
        self.cnt[e] += 1
        ins.then_inc(self.sem[e], 1)
        self._commit((e, self.sem[e], self.cnt[e]), reads, writes)
        self.ninst += 1

    def dma(self, q, out, in_, reads=(), writes=()):
        reads = [x.res if isinstance(x, Tile) else x for x in reads]
        writes = [x.res if isinstance(x, Tile) else x for x in writes]
        d = self.dq[q]
        i = d["n"] % self.R
        k = d["n"] // self.R
        d["n"] += 1
        key = "d_%s%d" % (q, i)
        h = d["sems"][i]
        if k > 0:
            self._wait(q, (key, h, 16 * k))
        self._deps(q, reads, writes)
        self.engs[q].dma_start(out=out, in_=in_).then_inc(h, 16)
        self._commit((key, h, 16 * (k + 1)), reads, writes)
        self.ninst += 1

    def barrier(self):
        for q, d in self.dq.items():
            if q == "poolc":
                continue
            n = d["n"]
            for i in range(self.R):
                if n > i:
                    last = ((n - 1 - i) // self.R) + 1
                    self._wait("sp", ("d_%s%d" % (q, i), d["sems"][i], 16 * last))
        for e in ("pe", "act", "dve", "pool"):
            if self.cnt[e] > 0:
                self._wait("sp", (e, self.sem[e], self.cnt[e]))
        self.nc.sync.nop().then_inc(self.sem["sp"], 1)
        self.cnt["sp"] += 1
        tok = ("sp", self.sem["sp"], self.cnt["sp"])
        for e in ("pe", "act", "dve", "pool"):
            self._wait(e, tok)


class K:
    def __init__(self, L, depth, debug=False):
        self.L, self.depth, self.debug = L, depth, debug
        self.NT, self.NG = L // 128, L // 512
        self.nc = bass.Bass("TRN2", target_bir_lowering=False)
        self.stack = ExitStack()
        self.S = Sched(self.nc, self.stack)
        self.dbg_names = []
        self.banks = []
        for i in range(8):
            t = self.stack.enter_context(self.nc.psum_tensor("bank%d" % i, [128, 512], F32))
            self.banks.append(Tile(t))
            self.banks[-1].res.ps = True
        self.bank_i = 0
        self.ntile = 0

    def din(self, name, shape, dt=F32):
        return self.nc.dram_tensor(name, list(shape), dt, kind="ExternalInput").ap()

    def dscr(self, name, shape, dt, dbg=False):
        kind = "ExternalOutput" if (self.debug and dbg) else "Internal"
        if self.debug and dbg:
            self.dbg_names.append(name)
        return self.nc.dram_tensor(name, list(shape), dt, kind=kind).ap()

    def tile(self, st, shape, dt, name=None):
        self.ntile += 1
        t = st.enter_context(self.nc.sbuf_tensor("%s_%d" % (name or "t", self.ntile), list(shape), dt))
        return Tile(t)

    def bank(self, lo=0, hi=8):
        n = hi - lo
        b = self.banks[lo + (self.bank_i % n)]
        self.bank_i += 1
        return b

    def mm(self, bank, out, lhsT, rhs, start, reads):
        self.S.op("pe", lambda e: e.matmul(out, lhsT=lhsT, rhs=rhs, start=start, stop=True, skip_group_check=True),
                  reads=reads, writes=[bank])

    def tr(self, bank, out, in_, ident, reads):
        self.S.op("pe", lambda e: e.transpose(out, in_, ident), reads=reads, writes=[bank])

    def act(self, out, in_, func, reads, writes, bias=None, scale=None, accum_out=None):
        kw = {}
        if bias is not None:
            kw["bias"] = bias
        if scale is not None:
            kw["scale"] = scale
        if accum_out is not None:
            kw["accum_out"] = accum_out
        self.S.op("act", lambda e: e.activation(out=out, in_=in_, func=func, **kw), reads=reads, writes=writes)

    def ts(self, out, in0, s1, op0, reads, writes, s2=None, op1=None, accum_out=None, eng="dve"):
        kw = {}
        if op1 is not None:
            kw["op1"] = op1
        if accum_out is not None:
            kw["accum_out"] = accum_out
        self.S.op(eng, lambda e: e.tensor_scalar(out=out, in0=in0, scalar1=s1, scalar2=s2, op0=op0, **kw),
                  reads=reads, writes=writes)

    def tt(self, out, in0, in1, op, reads, writes, eng="dve"):
        self.S.op(eng, lambda e: e.tensor_tensor(out=out, in0=in0, in1=in1, op=op), reads=reads, writes=writes)

    def stt(self, out, in0, scalar, in1, op0, op1, reads, writes, accum_out=None):
        kw = {}
        if accum_out is not None:
            kw["accum_out"] = accum_out
        self.S.op("dve", lambda e: e.scalar_tensor_tensor(out=out, in0=in0, scalar=scalar, in1=in1, op0=op0, op1=op1, **kw),
                  reads=reads, writes=writes)

    def copy(self, eng, out, in_, reads, writes):
        if eng == "act":
            self.S.op("act", lambda e: e.activation(out=out, in_=in_, func=AF.Copy), reads=reads, writes=writes)
        else:
            self.S.op(eng, lambda e: e.tensor_copy(out=out, in_=in_), reads=reads, writes=writes)

    def memset(self, out, val, writes, eng="dve"):
        self.S.op(eng, lambda e: e.memset(out, val), reads=[], writes=writes)

    def load(self, out, in_, tile, q="sp"):
        self.S.dma(q, out, in_, reads=[], writes=[tile])

    def store(self, out, in_, tile, q="sp"):
        self.S.dma(q, out, in_, reads=[tile], writes=[])


def build(L=4096, depth=4, debug=False, stop_after=None):
    k = K(L, depth, debug)
    nc, S = k.nc, k.S
    NT, NG = k.NT, k.NG
    x_in = k.din("x", [L, D])
    mem_in = k.din("mem", [256, D])
    w_in = k.din("w_in_p", [depth, D, NWIN])
    w_out = k.din("w_out", [depth, D, D])
    xq = k.din("xq", [depth, D, D])
    xk = k.din("xk", [depth, D, D])
    xv = k.din("xv", [depth, D, D])
    xo = k.din("xo", [depth, D, D])
    w1 = k.din("mlp_w1", [depth, D, 4 * D])
    w2 = k.din("mlp_w2", [depth, 4 * D, D])
    lngb = k.din("lngb", [depth, 128, 6, D])
    biasG = k.din("biasG", [128, 16, 2, 128])
    maskM = k.din("maskM", [128, 3, 128])
    cbias = k.din("cbias", [128, 16])
    identf = k.din("identf", [128, 128])
    ee64 = k.din("ee64", [64, L])
    ee256 = k.din("ee256", [16, L])
    cmpneg = k.din("cmpneg", [256, L])
    ovl = k.din("ovl", [256, 63])
    nsa_add = k.din("nsa_add", [L, 63])
    nsa_w1 = k.din("nsa_w1", [depth, 128, 32, 64])
    nsa_w2 = k.din("nsa_w2", [depth, 64, 192])
    nsa_posT = k.din("nsa_posT", [depth, 128, 32])
    diff_l = k.din("diff_l", [depth, 128, 4, 32])
    diff_g = k.din("diff_g", [depth, 128, 64])
    moba_add = k.din("moba_add", [128, 16, 4, 16])
    causneg = k.din("causneg", [128, 128])
    out_d = nc.dram_tensor("out", [L, D], F32, kind="ExternalOutput").ap()

    xres = [k.dscr("xresA", [L, D], F32), k.dscr("xresB", [L, D], F32)]
    xT = k.dscr("xT", [8, 128, L], BF16, dbg=True)
    hT = k.dscr("hT", [NFM, 128, L], BF16, dbg=True)
    vaug_a = k.dscr("vaug_a", [L, 260], BF16, dbg=True)
    vaug_c = k.dscr("vaug_c", [L, 260], BF16)
    vaug_d = k.dscr("vaug_d", [L, 260], BF16)
    vaug_b = k.dscr("vaug_b", [L, 130], BF16)
    smallf = k.dscr("smallf", [L, 20], F32, dbg=True)
    negT_d = k.dscr("negT", [NG, NT, 128, 512], BF16)
    oT = k.dscr("oT", [8, 128, L], BF16, dbg=True)
    memT = k.dscr("memT", [8, 128, 256], BF16)
    hmlp = k.dscr("hmlp", [32, 128, L], BF16)

    wspecs = [("w_in_fm", lambda l: w_in[l, :, 0:NFM * 128], D, NFM * 128), ("w_in_tm", lambda l: w_in[l, :, NFM * 128:NWIN], D, NTMA + NTMB),
              ("w_out", lambda l: w_out[l], D, D), ("xk", lambda l: xk[l], D, D), ("xv", lambda l: xv[l], D, D),
              ("xq", lambda l: xq[l], D, D), ("xo", lambda l: xo[l], D, D), ("w1", lambda l: w1[l], D, 4 * D), ("w2", lambda l: w2[l], 4 * D, D)]
    wbf = {}

    def convert_layer(l):
        for nme, src, rows, cols in wspecs:
            dst = k.dscr("wb_%s_%d" % (nme, l), [rows, cols], BF16)
            rl = []
            for r0 in range(0, rows, 256):
                rr = Res()
                S.dma("poolc", dst[r0:r0 + 256, :], src(l)[r0:r0 + 256, :], reads=[], writes=[rr])
                rl.append(rr)
            wbf[(nme, l)] = (dst, rl)

    convert_layer(0)

    gst = k.stack
    ident = k.tile(gst, [128, 128], F32, "ident")
    identb = k.tile(gst, [128, 128], BF16, "identb")
    onesb = k.tile(gst, [128, 128], BF16, "onesb")
    eps5 = k.tile(gst, [128, 1], F32, "eps5")
    eps6 = k.tile(gst, [128, 1], F32, "eps6")
    cb = k.tile(gst, [128, 16], F32, "cb")
    Bm = k.tile(gst, [128, 16, 2, 128], BF16, "Bm")
    B4 = k.tile(gst, [128, 128], BF16, "B4")

    k.load(ident[:], identf, ident)
    k.load(cb[:], cbias, cb)
    k.copy("dve", identb[:], ident[:], [ident], [identb])
    k.memset(onesb[:], 1.0, [onesb])
    k.memset(eps5[:], 1e-5, [eps5])
    k.memset(eps6[:], 1e-6, [eps6])
    with ExitStack() as st:
        bg = k.tile(st, [128, 16, 2, 128], F32, "bg")
        mk = k.tile(st, [128, 3, 128], F32, "mk")
        k.load(bg[:], biasG, bg)
        k.load(mk[:], maskM, mk)
        for h in range(16):
            inv = 1.0 / (32 ** -0.5) if h >= 12 else 1.0 / (64 ** -0.5)
            for d in range(2):
                k.ts(bg[:, h, d, :], bg[:, h, d, :], cb[:, h:h + 1], ALU.subtract, [bg, cb], [bg], s2=inv, op1=ALU.mult)
            k.tt(Bm[:, h, 0, :], bg[:, h, 0, :], mk[:, 0, :], ALU.add, [bg, mk], [Bm])
            k.copy("dve", Bm[:, h, 1, :], bg[:, h, 1, :], [bg], [Bm])
        k.copy("dve", B4[:], mk[:, 1, :], [mk], [B4])
        mt = k.tile(st, [128, 2, D], F32, "mt")
        mts = k.tile(st, [128, 8, 256], BF16, "mts")
        k.load(mt[:], mem_in.rearrange("(c p) d -> p c d", p=128), mt)
        for c in range(8):
            b = k.bank()
            for mc in range(2):
                k.tr(b, b[:, mc * 128:(mc + 1) * 128], mt[:, mc, c * 128:(c + 1) * 128], ident[:], [mt, ident])
            k.copy("act", mts[:, c, :], b[:, 0:256], [b], [mts])
        k.store(memT.rearrange("c p m -> p c m"), mts[:], mts)
        S.barrier()
    if stop_after == "setup":
        return k

    def load_w(st, wd, KC, M, name, split="col"):
        W = k.tile(st, [128, KC, M], BF16, name)
        wd, wres = wd
        if split == "row":
            W.sub = [Res() for _ in range(KC)]
            for kc in range(KC):
                S.dma("sp", W[:, kc, :], wd[kc * 128:(kc + 1) * 128, :], reads=[wres[(kc * 128) // 256]], writes=[W.sub[kc]])
        else:
            nb = (M + 511) // 512
            W.sub = [Res() for _ in range(nb)]
            for cb in range(nb):
                c0, c1 = cb * 512, min(M, (cb + 1) * 512)
                S.dma("sp", W[:, :, c0:c1], wd[:, c0:c1].rearrange("(kc p) n -> p kc n", p=128), reads=wres, writes=[W.sub[cb]])
        return W

    def phase_fm(actT, KC, ncols, wd, nchunks, outd, relu2=False):
        GW = min(512, ncols)
        with ExitStack() as st:
            W = load_w(st, wd, KC, nchunks * 128, "Wfm")
            at = [k.tile(st, [128, KC, GW], BF16, "at") for _ in range(2)]
            CB = 4
            stg = [k.tile(st, [128, CB, GW], BF16, "stg") for _ in range(3)]
            rt = [k.tile(st, [128, GW], F32, "rt") for _ in range(2)] if relu2 else None
            si = 0
            ev = 0
            ngr = ncols // GW

            def ld(g):
                k.load(at[g % 2][:], actT[:, :, g * GW:(g + 1) * GW].rearrange("c p n -> p c n"), at[g % 2])
            ld(0)
            for g in range(ngr):
                a = at[g % 2]
                if g + 1 < ngr:
                    ld(g + 1)
                for c0 in range(0, nchunks, CB):
                    sg = stg[si % 3]
                    si += 1
                    nb = min(CB, nchunks - c0)
                    for cc in range(nb):
                        c = c0 + cc
                        b = k.bank()
                        for kc in range(KC):
                            k.mm(b, b[:, 0:GW], W[:, kc, c * 128:(c + 1) * 128], a[:, kc, :], kc == 0, [W.sub[c // 4], a])
                        if relu2:
                            r = rt[ev % 2]
                            k.act(r[:], b[:, 0:GW], AF.Relu, [b], [r])
                            k.tt(sg[:, cc, :], r[:], r[:], ALU.mult, [r], [sg], eng="pool")
                        else:
                            k.copy("act" if ev % 2 == 0 else "dve", sg[:, cc, :], b[:, 0:GW], [b], [sg])
                        ev += 1
                    k.store(outd[c0:c0 + nb, :, g * GW:(g + 1) * GW].rearrange("c p n -> p c n"), sg[:, 0:nb, :], sg)
            S.barrier()

    def phase_tm(l):
        with ExitStack() as st:
            W = load_w(st, wbf[("w_in_tm", l)], 8, NTMA + NTMB, "Wtm")
            at = [k.tile(st, [128, 8, 512], BF16, "at") for _ in range(2)]
            sa = [k.tile(st, [128, 4, 4, 65], BF16, "sa") for _ in range(2)]
            sc = [k.tile(st, [128, 4, 4, 65], BF16, "sc") for _ in range(2)]
            sd = [k.tile(st, [128, 4, 4, 65], BF16, "sd") for _ in range(2)]
            sb = [k.tile(st, [128, 4, 2, 65], BF16, "sb") for _ in range(2)]
            sf = [k.tile(st, [128, 4, 20], F32, "sf") for _ in range(2)]
            import os
            tmdbg = int(os.environ.get("TMDBG", "0"))
            for tl in sa + sc + sd + sb:
                if not (tmdbg & 64):
                    k.memset(tl[:], 1.0, [tl])
            def ld(g):
                k.load(at[g % 2][:], xT[:, :, g * 512:(g + 1) * 512].rearrange("c p n -> p c n"), at[g % 2])
            ld(0)
            for g in range(NG):
                a = at[g % 2]
                if g + 1 < NG:
                    ld(g + 1)
                A, C, Dd, B, Fs = sa[g % 2], sc[g % 2], sd[g % 2], sb[g % 2], sf[g % 2]
                for r in range(4):
                    b = k.bank()
                    for kc in range(8):
                        k.mm(b, b[:, 0:512], a[:, kc, r * 128:(r + 1) * 128], W[:, kc, 0:512], kc == 0, [a, W.sub[0]])
                    if not (tmdbg & 8):
                        k.copy("act", A[:, r, :, 0:64], b[:, 0:256].rearrange("p (h d) -> p h d", d=64), [b], [A])
                    if not (tmdbg & 128):
                        k.copy("act", C[:, r, :, 0:64], b[:, 256:512].rearrange("p (h d) -> p h d", d=64), [b], [C])
                    b = k.bank()
                    for kc in range(8):
                        k.mm(b, b[:, 0:NTMB], a[:, kc, r * 128:(r + 1) * 128], W[:, kc, 512:512 + NTMB], kc == 0, [a, W.sub[1]])
                    if not (tmdbg & 16):
                        k.copy("dve", Dd[:, r, :, 0:64], b[:, 0:256].rearrange("p (h d) -> p h d", d=64), [b], [Dd])
                    if not (tmdbg & 256):
                        k.copy("dve", B[:, r, :, 0:64], b[:, 256:384].rearrange("p (h d) -> p h d", d=64), [b], [B])
                    if not (tmdbg & 32):
                        k.copy("dve", Fs[:, r, :], b[:, 384:404], [b], [Fs])
                sl = slice(g * 512, (g + 1) * 512)
                import os
                tmdbg = int(os.environ.get("TMDBG", "0"))
                if tmdbg & 1:
                    continue
                k.store(vaug_a[sl, :].rearrange("(r p) c -> p r c", p=128), A[:].rearrange("p r h c -> p r (h c)"), A)
                k.store(vaug_c[sl, :].rearrange("(r p) c -> p r c", p=128), C[:].rearrange("p r h c -> p r (h c)"), C)
                k.store(vaug_d[sl, :].rearrange("(r p) c -> p r c", p=128), Dd[:].rearrange("p r h c -> p r (h c)"), Dd)
                if tmdbg & 2:
                    continue
                k.store(vaug_b[sl, :].rearrange("(r p) c -> p r c", p=128), B[:].rearrange("p r h c -> p r (h c)"), B)
                if tmdbg & 4:
                    continue
                k.store(smallf[sl, :].rearrange("(r p) c -> p r c", p=128), Fs[:], Fs)
            S.barrier()

    def phase_outln(actT, KC, wd, gbl, j0, xin, xout):
        GW = 256
        with ExitStack() as st:
            W = load_w(st, wd, KC, D, "Wo", split="row")
            gb = k.tile(st, [128, 2, D], F32, "gb")
            k.load(gb[:], gbl[:, j0:j0 + 2, :], gb)
            at = [k.tile(st, [128, KC, GW], BF16, "at") for _ in range(2)]
            xr = [k.tile(st, [128, 2, D], F32, "xr") for _ in range(2)]
            zt = [k.tile(st, [128, D], F32, "z") for _ in range(2)]
            xn = [k.tile(st, [128, D], F32, "xn") for _ in range(2)]
            xts = [k.tile(st, [128, 8, GW], BF16, "xts") for _ in range(2)]
            sm = [k.tile(st, [128, 24], F32, "sm") for _ in range(2)]
            ti = 0
            def ld(g):
                k.load(at[g % 2][:], actT[:, :, g * GW:(g + 1) * GW].rearrange("c p n -> p c n"), at[g % 2])
                k.load(xr[g % 2][:], xin[g * GW:(g + 1) * GW, :].rearrange("(r p) d -> p r d", p=128), xr[g % 2])
            ld(0)
            for g in range(L // GW):
                a = at[g % 2]
                X = xr[g % 2]
                XT = xts[g % 2]
                if g + 1 < L // GW:
                    ld(g + 1)
                for r in range(2):
                    z, xo_, s = zt[ti % 2], xn[ti % 2], sm[ti % 2]
                    ti += 1
                    for nh in range(2):
                        b = k.bank()
                        for kc in range(KC):
                            k.mm(b, b[:, :], a[:, kc, r * 128:(r + 1) * 128], W[:, kc, nh * 512:(nh + 1) * 512], kc == 0, [a, W.sub[kc]])
                        k.stt(z[:, nh * 512:(nh + 1) * 512], X[:, r, nh * 512:(nh + 1) * 512], ALPHA, b[:, :],
                              ALU.mult, ALU.add, [X, b], [z])
                    S.op("dve", lambda e: e.bn_stats(s[:, 0:6], z[:, 0:512]), [z.res], [s.res])
                    S.op("dve", lambda e: e.bn_stats(s[:, 6:12], z[:, 512:1024]), [z.res], [s.res])
                    S.op("dve", lambda e: e.bn_aggr(s[:, 12:14], s[:, 0:12]), [s.res], [s.res])
                    k.act(s[:, 14:15], s[:, 13:14], AF.Ln, [s, eps5], [s], bias=eps5[:, 0:1])
                    k.act(s[:, 15:16], s[:, 14:15], AF.Exp, [s], [s], scale=-0.5)
                    k.ts(s[:, 16:17], s[:, 12:13], s[:, 15:16], ALU.mult, [s], [s], s2=-1.0, op1=ALU.mult)
                    k.act(xo_[:], z[:], AF.Identity, [z, s], [xo_], bias=s[:, 16:17], scale=s[:, 15:16])
                    k.tt(xo_[:], xo_[:], gb[:, 0, :], ALU.mult, [xo_, gb], [xo_], eng="pool")
                    k.tt(X[:, r, :], xo_[:], gb[:, 1, :], ALU.add, [xo_, gb], [X])
                    for c0 in range(0, 8, 4):
                        b = k.bank()
                        for cc in range(4):
                            c = c0 + cc
                            k.tr(b, b[:, cc * 128:(cc + 1) * 128], X[:, r, c * 128:(c + 1) * 128], ident[:], [X, ident])
                        k.copy("act", XT[:, c0:c0 + 4, r * 128:(r + 1) * 128],
                               b[:, :].rearrange("p (c n) -> p c n", n=128), [b], [XT])
                k.store(xout[g * GW:(g + 1) * GW, :].rearrange("(r p) d -> p r d", p=128), X[:], X)
                k.store(xT[:, :, g * GW:(g + 1) * GW].rearrange("c p n -> p c n"), XT[:], XT)
            S.barrier()

    ACC0 = 0

    def attn_group(g, maps, jlist, NV, pts, out_cb):
        accs = [k.banks[ACC0 + r] for r in range(4)]
        started = [False] * 4
        steps = [(j, rlo, rhi, mi, m) for (j, rlo, rhi) in jlist for mi, m in enumerate(maps)]
        ptof = {}

        def part_a(idxs):
            bks = {}
            for idx in idxs:
                j, rlo, rhi, mi, m = steps[idx]
                if m.get("pre") is not None:
                    m["pre"](j)
            for idx in idxs:
                j, rlo, rhi, mi, m = steps[idx]
                c0, c1 = rlo * 128, rhi * 128
                b = k.bank(4, 8)
                bks[idx] = b
                k.mm(b, b[:, c0:c1], m["kt"](j), m["qt"][:, c0:c1], True, m["reads"])
            for idx in idxs:
                j, rlo, rhi, mi, m = steps[idx]
                c0, c1 = rlo * 128, rhi * 128
                b = bks[idx]
                for r in range(rlo, rhi):
                    d = 4 * g + r - j
                    if d in m["bm"]:
                        bmt, bmap = m["bm"][d]
                        k.mm(b, b[:, r * 128:(r + 1) * 128], identb[:], bmap, False, [identb, bmt])
                mmx = m["mm"](j) if m.get("mm") else None
                if mmx is not None:
                    lh, rh, rds = mmx
                    k.mm(b, b[:, c0:c1], lh, rh[:, c0:c1], False, rds)
            for idx in idxs:
                j, rlo, rhi, mi, m = steps[idx]
                c0, c1 = rlo * 128, rhi * 128
                b = bks[idx]
                pt = pts[k.pt_i % len(pts)]
                k.pt_i += 1
                if m.get("cb") is not None:
                    k.act(pt[:, c0:c1], b[:, c0:c1], AF.Exp, [b] + m["cbr"], [pt], bias=m["cb"], scale=m["scale"])
                else:
                    k.act(pt[:, c0:c1], b[:, c0:c1], AF.Exp, [b], [pt], scale=m["scale"])
                if m.get("pmul") is not None:
                    mk_ap, mk_rd = m["pmul"](j)
                    k.tt(pt[:, c0:c1], pt[:, c0:c1], mk_ap[:, c0:c1], ALU.mult, [pt] + mk_rd, [pt],
                         eng=("pool" if idx % 4 == 3 else "dve"))
                ptof[idx] = pt

        def part_b(idxs):
            for idx in idxs:
                j, rlo, rhi, mi, m = steps[idx]
                pt = ptof.pop(idx)
                for r in range(rlo, rhi):
                    k.mm(accs[r], accs[r][:, mi * NV:(mi + 1) * NV], pt[:, r * 128:(r + 1) * 128], m["v"](j),
                         not started[r], [pt] + m["vreads"])
                    started[r] = True

        sss = [list(range(i, min(i + 2, len(steps)))) for i in range(0, len(steps), 2)]
        LA = 1
        for si in range(len(sss) + LA):
            if si < len(sss):
                part_a(sss[si])
            if si - LA >= 0:
                part_b(sss[si - LA])
        for r in range(4):
            out_cb(r, accs[r])

    k.pt_i = 0

    def causal_jlist(g):
        return [(j, max(0, j - 4 * g), 4) for j in range(0, 4 * g + 4)]

    def win_jlist(g):
        res = []
        for j in range(max(0, 4 * g - 4), 4 * g + 4):
            rlo = max(0, j - 4 * g)
            rhi = min(4, j + 4 - 4 * g + 1)
            if rhi > rlo:
                res.append((j, rlo, rhi))
        return res

    def finish_tile_out(st_tiles, o32, r, g, ostg, row0):
        b = k.bank(4, 8)
        for c in range(2):
            k.tr(b, b[:, c * 128:(c + 1) * 128], o32[:, c * 128:(c + 1) * 128], ident[:], [o32, ident])
        k.copy("act", ostg[:, :, r * 128:(r + 1) * 128], b[:, 0:256].rearrange("p (c n) -> p c n", n=128), [b], [ostg])

    def phase_dsa_select(l):
        NIT = 20
        with ExitStack() as st:
            qi = k.tile(st, [128, 4, L], BF16, "qi")
            ki = k.tile(st, [128, L], BF16, "ki")
            k.load(qi[:], hT[C_QI:C_QI + 4, :, :].rearrange("c p n -> p c n"), qi)
            k.load(ki[:], hT[C_KI, :, :], ki)
            cneg = k.tile(st, [128, 128], F32, "cneg")
            k.load(cneg[:], causneg, cneg)
            wall = k.tile(st, [128, NT, 8], F32, "wall")
            k.load(wall[:], smallf[:, 0:8].rearrange("(i p) c -> p i c", p=128), wall)
            wabs = k.tile(st, [128, NT, 8], F32, "wabs")
            wsgn = k.tile(st, [128, NT, 8], F32, "wsgn")
            k.ts(wsgn[:], wall[:], 0.0, ALU.is_ge, [wall], [wsgn], s2=2.0, op1=ALU.mult)
            k.ts(wsgn[:], wsgn[:], -1.0, ALU.add, [wsgn], [wsgn])
            k.tt(wabs[:], wall[:], wsgn[:], ALU.mult, [wall, wsgn], [wabs])
            NTL = 4
            accs = [k.tile(st, [128, L], F32, "acc") for _ in range(NTL)]
            junk = {"dve": k.tile(st, [128, L], BF16, "junkd"), "act": k.tile(st, [128, L], BF16, "junka")}
            tmps = [k.tile(st, [128, 512], F32, "tmp") for _ in range(6)]
            sms = [k.tile(st, [128, 8], F32, "sm") for _ in range(NTL)]
            rks = [k.tile(st, [128, 2, NIT], F32, "rk") for _ in range(NTL)]
            ckrow = k.tile(st, [128, NIT], F32, "ckrow")
            for it in range(NIT):
                k.memset(ckrow[:, it:it + 1], 2.0 ** -(it + 1), [ckrow])
            negst = k.tile(st, [128, NT, 512], BF16, "negst")
            k.memset(negst[:], 1.0, [negst])
            tmi = 0
            for g in range(NG):
                tiles = [i for i in range(4 * g, 4 * g + 4) if i >= 2]
                if True:
                    pair = tiles
                    ceng = ["dve" if (pi % 2 == 0) else "act" for pi in range(len(pair))]
                    for pi, i in enumerate(pair):
                        acc = accs[pi]
                        Lk = 128 * (i + 1)
                        for kb in range(0, Lk, 512):
                            n = min(512, Lk - kb)
                            for h in range(8):
                                pb = (h % 2) * 64
                                b = k.bank()
                                k.mm(b, b[:, 0:n], qi[pb:pb + 64, h // 2, i * 128:(i + 1) * 128], ki[pb:pb + 64, kb:kb + n], True, [qi, ki])
                                if h == 0:
                                    k.ts(acc[:, kb:kb + n], b[:, 0:n], 0.0, ALU.max, [b, wall], [acc], s2=wall[:, i, 0:1], op1=ALU.mult)
                                else:
                                    t = tmps[tmi % 6]
                                    tmi += 1
                                    k.act(t[:, 0:n], b[:, 0:n], AF.Relu, [b, wabs], [t], scale=wabs[:, i, h:h + 1])
                                    if False:
                                        k.ts(t[:, 0:n], t[:, 0:n], wsgn[:, i, h:h + 1], ALU.mult, [t, wsgn], [t], eng="pool")
                                        k.tt(acc[:, kb:kb + n], acc[:, kb:kb + n], t[:, 0:n], ALU.add, [acc, t], [acc], eng="pool")
                                    else:
                                        k.stt(acc[:, kb:kb + n], t[:, 0:n], wsgn[:, i, h:h + 1], acc[:, kb:kb + n], ALU.mult, ALU.add, [t, wsgn, acc], [acc])
                        k.tt(acc[:, i * 128:Lk], acc[:, i * 128:Lk], cneg[:], ALU.add, [acc, cneg], [acc])
                        s, rk = sms[pi], rks[pi]
                        S.op("dve", lambda e: e.tensor_reduce(out=s[:, 0:1], in_=acc[:, 0:Lk], axis=AX.X, op=ALU.max), [acc.res], [s.res])
                        S.op("dve", lambda e: e.tensor_reduce(out=s[:, 1:2], in_=acc[:, 0:i * 128], axis=AX.X, op=ALU.min), [acc.res], [s.res])
                        k.tt(s[:, 2:3], s[:, 0:1], s[:, 1:2], ALU.subtract, [s], [s])
                        k.ts(rk[:, 0, :], ckrow[:], s[:, 2:3], ALU.mult, [ckrow, s], [rk])
                        k.ts(rk[:, 1, :], rk[:, 0, :], 2.0, ALU.mult, [rk], [rk])
                        k.tt(s[:, 3:4], s[:, 1:2], rk[:, 0, 0:1], ALU.add, [s, rk], [s])
                    for it in range(NIT):
                        for pi, i in enumerate(pair):
                            acc, s = accs[pi], sms[pi]
                            Lk = 128 * (i + 1)
                            if ceng[pi] == "dve":
                                jk = junk["dve"]
                                k.ts(jk[:, 0:Lk], acc[:, 0:Lk], s[:, 3:4], ALU.is_ge, [acc, s], [jk, s], s2=0.0, op1=ALU.add, accum_out=s[:, 4:5])
                            else:
                                jk = junk["act"]
                                k.act(jk[:, 0:Lk], acc[:, 0:Lk], AF.Sign, [acc, s], [jk, s], bias=s[:, 3:4], scale=-1.0, accum_out=s[:, 4:5])
                        for pi, i in enumerate(pair):
                            s, rk = sms[pi], rks[pi]
                            Lk = 128 * (i + 1)
                            last = (it == NIT - 1)
                            if ceng[pi] == "dve":
                                cmpv, opge, oplt = 255.5, ALU.is_ge, ALU.is_lt
                            else:
                                cmpv, opge, oplt = float(Lk - 511), ALU.is_le, ALU.is_gt
                            if not last:
                                k.stt(s[:, 5:6], s[:, 4:5], cmpv, rk[:, 1, it + 1:it + 2], opge, ALU.mult, [s, rk], [s])
                                k.stt(s[:, 3:4], s[:, 5:6], rk[:, 0, it + 1:it + 2], s[:, 3:4], ALU.subtract, ALU.add, [s, rk], [s])
                            else:
                                k.stt(s[:, 5:6], s[:, 4:5], cmpv, rk[:, 0, it:it + 1], oplt, ALU.mult, [s, rk], [s])
                                k.tt(s[:, 1:2], s[:, 3:4], s[:, 5:6], ALU.subtract, [s], [s])
                    for pi, i in enumerate(pair):
                        acc, s = accs[pi], sms[pi]
                        Lk = 128 * (i + 1)
                        r = i - 4 * g
                        k.ts(acc[:, 0:Lk], acc[:, 0:Lk], s[:, 1:2], ALU.is_ge, [acc, s], [acc])
                        for j0 in range(0, i + 1, 4):
                            nb = min(4, i + 1 - j0)
                            b = k.bank()
                            for jj in range(nb):
                                k.tr(b, b[:, jj * 128:(jj + 1) * 128], acc[:, (j0 + jj) * 128:(j0 + jj + 1) * 128], ident[:], [acc, ident])
                            k.copy("act", negst[:, j0:j0 + nb, r * 128:(r + 1) * 128],
                                   b[:, 0:nb * 128].rearrange("p (c n) -> p c n", n=128), [b], [negst])
                nj = 4 * g + 4
                k.store(negT_d[g, 0:nj, :, :].rearrange("j p n -> p j n"), negst[:, 0:nj, :], negst)
            S.barrier()

    def phase_dsa_attn(l):
        with ExitStack() as st:
            qt = k.tile(st, [128, 2, L], BF16, "qt")
            kt = k.tile(st, [128, 2, L], BF16, "kt")
            va = k.tile(st, [128, NT, 260], BF16, "va")
            k.load(qt[:], hT[C_AQ:C_AQ + 2, :, :].rearrange("c p n -> p c n"), qt)
            k.load(kt[:], hT[C_AK:C_AK + 2, :, :].rearrange("c p n -> p c n"), kt)
            k.load(va[:], vaug_a.rearrange("(i p) c -> p i c", p=128), va)
            ngs = [k.tile(st, [128, 4, 512], BF16, "ng") for _ in range(3)]
            pts = [k.tile(st, [128, 512], BF16, "pt") for _ in range(6)]
            o32s = [k.tile(st, [128, 256], F32, "o32") for _ in range(2)]
            rds = [k.tile(st, [128, 4], F32, "rd") for _ in range(2)]
            ostg = [k.tile(st, [128, 2, 512], BF16, "ostg") for _ in range(2)]
            ngi = [0]
            oi = [0]
            for g in range(NG):
                ngcache = {}

                def get_ng(j, g=g, ngcache=ngcache):
                    jb = j // 4
                    if jb not in ngcache:
                        t = ngs[ngi[0] % 3]
                        ngi[0] += 1
                        k.load(t[:], negT_d[g, jb * 4:jb * 4 + 4, :, :].rearrange("j p n -> p j n"), t)
                        ngcache[jb] = t
                    return ngcache[jb]

                maps = []
                for h in range(4):
                    pb = (h % 2) * 64
                    c = h // 2

                    def mmf(j):
                        t = get_ng(j)
                        return (t[:, j % 4, :], [t])
                    maps.append(dict(
                        kt=(lambda j, pb=pb, c=c: kt[pb:pb + 64, c, j * 128:(j + 1) * 128]),
                        qt=qt[pb:pb + 64, c, g * 512:(g + 1) * 512], scale=0.125,
                        cb=cb[:, h:h + 1], cbr=[cb], bm={0: (Bm, Bm[:, h, 0, :]), 1: (Bm, Bm[:, h, 1, :])},
                        mm=None, pmul=mmf, v=(lambda j, h=h: va[:, j, h * 65:(h + 1) * 65]), reads=[kt, qt], vreads=[va]))
                OS = ostg[g % 2]

                def out_cb(r, acc, g=g, OS=OS):
                    o32 = o32s[oi[0] % 2]
                    rd = rds[oi[0] % 2]
                    oi[0] += 1
                    av = acc[:, 0:260].rearrange("p (h c) -> p h c", c=65)
                    S.op("dve", lambda e: e.reciprocal(rd[:, :], av[:, :, 64]), [acc.res], [rd.res])
                    for h in range(4):
                        k.ts(o32[:, h * 64:(h + 1) * 64], av[:, h, 0:64], rd[:, h:h + 1], ALU.mult, [acc, rd], [o32])
                    finish_tile_out(None, o32, r, g, OS, 0)
                attn_group(g, maps, causal_jlist(g), 65, pts, out_cb)
                k.store(oT[0:2, :, g * 512:(g + 1) * 512].rearrange("c p n -> p c n"), OS[:], OS)
            S.barrier()

    def phase_moba(l):
        with ExitStack() as st:
            qt = k.tile(st, [128, 2, L], BF16, "qt")
            kt = k.tile(st, [128, 2, L], BF16, "kt")
            va = k.tile(st, [128, NT, 260], BF16, "va")
            k.load(qt[:], hT[C_CQ:C_CQ + 2, :, :].rearrange("c p n -> p c n"), qt)
            k.load(kt[:], hT[C_CK:C_CK + 2, :, :].rearrange("c p n -> p c n"), kt)
            k.load(va[:], vaug_c.rearrange("(i p) c -> p i c", p=128), va)
            e256 = k.tile(st, [16, L], BF16, "e256")
            k.load(e256[:], ee256, e256, q="pool")
            madd = k.tile(st, [128, 16, 4, 16], F32, "madd")
            k.load(madd[:], moba_add, madd)
            nb = L // 256
            kmf = k.tile(st, [128, 2, 16], F32, "kmf")
            km = k.tile(st, [128, 2, 16], BF16, "km")
            k.memset(kmf[:], 0.0, [kmf])
            for c in range(2):
                S.op("dve", lambda e: e.tensor_reduce(out=kmf[:, c, 0:nb], in_=kt[:, c, :].rearrange("p (n s) -> p n s", s=256),
                                                       axis=AX.X, op=ALU.add), [kt.res], [kmf.res])
            k.ts(km[:], kmf[:], 1.0 / 256.0, ALU.mult, [kmf], [km])
            pts = [k.tile(st, [128, 512], BF16, "pt") for _ in range(6)]
            o32s = [k.tile(st, [128, 256], F32, "o32") for _ in range(2)]
            rds = [k.tile(st, [128, 4], F32, "rd") for _ in range(2)]
            ostg = [k.tile(st, [128, 2, 512], BF16, "ostg") for _ in range(2)]
            gms = [k.tile(st, [128, 4, 16], F32, "gm") for _ in range(2)]
            m8s = [k.tile(st, [128, 4, 8], F32, "m8") for _ in range(2)]
            bmT = [k.tile(st, [16, 4, 512], BF16, "bmT") for _ in range(2)]
            oi = [0]
            gi = 0
            for g in range(NG):
                BT = bmT[g % 2]
                for r in range(4):
                    i = 4 * g + r
                    cur = i // 2
                    gm, m8 = gms[gi % 2], m8s[gi % 2]
                    gi += 1
                    bA, bB = k.bank(4, 8), k.bank(4, 8)
                    for h in range(4):
                        pb = (h % 2) * 64
                        c = h // 2
                        b = bA if pb == 0 else bB
                        k.mm(b, b[:, (h // 2) * 16:(h // 2) * 16 + 16], qt[pb:pb + 64, c, i * 128:(i + 1) * 128], km[pb:pb + 64, c, :], h < 2, [qt, km])
                    for h in range(4):
                        b = bA if h % 2 == 0 else bB
                        k.tt(gm[:, h, :], b[:, (h // 2) * 16:(h // 2) * 16 + 16], madd[:, cur, h, :], ALU.add, [b, madd], [gm])
                    for h in range(4):
                        S.op("dve", lambda e, h=h: e.max(out=m8[:, h, :], in_=gm[:, h, :]), [gm.res], [m8.res])
                    k.ts(m8[:, :, 2], m8[:, :, 2], -1e29, ALU.max, [m8], [m8])
                    for h in range(4):
                        k.ts(gm[:, h, :], gm[:, h, :], m8[:, h, 2:3], ALU.is_lt, [gm, m8], [gm], s2=NEGM, op1=ALU.mult)
                    k.memset(gm[:, :, cur:cur + 1], 0.0, [gm])
                    b = k.bank(4, 8)
                    for h in range(4):
                        k.tr(b, b[0:16, h * 128:(h + 1) * 128], gm[:, h, :], ident[:], [gm, ident])
                    k.copy("act", BT[:, :, r * 128:(r + 1) * 128], b[0:16, :].rearrange("p (h n) -> p h n", n=128), [b], [BT])
                maps = []
                for h in range(4):
                    pb = (h % 2) * 64
                    c = h // 2
                    hh = 8 + h
                    maps.append(dict(
                        kt=(lambda j, pb=pb, c=c: kt[pb:pb + 64, c, j * 128:(j + 1) * 128]),
                        qt=qt[pb:pb + 64, c, g * 512:(g + 1) * 512], scale=0.125,
                        cb=cb[:, hh:hh + 1], cbr=[cb], bm={0: (Bm, Bm[:, hh, 0, :]), 1: (Bm, Bm[:, hh, 1, :])},
                        mm=(lambda j, h=h, BT=BT: (e256[:, j * 128:(j + 1) * 128], BT[:, h, :], [e256, BT])),
                        v=(lambda j, h=h: va[:, j, h * 65:(h + 1) * 65]), reads=[kt, qt], vreads=[va]))
                OS = ostg[g % 2]

                def out_cb(r, acc, g=g, OS=OS):
                    o32 = o32s[oi[0] % 2]
                    rd = rds[oi[0] % 2]
                    oi[0] += 1
                    av = acc[:, 0:260].rearrange("p (h c) -> p h c", c=65)
                    S.op("dve", lambda e: e.reciprocal(rd[:, :], av[:, :, 64]), [acc.res], [rd.res])
                    for h in range(4):
                        k.ts(o32[:, h * 64:(h + 1) * 64], av[:, h, 0:64], rd[:, h:h + 1], ALU.mult, [acc, rd], [o32])
                    finish_tile_out(None, o32, r, g, OS, 0)
                attn_group(g, maps, causal_jlist(g), 65, pts, out_cb)
                k.store(oT[4:6, :, g * 512:(g + 1) * 512].rearrange("c p n -> p c n"), OS[:], OS)
            S.barrier()

    def phase_diff(l):
        lam_init = 0.8 - 0.6 * math.exp(-0.3 * l)
        with ExitStack() as st:
            qt = k.tile(st, [128, 3, L], BF16, "qt")
            kt = k.tile(st, [128, 3, L], BF16, "kt")
            va = k.tile(st, [128, NT, 260], BF16, "va")
            k.load(qt[:], hT[C_DQ:C_DQ + 3, :, :].rearrange("c p n -> p c n"), qt)
            k.load(kt[:], hT[C_DK:C_DK + 3, :, :].rearrange("c p n -> p c n"), kt)
            k.load(va[:], vaug_d.rearrange("(i p) c -> p i c", p=128), va)
            dl = k.tile(st, [128, 4, 32], F32, "dl")
            gd = k.tile(st, [128, 64], F32, "gd")
            k.load(dl[:], diff_l[l], dl)
            k.load(gd[:], diff_g[l], gd)
            lm = k.tile(st, [128, 8], F32, "lm")
            pr = k.tile(st, [128, 2, 32], F32, "pr")
            k.tt(pr[:, 0, :], dl[:, 0, :], dl[:, 1, :], ALU.mult, [dl], [pr])
            k.tt(pr[:, 1, :], dl[:, 2, :], dl[:, 3, :], ALU.mult, [dl], [pr])
            S.op("dve", lambda e: e.tensor_reduce(out=lm[:, 0:2], in_=pr[:], axis=AX.X, op=ALU.add), [pr.res], [lm.res])
            k.act(lm[:, 2:4], lm[:, 0:2], AF.Exp, [lm], [lm])
            k.tt(lm[:, 4:5], lm[:, 2:3], lm[:, 3:4], ALU.subtract, [lm], [lm])
            k.ts(lm[:, 5:6], lm[:, 4:5], lam_init, ALU.add, [lm], [lm], s2=-1.0, op1=ALU.mult)
            k.ts(gd[:], gd[:], 1.0 - lam_init, ALU.mult, [gd], [gd])
            pts = [k.tile(st, [128, 512], BF16, "pt") for _ in range(6)]
            o32s = [k.tile(st, [128, 4, 256], F32, "o32") for _ in range(2)]
            rds = [k.tile(st, [128, 16], F32, "rd") for _ in range(2)]
            t1s = [k.tile(st, [128, 64], F32, "t1") for _ in range(2)]
            jks = [k.tile(st, [128, 64], F32, "jk") for _ in range(2)]
            ostg = [k.tile(st, [128, 2, 512], BF16, "ostg") for _ in range(2)]
            oi = [0]
            sc = 32 ** -0.5
            for g in range(NG):
                OS = ostg[g % 2]
                O32 = o32s[g % 2]
                for hp in range(2):
                    maps = []
                    for hl in range(2):
                        h = hp * 2 + hl
                        hh = 12 + h
                        for m_ in range(2):
                            mi_ = 2 * h + m_
                            pb = (mi_ % 3) * 32
                            c = mi_ // 3
                            maps.append(dict(
                                kt=(lambda j, pb=pb, c=c: kt[pb:pb + 32, c, j * 128:(j + 1) * 128]),
                                qt=qt[pb:pb + 32, c, g * 512:(g + 1) * 512], scale=sc,
                                cb=cb[:, hh:hh + 1], cbr=[cb], bm={0: (Bm, Bm[:, hh, 0, :]), 1: (Bm, Bm[:, hh, 1, :])},
                                mm=None, v=(lambda j, h=h: va[:, j, h * 65:(h + 1) * 65]), reads=[kt, qt], vreads=[va]))

                    def out_cb(r, acc, hp=hp, g=g, O32=O32, OS=OS):
                        rd = rds[oi[0] % 2]
                        t1 = t1s[oi[0] % 2]
                        jk = jks[oi[0] % 2]
                        oi[0] += 1
                        av = acc[:, 0:260].rearrange("p (h c) -> p h c", c=65)
                        S.op("dve", lambda e: e.reciprocal(rd[:, 0:4], av[:, :, 64]), [acc.res], [rd.res])
                        for hl in range(2):
                            h = hp * 2 + hl
                            od = O32[:, r, h * 64:(h + 1) * 64]
                            k.tt(rd[:, 4 + hl:5 + hl], rd[:, 2 * hl + 1:2 * hl + 2], lm[:, 5:6], ALU.mult, [rd, lm], [rd])
                            k.ts(t1[:], av[:, 2 * hl, 0:64], rd[:, 2 * hl:2 * hl + 1], ALU.mult, [acc, rd], [t1])
                            k.stt(od, av[:, 2 * hl + 1, 0:64], rd[:, 4 + hl:5 + hl], t1[:], ALU.mult, ALU.add, [acc, rd, t1], [O32])
                            k.stt(jk[:], od, 1.0, od, ALU.mult, ALU.mult, [O32], [jk, rd], accum_out=rd[:, 6 + hl:7 + hl])
                            k.act(rd[:, 8 + hl:9 + hl], rd[:, 6 + hl:7 + hl], AF.Ln, [rd, eps6], [rd], bias=eps6[:, 0:1], scale=1.0 / 64.0)
                            k.act(rd[:, 10 + hl:11 + hl], rd[:, 8 + hl:9 + hl], AF.Exp, [rd], [rd], scale=-0.5)
                            k.stt(od, od, rd[:, 10 + hl:11 + hl], gd[:], ALU.mult, ALU.mult, [O32, rd, gd], [O32])
                        if hp == 1:
                            b = k.bank(4, 8)
                            for c in range(2):
                                k.tr(b, b[:, c * 128:(c + 1) * 128], O32[:, r, c * 128:(c + 1) * 128], ident[:], [O32, ident])
                            k.copy("act", OS[:, :, r * 128:(r + 1) * 128], b[:, 0:256].rearrange("p (c n) -> p c n", n=128), [b], [OS])
                    attn_group(g, maps, causal_jlist(g), 65, pts, out_cb)
                k.store(oT[6:8, :, g * 512:(g + 1) * 512].rearrange("c p n -> p c n"), OS[:], OS)
            S.barrier()

    def phase_nsa(l):
        with ExitStack() as st:
            qt = k.tile(st, [128, 2, L], BF16, "qt")
            raw = k.tile(st, [128, L], BF16, "raw")
            ks = k.tile(st, [128, L], BF16, "ks")
            kw = k.tile(st, [128, L], BF16, "kw")
            vb = k.tile(st, [128, NT, 130], BF16, "vb")
            k.load(qt[:], hT[C_BQ:C_BQ + 2, :, :].rearrange("c p n -> p c n"), qt)
            k.load(raw[:], hT[C_KCVC, :, :], raw)
            k.load(ks[:], hT[C_KS, :, :], ks)
            k.load(kw[:], hT[C_KW, :, :], kw)
            k.load(vb[:], vaug_b.rearrange("(i p) c -> p i c", p=128), vb)
            e64 = k.tile(st, [128, L], BF16, "e64")
            k.memset(e64[:], 0.0, [e64])
            k.load(e64[0:64, :], ee64, e64, q="pool")
            cng = k.tile(st, [128, 2, L], BF16, "cng")
            k.load(cng[:], cmpneg.rearrange("(c p) n -> p c n", p=128), cng, q="pool")
            w1t = k.tile(st, [128, 32, 64], BF16, "w1t")
            k.load(w1t[:], nsa_w1[l], w1t, q="pool")
            w2t = k.tile(st, [64, 192], BF16, "w2t")
            k.load(w2t[:], nsa_w2[l], w2t, q="pool")
            posT = k.tile(st, [128, 32], BF16, "posT")
            k.load(posT[:], nsa_posT[l], posT, q="pool")
            vca = k.tile(st, [128, 2, 128], BF16, "vca")
            k.memset(vca[:], 1.0, [vca])
            k.load(vca[:, :, 65:128], ovl.rearrange("(c p) j -> p c j", p=128), vca, q="pool")
            kcT = k.tile(st, [128, 256], BF16, "kcT")
            gates = k.tile(st, [128, NT, 12], F32, "gates")
            k.load(gates[:], smallf[:, 8:20].rearrange("(i p) c -> p i c", p=128), gates)
            k.act(gates[:], gates[:], AF.Exp, [gates], [gates], scale=-1.0)
            k.ts(gates[:], gates[:], 1.0, ALU.add, [gates], [gates])
            S.op("dve", lambda e: e.reciprocal(gates[:], gates[:]), [gates.res], [gates.res])
            nadd = k.tile(st, [128, NT, 63], F32, "nadd")
            k.load(nadd[:], nsa_add.rearrange("(i p) c -> p i c", p=128), nadd)
            ncmp = (L - 32) // 16 + 1
            gT = [k.tile(st, [64, 256], BF16, "gT") for _ in range(2)]
            u = k.tile(st, [64, 256], F32, "u")
            u2 = k.tile(st, [64, 256], F32, "u2")
            cc_ = k.tile(st, [64, 1], F32, "cc")
            for kv in range(2):
                pb = kv * 64
                b = k.bank()
                for j in range(32):
                    rhs = raw[pb:pb + 64, j:j + 16 * (ncmp - 1) + 1:16]
                    k.mm(b, b[0:64, 0:ncmp], w1t[pb:pb + 64, j, :], rhs, j == 0, [w1t, raw])
                for j in range(32):
                    k.mm(b, b[0:64, 256:257], w1t[pb:pb + 64, j, :], posT[pb:pb + 64, j:j + 1], False, [w1t, posT])
                k.copy("dve", cc_[:], b[0:64, 256:257], [b], [cc_])
                k.memset(u[:], 0.0, [u])
                k.act(u[:, 0:ncmp], b[0:64, 0:ncmp], AF.Identity, [b, cc_], [u], bias=cc_[:, 0:1])
                k.tt(u2[:], u[:], u[:], ALU.mult, [u], [u2])
                k.ts(u2[:], u2[:], 0.044715, ALU.mult, [u2], [u2], s2=1.0, op1=ALU.add)
                k.tt(u2[:], u2[:], u[:], ALU.mult, [u2, u], [u2])
                k.act(u2[:], u2[:], AF.Tanh, [u2], [u2], scale=0.7978845608028654)
                k.ts(u2[:], u2[:], 1.0, ALU.add, [u2], [u2], s2=0.5, op1=ALU.mult)
                k.tt(gT[kv][:], u2[:], u[:], ALU.mult, [u2, u], [gT[kv]])
            b = k.bank()
            k.mm(b, b[:, 0:256], w2t[:, 0:128], gT[0][:], True, [w2t, gT[0]])
            k.copy("dve", kcT[:], b[:, 0:256], [b], [kcT])
            for c in range(2):
                b = k.bank()
                k.mm(b, b[:, 0:64], gT[1][:, c * 128:(c + 1) * 128], w2t[:, 128:192], True, [w2t, gT[1]])
                k.copy("dve", vca[:, c, 0:64], b[:, 0:64], [b], [vca])
            import os
            nsadbg = int(os.environ.get("NSADBG", "0"))
            if nsadbg == 1:
                S.barrier()
                return
            pts = [k.tile(st, [128, 512], BF16, "pt") for _ in range(6)]
            oacc = [k.tile(st, [128, 4, 256], F32, "oacc") for _ in range(2)]
            rds = [k.tile(st, [128, 12], F32, "rd") for _ in range(2)]
            imps = [k.tile(st, [128, 64], F32, "imp") for _ in range(2)]
            imp2 = [k.tile(st, [128, 64], F32, "imp2") for _ in range(2)]
            m8s = [k.tile(st, [128, 16], F32, "m8") for _ in range(2)]
            bmn = [k.tile(st, [128, 64], F32, "bmn") for _ in range(2)]
            for t_ in bmn:
                k.memset(t_[:], 1.0, [t_])
            mks = [k.tile(st, [128, 512], BF16, "mk") for _ in range(3)]
            mki = [0]
            bmT = [k.tile(st, [128, 512], BF16, "bmT") for _ in range(2)]
            for t_ in bmT:
                k.memset(t_[:], 0.0, [t_])
            ostg = [k.tile(st, [128, 2, 512], BF16, "ostg") for _ in range(2)]
            oi = [0]
            for g in range(NG):
                OA = oacc[g % 2]
                BT = bmT[g % 2]
                OS = ostg[g % 2]
                maps = []
                for h in range(4):
                    pb = (h % 2) * 64
                    c = h // 2
                    maps.append(dict(
                        kt=(lambda j, pb=pb: kcT[pb:pb + 64, j * 128:(j + 1) * 128]),
                        qt=qt[pb:pb + 64, c, g * 512:(g + 1) * 512], scale=0.125, cb=None, bm={},
                        mm=(lambda j, g=g: (identb[:], cng[:, j, g * 512:(g + 1) * 512], [identb, cng])),
                        v=(lambda j: vca[:, j, :]), reads=[kcT, qt], vreads=[vca]))

                def cb_cmp(r, acc, g=g, OA=OA, BT=BT):
                    i = 4 * g + r
                    rd = rds[oi[0] % 2]
                    imp, i2, m8, bn = imps[oi[0] % 2], imp2[oi[0] % 2], m8s[oi[0] % 2], bmn[oi[0] % 2]
                    oi[0] += 1
                    av = acc[:, :].rearrange("p (h c) -> p h c", c=128)
                    k.ts(rd[:, 0:4], av[:, :, 64], 1e-30, ALU.max, [acc], [rd])
                    S.op("dve", lambda e: e.reciprocal(rd[:, 0:4], rd[:, 0:4]), [rd.res], [rd.res])
                    k.tt(rd[:, 4:8], rd[:, 0:4], gates[:, i, 0:12:3], ALU.mult, [rd, gates], [rd])
                    for h in range(4):
                        k.ts(OA[:, r, h * 64:(h + 1) * 64], av[:, h, 0:64], rd[:, 4 + h:5 + h], ALU.mult, [acc, rd], [OA])
                    k.ts(imp[:, 0:63], av[:, 0, 65:128], rd[:, 0:1], ALU.mult, [acc, rd], [imp])
                    for h in range(1, 4):
                        k.stt(imp[:, 0:63], av[:, h, 65:128], rd[:, h:h + 1], imp[:, 0:63], ALU.mult, ALU.add, [acc, rd, imp], [imp])
                    k.tt(imp[:, 0:63], imp[:, 0:63], nadd[:, i, :], ALU.add, [imp, nadd], [imp])
                    S.op("dve", lambda e: e.max(out=m8[:, 0:8], in_=imp[:, 0:63]), [imp.res], [m8.res])
                    S.op("dve", lambda e: e.match_replace(out=i2[:, 0:63], in_to_replace=m8[:, 0:8], in_values=imp[:, 0:63], imm_value=-3e6),
                         [imp.res, m8.res], [i2.res])
                    S.op("dve", lambda e: e.max(out=m8[:, 8:16], in_=i2[:, 0:63]), [i2.res], [m8.res])
                    k.ts(bn[:, 1:64], imp[:, 0:63], m8[:, 14:15], ALU.is_ge, [imp, m8], [bn])
                    b = k.bank(4, 8)
                    k.tr(b, b[0:64, 0:128], bn[:, :], ident[:], [bn, ident])
                    k.copy("act", BT[0:64, r * 128:(r + 1) * 128], b[0:64, 0:128], [b], [BT])
                jl = [(j, 0, 4) for j in range(2) if 16 * 128 * j + 31 <= 512 * g + 511]
                attn_group(g, maps, jl, 128, pts, cb_cmp)

                mkc = {}

                def get_mk(j, BT=BT, mkc=mkc):
                    if j not in mkc:
                        t = mks[mki[0] % 3]
                        mki[0] += 1
                        b = k.bank(4, 8)
                        k.mm(b, b[:, :], e64[:, j * 128:(j + 1) * 128], BT[:, :], True, [e64, BT])
                        k.copy("dve", t[:, :], b[:, :], [b], [t])
                        mkc[j] = t
                    return mkc[j]

                for br in range(2):
                    if nsadbg == 2 or (nsadbg in (3, 4) and br == 1) or (nsadbg == 5 and br == 0):
                        continue
                    maps = []
                    for h in range(4):
                        pb = (h % 2) * 64
                        c = h // 2
                        hh = 4 + h
                        kk = ks if br == 0 else kw
                        bmd = {0: (Bm, Bm[:, hh, 0, :]), 1: (Bm, Bm[:, hh, 1, :])}
                        if br == 1:
                            bmd[4] = (B4, B4[:])
                        maps.append(dict(
                            kt=(lambda j, pb=pb, kk=kk: kk[pb:pb + 64, j * 128:(j + 1) * 128]),
                            qt=qt[pb:pb + 64, c, g * 512:(g + 1) * 512], scale=0.125,
                            cb=cb[:, hh:hh + 1], cbr=[cb], bm=bmd,
                            mm=None, pre=(get_mk if br == 0 else None),
                            pmul=((lambda j: (get_mk(j)[:, :], [get_mk(j)])) if br == 0 else None),
                            v=(lambda j, br=br: vb[:, j, br * 65:(br + 1) * 65]), reads=[kk, qt], vreads=[vb]))

                    def cb_br(r, acc, g=g, br=br, OA=OA, OS=OS):
                        i = 4 * g + r
                        rd = rds[oi[0] % 2]
                        oi[0] += 1
                        av = acc[:, 0:260].rearrange("p (h c) -> p h c", c=65)
                        S.op("dve", lambda e: e.reciprocal(rd[:, 0:4], av[:, :, 64]), [acc.res], [rd.res])
                        k.tt(rd[:, 4:8], rd[:, 0:4], gates[:, i, 1 + br:12:3], ALU.mult, [rd, gates], [rd])
                        for h in range(4):
                            k.stt(OA[:, r, h * 64:(h + 1) * 64], av[:, h, 0:64], rd[:, 4 + h:5 + h], OA[:, r, h * 64:(h + 1) * 64],
                                  ALU.mult, ALU.add, [acc, rd, OA], [OA])
                        if br == 1:
                            b = k.bank(4, 8)
                            for c in range(2):
                                k.tr(b, b[:, c * 128:(c + 1) * 128], OA[:, r, c * 128:(c + 1) * 128], ident[:], [OA, ident])
                            k.copy("act", OS[:, :, r * 128:(r + 1) * 128], b[:, 0:256].rearrange("p (c n) -> p c n", n=128), [b], [OS])
                    attn_group(g, maps, causal_jlist(g) if br == 0 else win_jlist(g), 65, pts, cb_br)
                k.store(oT[2:4, :, g * 512:(g + 1) * 512].rearrange("c p n -> p c n"), OS[:], OS)
            S.barrier()

    def phase_cross(l):
        with ExitStack() as st:
            Wk = load_w(st, wbf[("xk", l)], 8, D, "Wk")
            Wv = load_w(st, wbf[("xv", l)], 8, D, "Wv", split="row")
            Wq = load_w(st, wbf[("xq", l)], 8, D, "Wq")
            mT = k.tile(st, [128, 8, 256], BF16, "mT")
            k.load(mT[:], memT.rearrange("c p m -> p c m"), mT)
            kxT = k.tile(st, [128, 8, 256], BF16, "kxT")
            vx = k.tile(st, [128, 2, D], BF16, "vx")
            for c in range(8):
                b = k.bank()
                for kc in range(8):
                    k.mm(b, b[:, 0:256], Wk[:, kc, c * 128:(c + 1) * 128], mT[:, kc, :], kc == 0, [Wk.sub[c // 4], mT])
                k.copy("act", kxT[:, c, :], b[:, 0:256], [b], [kxT])
            for mc in range(2):
                for nh in range(2):
                    b = k.bank()
                    for kc in range(8):
                        k.mm(b, b[:, :], mT[:, kc, mc * 128:(mc + 1) * 128], Wv[:, kc, nh * 512:(nh + 1) * 512], kc == 0, [Wv.sub[kc], mT])
                    k.copy("act", vx[:, mc, nh * 512:(nh + 1) * 512], b[:, :], [b], [vx])
            at = [k.tile(st, [128, 8, 512], BF16, "at") for _ in range(2)]
            qx = [k.tile(st, [128, 8, 512], BF16, "qx") for _ in range(2)]
            Es = [k.tile(st, [128, 2, 512], BF16, "E") for _ in range(2)]
            rdn = [k.tile(st, [128, 512], F32, "rdn") for _ in range(2)]
            ostg = [k.tile(st, [128, 8, 512], BF16, "ostg") for _ in range(2)]
            sc = 256 ** -0.5
            ei = 0
            def ld(g):
                k.load(at[g % 2][:], xT[:, :, g * 512:(g + 1) * 512].rearrange("c p n -> p c n"), at[g % 2])
            ld(0)
            for g in range(NG):
                a, Q, OS = at[g % 2], qx[g % 2], ostg[g % 2]
                if g + 1 < NG:
                    ld(g + 1)
                for c in range(8):
                    b = k.bank()
                    for kc in range(8):
                        k.mm(b, b[:, :], Wq[:, kc, c * 128:(c + 1) * 128], a[:, kc, :], kc == 0, [Wq.sub[c // 4], a])
                    k.copy("act" if c % 2 == 0 else "dve", Q[:, c, :], b[:, :], [b], [Q])
                for h in range(4):
                    E, rd = Es[ei % 2], rdn[ei % 2]
                    ei += 1
                    for mc in range(2):
                        b = k.bank()
                        for dc in range(2):
                            k.mm(b, b[:, :], kxT[:, 2 * h + dc, mc * 128:(mc + 1) * 128], Q[:, 2 * h + dc, :], dc == 0, [kxT, Q])
                        k.act(E[:, mc, :], b[:, :], AF.Exp, [b], [E], scale=sc)
                    b = k.bank()
                    for mc in range(2):
                        k.mm(b, b[:, :], onesb[:], E[:, mc, :], mc == 0, [onesb, E])
                    S.op("dve", lambda e: e.reciprocal(rd[:], b[:, :]), [b.res], [rd.res])
                    for dc in range(2):
                        b = k.bank()
                        for mc in range(2):
                            k.mm(b, b[:, :], vx[:, mc, (2 * h + dc) * 128:(2 * h + dc + 1) * 128], E[:, mc, :], mc == 0, [vx, E])
                        k.tt(OS[:, 2 * h + dc, :], b[:, :], rd[:], ALU.mult, [b, rd], [OS])
                k.store(oT[:, :, g * 512:(g + 1) * 512].rearrange("c p n -> p c n"), OS[:], OS)
            S.barrier()

    with ExitStack() as st:
        xr = [k.tile(st, [128, 2, D], F32, "xr") for _ in range(2)]
        xts = [k.tile(st, [128, 8, 256], BF16, "xts") for _ in range(2)]
        for g in range(L // 256):
            X, XT = xr[g % 2], xts[g % 2]
            k.load(X[:], x_in[g * 256:(g + 1) * 256, :].rearrange("(r p) d -> p r d", p=128), X)
            for r in range(2):
                for c0 in range(0, 8, 4):
                    b = k.bank()
                    for cc in range(4):
                        c = c0 + cc
                        k.tr(b, b[:, cc * 128:(cc + 1) * 128], X[:, r, c * 128:(c + 1) * 128], ident[:], [X, ident])
                    k.copy("act" if c0 == 0 else "dve", XT[:, c0:c0 + 4, r * 128:(r + 1) * 128],
                           b[:, :].rearrange("p (c n) -> p c n", n=128), [b], [XT])
            k.store(xT[:, :, g * 256:(g + 1) * 256].rearrange("c p n -> p c n"), XT[:], XT)
        S.barrier()

    if stop_after == "xt0":
        return k
    cur_x = x_in
    pp = 0
    done = False
    for l in range(depth):
        last = (l == depth - 1)
        phase_fm(xT, 8, L, wbf[("w_in_fm", l)], NFM, hT)
        if stop_after == "fm":
            break
        phase_tm(l)
        if stop_after == "proj":
            break
        import os
        only = os.environ.get("ONLY", "")
        if l + 1 < depth:
            convert_layer(l + 1)
        if only in ("", "dsa_sel", "dsa"):
            phase_dsa_select(l)
        if only in ("", "dsa"):
            phase_dsa_attn(l)
        if only in ("", "nsa"):
            phase_nsa(l)
        if only in ("", "moba"):
            phase_moba(l)
        if only in ("", "diff"):
            phase_diff(l)
        if stop_after == "mix":
            break
        nxt = xres[pp]
        phase_outln(oT, 8, wbf[("w_out", l)], lngb[l], 0, cur_x, nxt)
        cur_x = nxt
        pp ^= 1
        phase_cross(l)
        nxt = xres[pp]
        phase_outln(oT, 8, wbf[("xo", l)], lngb[l], 2, cur_x, nxt)
        cur_x = nxt
        pp ^= 1
        phase_fm(xT, 8, L, wbf[("w1", l)], 32, hmlp, relu2=True)
        nxt = out_d if last else xres[pp]
        phase_outln(hmlp, 32, wbf[("w2", l)], lngb[l], 4, cur_x, nxt)
        cur_x = nxt
        pp ^= 1
    if stop_after is not None:
        pass
    S.barrier()
    k.ninst = S.ninst
    return k


def host_consts(L, rel_bias):
    c = {}
    s = np.arange(128)[:, None]
    t = np.arange(128)[None, :]
    G = np.zeros((128, 16, 2, 128), np.float32)
    for d in range(2):
        bk = rel_bucket_np(t - s + 128 * d)
        G[:, :, d, :] = np.transpose(rel_bias[bk], (0, 2, 1))
    c["biasG"] = G
    M = np.zeros((128, 3, 128), np.float32)
    M[:, 0, :] = np.where(t - s >= 0, 0.0, NEGM)
    M[:, 1, :] = np.where(s > t, 0.0, NEGM)
    c["maskM"] = M
    c["cbias"] = np.ascontiguousarray(np.broadcast_to(rel_bias[31][None, :], (128, 16))).astype(np.float32)
    c["identf"] = np.eye(128, dtype=np.float32)
    pos = np.arange(L)
    c["ee64"] = (pos[None, :] // 64 == np.arange(64)[:, None]).astype(np.float32)
    c["ee256"] = (pos[None, :] // 256 == np.arange(16)[:, None]).astype(np.float32)
    n = np.arange(256)[:, None]
    cm = np.where((16 * n + 31 <= pos[None, :]) & (n < (L - 32) // 16 + 1), 0.0, NEGM).astype(np.float32)
    c["cmpneg"] = cm
    cs = np.arange(256) * 16
    ss = np.arange(1, 64) * 64
    ov = ((cs[:, None] <= ss[None, :] + 63) & (cs[:, None] + 31 >= ss[None, :])).astype(np.float32)
    ov[(L - 32) // 16 + 1:, :] = 0.0
    c["ovl"] = ov
    j = np.arange(1, 64)[None, :]
    cur = (pos // 64)[:, None]
    add = np.zeros((L, 63), np.float32)
    add = np.where((j == cur) | (j == cur - 1), 1e6 + 16.0 * j, add)
    add = np.where(j > cur, -1e6 - 16.0 * j, add)
    c["nsa_add"] = add.astype(np.float32)
    ma = np.zeros((128, 16, 4, 16), np.float32)
    for cu in range(16):
        ma[:, cu, :, cu:] = -1e30
    c["moba_add"] = ma
    tt_ = np.arange(128)[:, None]
    s_ = np.arange(128)[None, :]
    c["causneg"] = np.where(s_ <= tt_, 0.0, -1e30).astype(np.float32)
    return c


def host_weights(inp, depth):
    w = {}
    w["w_in_p"] = np.ascontiguousarray(inp["w_in"][:depth][:, :, FM_COLS + TM_COLS])
    for nme in ("w_out", "xq", "xk", "xv", "xo", "mlp_w1", "mlp_w2"):
        w[nme] = np.ascontiguousarray(inp[nme][:depth])
    gb = np.stack([inp["ln1_g"], inp["ln1_b"], inp["ln2_g"], inp["ln2_b"], inp["ln3_g"], inp["ln3_b"]], axis=1)[:depth]
    w["lngb"] = np.ascontiguousarray(np.broadcast_to(gb[:, None], (depth, 128, 6, D))).astype(np.float32)
    w1k = inp["nsa_w1_k"][:depth].reshape(depth, 32, 64, 64).transpose(0, 2, 1, 3)
    w1v = inp["nsa_w1_v"][:depth].reshape(depth, 32, 64, 64).transpose(0, 2, 1, 3)
    w["nsa_w1"] = np.ascontiguousarray(np.concatenate([w1k, w1v], axis=1))
    w["nsa_w2"] = np.ascontiguousarray(np.concatenate([inp["nsa_w2_k"][:depth], inp["nsa_w2_k"][:depth], inp["nsa_w2_v"][:depth]], axis=2))
    w["nsa_posT"] = np.ascontiguousarray(np.concatenate([inp["nsa_pos_k"][:depth].transpose(0, 2, 1),
                                                         inp["nsa_pos_v"][:depth].transpose(0, 2, 1)], axis=1))
    dl = np.stack([inp["diff_lq1"], inp["diff_lk1"], inp["diff_lq2"], inp["diff_lk2"]], axis=1)[:depth]
    w["diff_l"] = np.ascontiguousarray(np.broadcast_to(dl[:, None], (depth, 128, 4, 32))).astype(np.float32)
    w["diff_g"] = np.ascontiguousarray(np.broadcast_to(inp["diff_g"][:depth][:, None], (depth, 128, 64))).astype(np.float32)
    return w


_CACHE = {}


def kernel(**inputs):
    inp = {k_: np.asarray(v, dtype=np.float32) for k_, v in inputs.items()}
    L, depth = 4096, 4
    if "prog" not in _CACHE:
        _CACHE["prog"] = build(L, depth)
    kb = _CACHE["prog"]
    shared = {}
    shared.update(host_consts(L, inp["rel_bias"]))
    shared.update(host_weights(inp, depth))
    in_maps = []
    for b in range(8):
        m = dict(shared)
        m["x"] = np.ascontiguousarray(inp["x"][b])
        m["mem"] = np.ascontiguousarray(inp["mem"][b])
        in_maps.append(m)
    res = run_bass_kernel_spmd(kb.nc, in_maps, core_ids=list(range(8)))
    return np.stack([np.asarray(r["out"], dtype=np.float32) for r in res.results], axis=0)
```

```python
import math
from contextlib import ExitStack
import numpy as np
import concourse.bass as bass
import concourse.mybir as mybir
from concourse.bass_utils import run_bass_kernel_spmd

F32 = mybir.dt.float32
BF16 = mybir.dt.bfloat16
AF = mybir.ActivationFunctionType
ALU = mybir.AluOpType
AX = mybir.AxisListType

D = 1024
DEPTH_FULL = 4
ALPHA = (2 * DEPTH_FULL) ** 0.25
NEGM = -30000.0
REL_BUCKETS = 32

O_AQ, O_AK, O_AV, O_AQI, O_AKI, O_AW = 0, 256, 512, 768, 1280, 1344
O_BQ, O_BKC, O_BVC, O_BKS, O_BVS, O_BKW, O_BVW, O_BG = 1352, 1608, 1672, 1736, 1800, 1864, 1928, 1992
O_CQ, O_CK, O_CV, O_DQ, O_DK, O_DV = 2004, 2260, 2516, 2772, 3028, 3284


def _r(a, n):
    return list(range(a, a + n))


def _diffcols(o):
    cols = []
    for ch in range(3):
        for p in range(4):
            mi = min(ch * 3 + min(p, 2), 7)
            cols += _r(o + mi * 32, 32)
    return cols


FM_COLS = (_r(O_AQ, 256) + _r(O_AK, 256) + _r(O_AQI, 512) + _r(O_AKI, 64) * 2
           + _r(O_BQ, 256) + _r(O_BKC, 64) + _r(O_BVC, 64) + _r(O_BKS, 64) * 2 + _r(O_BKW, 64) * 2
           + _r(O_CQ, 256) + _r(O_CK, 256) + _diffcols(O_DQ) + _diffcols(O_DK))
NFM = len(FM_COLS) // 128
C_AQ, C_AK, C_QI, C_KI, C_BQ, C_KCVC, C_KS, C_KW, C_CQ, C_CK, C_DQ, C_DK = 0, 2, 4, 8, 9, 11, 12, 13, 14, 16, 18, 21
TM_COLS = (_r(O_AV, 256) + _r(O_CV, 256)
           + _r(O_DV, 256) + _r(O_BVS, 64) + _r(O_BVW, 64) + _r(O_AW, 8) + _r(O_BG, 12))
NTMA, NTMB = 512, 404
NWIN = len(FM_COLS) + len(TM_COLS)


def rel_bucket_np(n):
    n = np.maximum(n, 0)
    me = REL_BUCKETS // 2
    nf = np.maximum(n, me).astype(np.float32)
    large = me + (np.log(nf / np.float32(me)) / np.float32(math.log(128 / me)) * (REL_BUCKETS - me)).astype(np.int32)
    large = np.minimum(large, REL_BUCKETS - 1)
    return np.where(n < me, n, large)


class Res:
    __slots__ = ("w", "r", "ps")

    def __init__(self):
        self.w = None
        self.r = {}
        self.ps = False


class Tile:
    __slots__ = ("t", "res", "sub")

    def __init__(self, t):
        self.t = t
        self.res = Res()
        self.sub = None

    def __getitem__(self, idx):
        return self.t[idx]


class Sched:
    def __init__(self, nc, stack):
        self.nc = nc
        self.engs = {"pe": nc.tensor, "act": nc.scalar, "dve": nc.vector, "pool": nc.gpsimd, "sp": nc.sync}
        self.sem = {k: stack.enter_context(nc.semaphore("s_" + k)) for k in self.engs}
        self.cnt = {k: 0 for k in self.engs}
        self.seen = {k: {} for k in self.engs}
        self.R = 10
        self.dq = {q: dict(sems=[stack.enter_context(nc.semaphore("d_%s%d" % (q, i))) for i in range(self.R)], n=0)
                   for q in ("sp", "pool", "poolc")}
        self.engs["poolc"] = nc.gpsimd
        self.seen["poolc"] = self.seen["pool"]
        self.ninst = 0

    def _wait(self, e, tok):
        key, h, v = tok
        if self.seen[e].get(key, 0) >= v:
            return
        self.engs[e].wait_ge(h, v)
        self.seen[e][key] = v

    def _deps(self, e, reads, writes):
        for b in reads:
            if b.w is not None:
                if not (e == "pe" and b.w[0] == "pe"):
                    self._wait(e, b.w)
        for b in writes:
            if b.w is not None:
                if not (e == "pe" and b.w[0] == "pe"):
                    self._wait(e, b.w)
            for t in list(b.r.values()):
                if not (e == "pe" and t[0] == "pe"):
                    self._wait(e, t)

    def _commit(self, tok, reads, writes):
        for b in writes:
            b.w = tok
            b.r = {}
        for b in reads:
            b.r[tok[0]] = tok

    def op(self, e, fn, reads=(), writes=()):
        reads = [x.res if isinstance(x, Tile) else x for x in reads]
        writes = [x.res if isinstance(x, Tile) else x for x in writes]
        if e != "pe":
            writes = writes + [x for x in reads if x.ps and x not in writes]
        self._deps(e, reads, writes)
        ins = fn(self.engs[e])
        self.cnt[e] += 1
        ins.then_inc(self.sem[e], 1)
        self._commit((e, self.sem[e], self.cnt[e]), reads, writes)
        self.ninst += 1

    def dma(self, q, out, in_, reads=(), writes=()):
        reads = [x.res if isinstance(x, Tile) else x for x in reads]
        writes = [x.res if isinstance(x, Tile) else x for x in writes]
        d = self.dq[q]
        i = d["n"] % self.R
        k = d["n"] // self.R
        d["n"] += 1
        key = "d_%s%d" % (q, i)
        h = d["sems"][i]
        if k > 0:
            self._wait(q, (key, h, 16 * k))
        self._deps(q, reads, writes)
        self.engs[q].dma_start(out=out, in_=in_).then_inc(h, 16)
        self._commit((key, h, 16 * (k + 1)), reads, writes)
        self.ninst += 1

    def barrier(self):
        for q, d in self.dq.items():
            if q == "poolc":
                continue
            n = d["n"]
            for i in range(self.R):
                if n > i:
                    last = ((n - 1 - i) // self.R) + 1
                    self._wait("sp", ("d_%s%d" % (q, i), d["sems"][i], 16 * last))
        for e in ("pe", "act", "dve", "pool"):
            if self.cnt[e] > 0:
                self._wait("sp", (e, self.sem[e], self.cnt[e]))
        self.nc.sync.nop().then_inc(self.sem["sp"], 1)
        self.cnt["sp"] += 1
        tok = ("sp", self.sem["sp"], self.cnt["sp"])
        for e in ("pe", "act", "dve", "pool"):
            self._wait(e, tok)


class K:
    def __init__(self, L, depth, debug=False):
        self.L, self.depth, self.debug = L, depth, debug
        self.NT, self.NG = L // 128, L // 512
        self.nc = bass.Bass("TRN2", target_bir_lowering=False)
        self.stack = ExitStack()
        self.S = Sched(self.nc, self.stack)
        self.dbg_names = []
        self.banks = []
        for i in range(8):
            t = self.stack.enter_context(self.nc.psum_tensor("bank%d" % i, [128, 512], F32))
            self.banks.append(Tile(t))
            self.banks[-1].res.ps = True
        self.bank_i = 0
        self.ntile = 0

    def din(self, name, shape, dt=F32):
        return self.nc.dram_tensor(name, list(shape), dt, kind="ExternalInput").ap()

    def dscr(self, name, shape, dt, dbg=False):
        kind = "ExternalOutput" if (self.debug and dbg) else "Internal"
        if self.debug and dbg:
            self.dbg_names.append(name)
        return self.nc.dram_tensor(name, list(shape), dt, kind=kind).ap()

    def tile(self, st, shape, dt, name=None):
        self.ntile += 1
        t = st.enter_context(self.nc.sbuf_tensor("%s_%d" % (name or "t", self.ntile), list(shape), dt))
        return Tile(t)

    def bank(self, lo=0, hi=8):
        n = hi - lo
        b = self.banks[lo + (self.bank_i % n)]
        self.bank_i += 1
        return b

    def mm(self, bank, out, lhsT, rhs, start, reads):
        self.S.op("pe", lambda e: e.matmul(out, lhsT=lhsT, rhs=rhs, start=start, stop=True, skip_group_check=True),
                  reads=reads, writes=[bank])

    def tr(self, bank, out, in_, ident, reads):
        self.S.op("pe", lambda e: e.transpose(out, in_, ident), reads=reads, writes=[bank])

    def act(self, out, in_, func, reads, writes, bias=None, scale=None, accum_out=None):
        kw = {}
        if bias is not None:
            kw["bias"] = bias
        if scale is not None:
            kw["scale"] = scale
        if accum_out is not None:
            kw["accum_out"] = accum_out
        self.S.op("act", lambda e: e.activation(out=out, in_=in_, func=func, **kw), reads=reads, writes=writes)

    def ts(self, out, in0, s1, op0, reads, writes, s2=None, op1=None, accum_out=None, eng="dve"):
        kw = {}
        if op1 is not None:
            kw["op1"] = op1
        if accum_out is not None:
            kw["accum_out"] = accum_out
        self.S.op(eng, lambda e: e.tensor_scalar(out=out, in0=in0, scalar1=s1, scalar2=s2, op0=op0, **kw),
                  reads=reads, writes=writes)

    def tt(self, out, in0, in1, op, reads, writes, eng="dve"):
        self.S.op(eng, lambda e: e.tensor_tensor(out=out, in0=in0, in1=in1, op=op), reads=reads, writes=writes)

    def stt(self, out, in0, scalar, in1, op0, op1, reads, writes, accum_out=None):
        kw = {}
        if accum_out is not None:
            kw["accum_out"] = accum_out
        self.S.op("dve", lambda e: e.scalar_tensor_tensor(out=out, in0=in0, scalar=scalar, in1=in1, op0=op0, op1=op1, **kw),
                  reads=reads, writes=writes)

    def copy(self, eng, out, in_, reads, writes):
        if eng == "act":
            self.S.op("act", lambda e: e.activation(out=out, in_=in_, func=AF.Copy), reads=reads, writes=writes)
        else:
            self.S.op(eng, lambda e: e.tensor_copy(out=out, in_=in_), reads=reads, writes=writes)

    def memset(self, out, val, writes, eng="dve"):
        self.S.op(eng, lambda e: e.memset(out, val), reads=[], writes=writes)

    def load(self, out, in_, tile, q="sp"):
        self.S.dma(q, out, in_, reads=[], writes=[tile])

    def store(self, out, in_, tile, q="sp"):
        self.S.dma(q, out, in_, reads=[tile], writes=[])


def build(L=4096, depth=4, debug=False, stop_after=None):
    k = K(L, depth, debug)
    nc, S = k.nc, k.S
    NT, NG = k.NT, k.NG
    x_in = k.din("x", [L, D])
    mem_in = k.din("mem", [256, D])
    w_in = k.din("w_in_p", [depth, D, NWIN])
    w_out = k.din("w_out", [depth, D, D])
    xq = k.din("xq", [depth, D, D])
    xk = k.din("xk", [depth, D, D])
    xv = k.din("xv", [depth, D, D])
    xo = k.din("xo", [depth, D, D])
    w1 = k.din("mlp_w1", [depth, D, 4 * D])
    w2 = k.din("mlp_w2", [depth, 4 * D, D])
    lngb = k.din("lngb", [depth, 128, 6, D])
    biasG = k.din("biasG", [128, 16, 2, 128])
    maskM = k.din("maskM", [128, 3, 128])
    cbias = k.din("cbias", [128, 16])
    identf = k.din("identf", [128, 128])
    ee64 = k.din("ee64", [64, L])
    ee256 = k.din("ee256", [16, L])
    cmpneg = k.din("cmpneg", [256, L])
    ovl = k.din("ovl", [256, 63])
    nsa_add = k.din("nsa_add", [L, 63])
    nsa_w1 = k.din("nsa_w1", [depth, 128, 32, 64])
    nsa_w2 = k.din("nsa_w2", [depth, 64, 192])
    nsa_posT = k.din("nsa_posT", [depth, 128, 32])
    diff_l = k.din("diff_l", [depth, 128, 4, 32])
    diff_g = k.din("diff_g", [depth, 128, 64])
    moba_add = k.din("moba_add", [128, 16, 4, 16])
    causneg = k.din("causneg", [128, 128])
    out_d = nc.dram_tensor("out", [L, D], F32, kind="ExternalOutput").ap()

    xres = [k.dscr("xresA", [L, D], F32), k.dscr("xresB", [L, D], F32)]
    xT = k.dscr("xT", [8, 128, L], BF16, dbg=True)
    hT = k.dscr("hT", [NFM, 128, L], BF16, dbg=True)
    vaug_a = k.dscr("vaug_a", [L, 260], BF16, dbg=True)
    vaug_c = k.dscr("vaug_c", [L, 260], BF16)
    vaug_d = k.dscr("vaug_d", [L, 260], BF16)
    vaug_b = k.dscr("vaug_b", [L, 130], BF16)
    smallf = k.dscr("smallf", [L, 20], F32, dbg=True)
    negT_d = k.dscr("negT", [NG, NT, 128, 512], BF16)
    oT = k.dscr("oT", [8, 128, L], BF16, dbg=True)
    memT = k.dscr("memT", [8, 128, 256], BF16)
    hmlp = k.dscr("hmlp", [32, 128, L], BF16)

    wspecs = [("w_in_fm", lambda l: w_in[l, :, 0:NFM * 128], D, NFM * 128), ("w_in_tm", lambda l: w_in[l, :, NFM * 128:NWIN], D, NTMA + NTMB),
              ("w_out", lambda l: w_out[l], D, D), ("xk", lambda l: xk[l], D, D), ("xv", lambda l: xv[l], D, D),
              ("xq", lambda l: xq[l], D, D), ("xo", lambda l: xo[l], D, D), ("w1", lambda l: w1[l], D, 4 * D), ("w2", lambda l: w2[l], 4 * D, D)]
    wbf = {}

    def convert_layer(l):
        for nme, src, rows, cols in wspecs:
            dst = k.dscr("wb_%s_%d" % (nme, l), [rows, cols], BF16)
            rl = []
            for r0 in range(0, rows, 256):
                rr = Res()
                S.dma("poolc", dst[r0:r0 + 256, :], src(l)[r0:r0 + 256, :], reads=[], writes=[rr])
                rl.append(rr)
            wbf[(nme, l)] = (dst, rl)

    convert_layer(0)

    gst = k.stack
    ident = k.tile(gst, [128, 128], F32, "ident")
    identb = k.tile(gst, [128, 128], BF16, "identb")
    onesb = k.tile(gst, [128, 128], BF16, "onesb")
    eps5 = k.tile(gst, [128, 1], F32, "eps5")
    eps6 = k.tile(gst, [128, 1], F32, "eps6")
    cb = k.tile(gst, [128, 16], F32, "cb")
    Bm = k.tile(gst, [128, 16, 2, 128], BF16, "Bm")
    B4 = k.tile(gst, [128, 128], BF16, "B4")

    k.load(ident[:], identf, ident)
    k.load(cb[:], cbias, cb)
    k.copy("dve", identb[:], ident[:], [ident], [identb])
    k.memset(onesb[:], 1.0, [onesb])
    k.memset(eps5[:], 1e-5, [eps5])
    k.memset(eps6[:], 1e-6, [eps6])
    with ExitStack() as st:
        bg = k.tile(st, [128, 16, 2, 128], F32, "bg")
        mk = k.tile(st, [128, 3, 128], F32, "mk")
        k.load(bg[:], biasG, bg)
        k.load(mk[:], maskM, mk)
        for h in range(16):
            inv = 1.0 / (32 ** -0.5) if h >= 12 else 1.0 / (64 ** -0.5)
            for d in range(2):
                k.ts(bg[:, h, d, :], bg[:, h, d, :], cb[:, h:h + 1], ALU.subtract, [bg, cb], [bg], s2=inv, op1=ALU.mult)
            k.tt(Bm[:, h, 0, :], bg[:, h, 0, :], mk[:, 0, :], ALU.add, [bg, mk], [Bm])
            k.copy("dve", Bm[:, h, 1, :], bg[:, h, 1, :], [bg], [Bm])
        k.copy("dve", B4[:], mk[:, 1, :], [mk], [B4])
        mt = k.tile(st, [128, 2, D], F32, "mt")
        mts = k.tile(st, [128, 8, 256], BF16, "mts")
        k.load(mt[:], mem_in.rearrange("(c p) d -> p c d", p=128), mt)
        for c in range(8):
            b = k.bank()
            for mc in range(2):
                k.tr(b, b[:, mc * 128:(mc + 1) * 128], mt[:, mc, c * 128:(c + 1) * 128], ident[:], [mt, ident])
            k.copy("act", mts[:, c, :], b[:, 0:256], [b], [mts])
        k.store(memT.rearrange("c p m -> p c m"), mts[:], mts)
        S.barrier()
    if stop_after == "setup":
        return k

    def load_w(st, wd, KC, M, name, split="col"):
        W = k.tile(st, [128, KC, M], BF16, name)
        wd, wres = wd
        if split == "row":
            W.sub = [Res() for _ in range(KC)]
            for kc in range(KC):
                S.dma("sp", W[:, kc, :], wd[kc * 128:(kc + 1) * 128, :], reads=[wres[(kc * 128) // 256]], writes=[W.sub[kc]])
        else:
            nb = (M + 511) // 512
            W.sub = [Res() for _ in range(nb)]
            for cb in range(nb):
                c0, c1 = cb * 512, min(M, (cb + 1) * 512)
                S.dma("sp", W[:, :, c0:c1], wd[:, c0:c1].rearrange("(kc p) n -> p kc n", p=128), reads=wres, writes=[W.sub[cb]])
        return W

    def phase_fm(actT, KC, ncols, wd, nchunks, outd, relu2=False):
        GW = min(512, ncols)
        with ExitStack() as st:
            W = load_w(st, wd, KC, nchunks * 128, "Wfm")
            at = [k.tile(st, [128, KC, GW], BF16, "at") for _ in range(2)]
            CB = 4
            stg = [k.tile(st, [128, CB, GW], BF16, "stg") for _ in range(3)]
            rt = [k.tile(st, [128, GW], F32, "rt") for _ in range(2)] if relu2 else None
            si = 0
            ev = 0
            ngr = ncols // GW

            def ld(g):
                k.load(at[g % 2][:], actT[:, :, g * GW:(g + 1) * GW].rearrange("c p n -> p c n"), at[g % 2])
            ld(0)
            for g in range(ngr):
                a = at[g % 2]
                if g + 1 < ngr:
                    ld(g + 1)
                for c0 in range(0, nchunks, CB):
                    sg = stg[si % 3]
                    si += 1
                    nb = min(CB, nchunks - c0)
                    for cc in range(nb):
                        c = c0 + cc
                        b = k.bank()
                        for kc in range(KC):
                            k.mm(b, b[:, 0:GW], W[:, kc, c * 128:(c + 1) * 128], a[:, kc, :], kc == 0, [W.sub[c // 4], a])
                        if relu2:
                            r = rt[ev % 2]
                            k.act(r[:], b[:, 0:GW], AF.Relu, [b], [r])
                            k.tt(sg[:, cc, :], r[:], r[:], ALU.mult, [r], [sg], eng="pool")
                        else:
                            k.copy("act" if ev % 2 == 0 else "dve", sg[:, cc, :], b[:, 0:GW], [b], [sg])
                        ev += 1
                    k.store(outd[c0:c0 + nb, :, g * GW:(g + 1) * GW].rearrange("c p n -> p c n"), sg[:, 0:nb, :], sg)
            S.barrier()

    def phase_tm(l):
        with ExitStack() as st:
            W = load_w(st, wbf[("w_in_tm", l)], 8, NTMA + NTMB, "Wtm")
            at = [k.tile(st, [128, 8, 512], BF16, "at") for _ in range(2)]
            sa = [k.tile(st, [128, 4, 4, 65], BF16, "sa") for _ in range(2)]
            sc = [k.tile(st, [128, 4, 4, 65], BF16, "sc") for _ in range(2)]
            sd = [k.tile(st, [128, 4, 4, 65], BF16, "sd") for _ in range(2)]
            sb = [k.tile(st, [128, 4, 2, 65], BF16, "sb") for _ in range(2)]
            sf = [k.tile(st, [128, 4, 20], F32, "sf") for _ in range(2)]
            import os
            tmdbg = int(os.environ.get("TMDBG", "0"))
            for tl in sa + sc + sd + sb:
                if not (tmdbg & 64):
                    k.memset(tl[:], 1.0, [tl])
            def ld(g):
                k.load(at[g % 2][:], xT[:, :, g * 512:(g + 1) * 512].rearrange("c p n -> p c n"), at[g % 2])
            ld(0)
            for g in range(NG):
                a = at[g % 2]
                if g + 1 < NG:
                    ld(g + 1)
                A, C, Dd, B, Fs = sa[g % 2], sc[g % 2], sd[g % 2], sb[g % 2], sf[g % 2]
                for r in range(4):
                    b = k.bank()
                    for kc in range(8):
                        k.mm(b, b[:, 0:512], a[:, kc, r * 128:(r + 1) * 128], W[:, kc, 0:512], kc == 0, [a, W.sub[0]])
                    if not (tmdbg & 8):
                        k.copy("act", A[:, r, :, 0:64], b[:, 0:256].rearrange("p (h d) -> p h d", d=64), [b], [A])
                    if not (tmdbg & 128):
                        k.copy("act", C[:, r, :, 0:64], b[:, 256:512].rearrange("p (h d) -> p h d", d=64), [b], [C])
                    b = k.bank()
                    for kc in range(8):
                        k.mm(b, b[:, 0:NTMB], a[:, kc, r * 128:(r + 1) * 128], W[:, kc, 512:512 + NTMB], kc == 0, [a, W.sub[1]])
                    if not (tmdbg & 16):
                        k.copy("dve", Dd[:, r, :, 0:64], b[:, 0:256].rearrange("p (h d) -> p h d", d=64), [b], [Dd])
                    if not (tmdbg & 256):
                        k.copy("dve", B[:, r, :, 0:64], b[:, 256:384].rearrange("p (h d) -> p h d", d=64), [b], [B])
                    if not (tmdbg & 32):
                        k.copy("dve", Fs[:, r, :], b[:, 384:404], [b], [Fs])
                sl = slice(g * 512, (g + 1) * 512)
                import os
                tmdbg = int(os.environ.get("TMDBG", "0"))
                if tmdbg & 1:
                    continue
                k.store(vaug_a[sl, :].rearrange("(r p) c -> p r c", p=128), A[:].rearrange("p r h c -> p r (h c)"), A)
                k.store(vaug_c[sl, :].rearrange("(r p) c -> p r c", p=128), C[:].rearrange("p r h c -> p r (h c)"), C)
                k.store(vaug_d[sl, :].rearrange("(r p) c -> p r c", p=128), Dd[:].rearrange("p r h c -> p r (h c)"), Dd)
                if tmdbg & 2:
                    continue
                k.store(vaug_b[sl, :].rearrange("(r p) c -> p r c", p=128), B[:].rearrange("p r h c -> p r (h c)"), B)
                if tmdbg & 4:
                    continue
                k.store(smallf[sl, :].rearrange("(r p) c -> p r c", p=128), Fs[:], Fs)
            S.barrier()

    def phase_outln(actT, KC, wd, gbl, j0, xin, xout):
        GW = 256
        with ExitStack() as st:
            W = load_w(st, wd, KC, D, "Wo", split="row")
            gb = k.tile(st, [128, 2, D], F32, "gb")
            k.load(gb[:], gbl[:, j0:j0 + 2, :], gb)
            at = [k.tile(st, [128, KC, GW], BF16, "at") for _ in range(2)]
            xr = [k.tile(st, [128, 2, D], F32, "xr") for _ in range(2)]
            zt = [k.tile(st, [128, D], F32, "z") for _ in range(2)]
            xn = [k.tile(st, [128, D], F32, "xn") for _ in range(2)]
            xts = [k.tile(st, [128, 8, GW], BF16, "xts") for _ in range(2)]
            sm = [k.tile(st, [128, 24], F32, "sm") for _ in range(2)]
            ti = 0
            def ld(g):
                k.load(at[g % 2][:], actT[:, :, g * GW:(g + 1) * GW].rearrange("c p n -> p c n"), at[g % 2])
                k.load(xr[g % 2][:], xin[g * GW:(g + 1) * GW, :].rearrange("(r p) d -> p r d", p=128), xr[g % 2])
            ld(0)
            for g in range(L // GW):
                a = at[g % 2]
                X = xr[g % 2]
                XT = xts[g % 2]
                if g + 1 < L // GW:
                    ld(g + 1)
                for r in range(2):
                    z, xo_, s = zt[ti % 2], xn[ti % 2], sm[ti % 2]
                    ti += 1
                    for nh in range(2):
                        b = k.bank()
                        for kc in range(KC):
                            k.mm(b, b[:, :], a[:, kc, r * 128:(r + 1) * 128], W[:, kc, nh * 512:(nh + 1) * 512], kc == 0, [a, W.sub[kc]])
                        k.stt(z[:, nh * 512:(nh + 1) * 512], X[:, r, nh * 512:(nh + 1) * 512], ALPHA, b[:, :],
                              ALU.mult, ALU.add, [X, b], [z])
                    S.op("dve", lambda e: e.bn_stats(s[:, 0:6], z[:, 0:512]), [z.res], [s.res])
                    S.op("dve", lambda e: e.bn_stats(s[:, 6:12], z[:, 512:1024]), [z.res], [s.res])
                    S.op("dve", lambda e: e.bn_aggr(s[:, 12:14], s[:, 0:12]), [s.res], [s.res])
                    k.act(s[:, 14:15], s[:, 13:14], AF.Ln, [s, eps5], [s], bias=eps5[:, 0:1])
                    k.act(s[:, 15:16], s[:, 14:15], AF.Exp, [s], [s], scale=-0.5)
                    k.ts(s[:, 16:17], s[:, 12:13], s[:, 15:16], ALU.mult, [s], [s], s2=-1.0, op1=ALU.mult)
                    k.act(xo_[:], z[:], AF.Identity, [z, s], [xo_], bias=s[:, 16:17], scale=s[:, 15:16])
                    k.tt(xo_[:], xo_[:], gb[:, 0, :], ALU.mult, [xo_, gb], [xo_], eng="pool")
                    k.tt(X[:, r, :], xo_[:], gb[:, 1, :], ALU.add, [xo_, gb], [X])
                    for c0 in range(0, 8, 4):
                        b = k.bank()
                        for cc in range(4):
                            c = c0 + cc
                            k.tr(b, b[:, cc * 128:(cc + 1) * 128], X[:, r, c * 128:(c + 1) * 128], ident[:], [X, ident])
                        k.copy("act", XT[:, c0:c0 + 4, r * 128:(r + 1) * 128],
                               b[:, :].rearrange("p (c n) -> p c n", n=128), [b], [XT])
                k.store(xout[g * GW:(g + 1) * GW, :].rearrange("(r p) d -> p r d", p=128), X[:], X)
                k.store(xT[:, :, g * GW:(g + 1) * GW].rearrange("c p n -> p c n"), XT[:], XT)
            S.barrier()

    ACC0 = 0

    def attn_group(g, maps, jlist, NV, pts, out_cb):
        accs = [k.banks[ACC0 + r] for r in range(4)]
        started = [False] * 4
        steps = [(j, rlo, rhi, mi, m) for (j, rlo, rhi) in jlist for mi, m in enumerate(maps)]
        ptof = {}

        def part_a(idxs):
            bks = {}
            for idx in idxs:
                j, rlo, rhi, mi, m = steps[idx]
                if m.get("pre") is not None:
                    m["pre"](j)
            for idx in idxs:
                j, rlo, rhi, mi, m = steps[idx]
                c0, c1 = rlo * 128, rhi * 128
                b = k.bank(4, 8)
                bks[idx] = b
                k.mm(b, b[:, c0:c1], m["kt"](j), m["qt"][:, c0:c1], True, m["reads"])
            for idx in idxs:
                j, rlo, rhi, mi, m = steps[idx]
                c0, c1 = rlo * 128, rhi * 128
                b = bks[idx]
                for r in range(rlo, rhi):
                    d = 4 * g + r - j
                    if d in m["bm"]:
                        bmt, bmap = m["bm"][d]
                        k.mm(b, b[:, r * 128:(r + 1) * 128], identb[:], bmap, False, [identb, bmt])
                mmx = m["mm"](j) if m.get("mm") else None
                if mmx is not None:
                    lh, rh, rds = mmx
                    k.mm(b, b[:, c0:c1], lh, rh[:, c0:c1], False, rds)
            for idx in idxs:
                j, rlo, rhi, mi, m = steps[idx]
                c0, c1 = rlo * 128, rhi * 128
                b = bks[idx]
                pt = pts[k.pt_i % len(pts)]
                k.pt_i += 1
                if m.get("cb") is not None:
                    k.act(pt[:, c0:c1], b[:, c0:c1], AF.Exp, [b] + m["cbr"], [pt], bias=m["cb"], scale=m["scale"])
                else:
                    k.act(pt[:, c0:c1], b[:, c0:c1], AF.Exp, [b], [pt], scale=m["scale"])
                if m.get("pmul") is not None:
                    mk_ap, mk_rd = m["pmul"](j)
                    k.tt(pt[:, c0:c1], pt[:, c0:c1], mk_ap[:, c0:c1], ALU.mult, [pt] + mk_rd, [pt],
                         eng=("pool" if idx % 4 == 3 else "dve"))
                ptof[idx] = pt

        def part_b(idxs):
            for idx in idxs:
                j, rlo, rhi, mi, m = steps[idx]
                pt = ptof.pop(idx)
                for r in range(rlo, rhi):
                    k.mm(accs[r], accs[r][:, mi * NV:(mi + 1) * NV], pt[:, r * 128:(r + 1) * 128], m["v"](j),
                         not started[r], [pt] + m["vreads"])
                    started[r] = True

        sss = [list(range(i, min(i + 2, len(steps)))) for i in range(0, len(steps), 2)]
        LA = 2
        for si in range(len(sss) + LA):
            if si < len(sss):
                part_a(sss[si])
            if si - LA >= 0:
                part_b(sss[si - LA])
        for r in range(4):
            out_cb(r, accs[r])

    k.pt_i = 0

    def causal_jlist(g):
        return [(j, max(0, j - 4 * g), 4) for j in range(0, 4 * g + 4)]

    def win_jlist(g):
        res = []
        for j in range(max(0, 4 * g - 4), 4 * g + 4):
            rlo = max(0, j - 4 * g)
            rhi = min(4, j + 4 - 4 * g + 1)
            if rhi > rlo:
                res.append((j, rlo, rhi))
        return res

    def finish_tile_out(st_tiles, o32, r, g, ostg, row0):
        b = k.bank(4, 8)
        for c in range(2):
            k.tr(b, b[:, c * 128:(c + 1) * 128], o32[:, c * 128:(c + 1) * 128], ident[:], [o32, ident])
        k.copy("act", ostg[:, :, r * 128:(r + 1) * 128], b[:, 0:256].rearrange("p (c n) -> p c n", n=128), [b], [ostg])

    def phase_dsa_select(l):
        NIT = 20
        with ExitStack() as st:
            qi = k.tile(st, [128, 4, L], BF16, "qi")
            ki = k.tile(st, [128, L], BF16, "ki")
            k.load(qi[:], hT[C_QI:C_QI + 4, :, :].rearrange("c p n -> p c n"), qi)
            k.load(ki[:], hT[C_KI, :, :], ki)
            cneg = k.tile(st, [128, 128], F32, "cneg")
            k.load(cneg[:], causneg, cneg)
            wall = k.tile(st, [128, NT, 8], F32, "wall")
            k.load(wall[:], smallf[:, 0:8].rearrange("(i p) c -> p i c", p=128), wall)
            wabs = k.tile(st, [128, NT, 8], F32, "wabs")
            wsgn = k.tile(st, [128, NT, 8], F32, "wsgn")
            k.ts(wsgn[:], wall[:], 0.0, ALU.is_ge, [wall], [wsgn], s2=2.0, op1=ALU.mult)
            k.ts(wsgn[:], wsgn[:], -1.0, ALU.add, [wsgn], [wsgn])
            k.tt(wabs[:], wall[:], wsgn[:], ALU.mult, [wall, wsgn], [wabs])
            NTL = 4
            accs = [k.tile(st, [128, L], F32, "acc") for _ in range(NTL)]
            junk = {"dve": k.tile(st, [128, L], BF16, "junkd"), "act": k.tile(st, [128, L], BF16, "junka")}
            tmps = [k.tile(st, [128, 512], F32, "tmp") for _ in range(6)]
            sms = [k.tile(st, [128, 8], F32, "sm") for _ in range(NTL)]
            rks = [k.tile(st, [128, 2, NIT], F32, "rk") for _ in range(NTL)]
            ckrow = k.tile(st, [128, NIT], F32, "ckrow")
            for it in range(NIT):
                k.memset(ckrow[:, it:it + 1], 2.0 ** -(it + 1), [ckrow])
            negst = k.tile(st, [128, NT, 512], BF16, "negst")
            k.memset(negst[:], 1.0, [negst])
            tmi = 0
            for g in range(NG):
                tiles = [i for i in range(4 * g, 4 * g + 4) if i >= 2]
                if True:
                    pair = tiles
                    ceng = ["dve" if (pi % 2 == 0) else "act" for pi in range(len(pair))]
                    for pi, i in enumerate(pair):
                        acc = accs[pi]
                        Lk = 128 * (i + 1)
                        for kb in range(0, Lk, 512):
                            n = min(512, Lk - kb)
                            for h in range(8):
                                pb = (h % 2) * 64
                                b = k.bank()
                                k.mm(b, b[:, 0:n], qi[pb:pb + 64, h // 2, i * 128:(i + 1) * 128], ki[pb:pb + 64, kb:kb + n], True, [qi, ki])
                                if h == 0:
                                    k.ts(acc[:, kb:kb + n], b[:, 0:n], 0.0, ALU.max, [b, wall], [acc], s2=wall[:, i, 0:1], op1=ALU.mult)
                                else:
                                    t = tmps[tmi % 6]
                                    tmi += 1
                                    k.act(t[:, 0:n], b[:, 0:n], AF.Relu, [b, wabs], [t], scale=wabs[:, i, h:h + 1])
                                    if False:
                                        k.ts(t[:, 0:n], t[:, 0:n], wsgn[:, i, h:h + 1], ALU.mult, [t, wsgn], [t], eng="pool")
                                        k.tt(acc[:, kb:kb + n], acc[:, kb:kb + n], t[:, 0:n], ALU.add, [acc, t], [acc], eng="pool")
                                    else:
                                        k.stt(acc[:, kb:kb + n], t[:, 0:n], wsgn[:, i, h:h + 1], acc[:, kb:kb + n], ALU.mult, ALU.add, [t, wsgn, acc], [acc])
                        k.tt(acc[:, i * 128:Lk], acc[:, i * 128:Lk], cneg[:], ALU.add, [acc, cneg], [acc])
                        s, rk = sms[pi], rks[pi]
                        S.op("dve", lambda e: e.tensor_reduce(out=s[:, 0:1], in_=acc[:, 0:Lk], axis=AX.X, op=ALU.max), [acc.res], [s.res])
                        S.op("dve", lambda e: e.tensor_reduce(out=s[:, 1:2], in_=acc[:, 0:i * 128], axis=AX.X, op=ALU.min), [acc.res], [s.res])
                        k.tt(s[:, 2:3], s[:, 0:1], s[:, 1:2], ALU.subtract, [s], [s])
                        k.ts(rk[:, 0, :], ckrow[:], s[:, 2:3], ALU.mult, [ckrow, s], [rk])
                        k.ts(rk[:, 1, :], rk[:, 0, :], 2.0, ALU.mult, [rk], [rk])
                        k.tt(s[:, 3:4], s[:, 1:2], rk[:, 0, 0:1], ALU.add, [s, rk], [s])
                    for it in range(NIT):
                        for pi, i in enumerate(pair):
                            acc, s = accs[pi], sms[pi]
                            Lk = 128 * (i + 1)
                            if ceng[pi] == "dve":
                                jk = junk["dve"]
                                k.ts(jk[:, 0:Lk], acc[:, 0:Lk], s[:, 3:4], ALU.is_ge, [acc, s], [jk, s], s2=0.0, op1=ALU.add, accum_out=s[:, 4:5])
                            else:
                                jk = junk["act"]
                                k.act(jk[:, 0:Lk], acc[:, 0:Lk], AF.Sign, [acc, s], [jk, s], bias=s[:, 3:4], scale=-1.0, accum_out=s[:, 4:5])
                        for pi, i in enumerate(pair):
                            s, rk = sms[pi], rks[pi]
                            Lk = 128 * (i + 1)
                            last = (it == NIT - 1)
                            if ceng[pi] == "dve":
                                cmpv, opge, oplt = 255.5, ALU.is_ge, ALU.is_lt
                            else:
                                cmpv, opge, oplt = float(Lk - 511), ALU.is_le, ALU.is_gt
                            if not last:
                                k.stt(s[:, 5:6], s[:, 4:5], cmpv, rk[:, 1, it + 1:it + 2], opge, ALU.mult, [s, rk], [s])
                                k.stt(s[:, 3:4], s[:, 5:6], rk[:, 0, it + 1:it + 2], s[:, 3:4], ALU.subtract, ALU.add, [s, rk], [s])
                            else:
                                k.stt(s[:, 5:6], s[:, 4:5], cmpv, rk[:, 0, it:it + 1], oplt, ALU.mult, [s, rk], [s])
                                k.tt(s[:, 1:2], s[:, 3:4], s[:, 5:6], ALU.subtract, [s], [s])
                    for pi, i in enumerate(pair):
                        acc, s = accs[pi], sms[pi]
                        Lk = 128 * (i + 1)
                        r = i - 4 * g
                        k.ts(acc[:, 0:Lk], acc[:, 0:Lk], s[:, 1:2], ALU.is_ge, [acc, s], [acc])
                        for j0 in range(0, i + 1, 4):
                            nb = min(4, i + 1 - j0)
                            b = k.bank()
                            for jj in range(nb):
                                k.tr(b, b[:, jj * 128:(jj + 1) * 128], acc[:, (j0 + jj) * 128:(j0 + jj + 1) * 128], ident[:], [acc, ident])
                            k.copy("act", negst[:, j0:j0 + nb, r * 128:(r + 1) * 128],
                                   b[:, 0:nb * 128].rearrange("p (c n) -> p c n", n=128), [b], [negst])
                nj = 4 * g + 4
                k.store(negT_d[g, 0:nj, :, :].rearrange("j p n -> p j n"), negst[:, 0:nj, :], negst)
            S.barrier()

    def phase_dsa_attn(l):
        with ExitStack() as st:
            qt = k.tile(st, [128, 2, L], BF16, "qt")
            kt = k.tile(st, [128, 2, L], BF16, "kt")
            va = k.tile(st, [128, NT, 260], BF16, "va")
            k.load(qt[:], hT[C_AQ:C_AQ + 2, :, :].rearrange("c p n -> p c n"), qt)
            k.load(kt[:], hT[C_AK:C_AK + 2, :, :].rearrange("c p n -> p c n"), kt)
            k.load(va[:], vaug_a.rearrange("(i p) c -> p i c", p=128), va)
            ngs = [k.tile(st, [128, 4, 512], BF16, "ng") for _ in range(3)]
            pts = [k.tile(st, [128, 512], BF16, "pt") for _ in range(8)]
            o32s = [k.tile(st, [128, 256], F32, "o32") for _ in range(2)]
            rds = [k.tile(st, [128, 4], F32, "rd") for _ in range(2)]
            ostg = [k.tile(st, [128, 2, 512], BF16, "ostg") for _ in range(2)]
            ngi = [0]
            oi = [0]
            for g in range(NG):
                ngcache = {}

                def get_ng(j, g=g, ngcache=ngcache):
                    jb = j // 4
                    if jb not in ngcache:
                        t = ngs[ngi[0] % 3]
                        ngi[0] += 1
                        k.load(t[:], negT_d[g, jb * 4:jb * 4 + 4, :, :].rearrange("j p n -> p j n"), t)
                        ngcache[jb] = t
                    return ngcache[jb]

                maps = []
                for h in range(4):
                    pb = (h % 2) * 64
                    c = h // 2

                    def mmf(j):
                        t = get_ng(j)
                        return (t[:, j % 4, :], [t])
                    maps.append(dict(
                        kt=(lambda j, pb=pb, c=c: kt[pb:pb + 64, c, j * 128:(j + 1) * 128]),
                        qt=qt[pb:pb + 64, c, g * 512:(g + 1) * 512], scale=0.125,
                        cb=cb[:, h:h + 1], cbr=[cb], bm={0: (Bm, Bm[:, h, 0, :]), 1: (Bm, Bm[:, h, 1, :])},
                        mm=None, pmul=mmf, v=(lambda j, h=h: va[:, j, h * 65:(h + 1) * 65]), reads=[kt, qt], vreads=[va]))
                OS = ostg[g % 2]

                def out_cb(r, acc, g=g, OS=OS):
                    o32 = o32s[oi[0] % 2]
                    rd = rds[oi[0] % 2]
                    oi[0] += 1
                    av = acc[:, 0:260].rearrange("p (h c) -> p h c", c=65)
                    S.op("dve", lambda e: e.reciprocal(rd[:, :], av[:, :, 64]), [acc.res], [rd.res])
                    for h in range(4):
                        k.ts(o32[:, h * 64:(h + 1) * 64], av[:, h, 0:64], rd[:, h:h + 1], ALU.mult, [acc, rd], [o32])
                    finish_tile_out(None, o32, r, g, OS, 0)
                attn_group(g, maps, causal_jlist(g), 65, pts, out_cb)
                k.store(oT[0:2, :, g * 512:(g + 1) * 512].rearrange("c p n -> p c n"), OS[:], OS)
            S.barrier()

    def phase_moba(l):
        with ExitStack() as st:
            qt = k.tile(st, [128, 2, L], BF16, "qt")
            kt = k.tile(st, [128, 2, L], BF16, "kt")
            va = k.tile(st, [128, NT, 260], BF16, "va")
            k.load(qt[:], hT[C_CQ:C_CQ + 2, :, :].rearrange("c p n -> p c n"), qt)
            k.load(kt[:], hT[C_CK:C_CK + 2, :, :].rearrange("c p n -> p c n"), kt)
            k.load(va[:], vaug_c.rearrange("(i p) c -> p i c", p=128), va)
            e256 = k.tile(st, [16, L], BF16, "e256")
            k.load(e256[:], ee256, e256, q="pool")
            madd = k.tile(st, [128, 16, 4, 16], F32, "madd")
            k.load(madd[:], moba_add, madd)
            nb = L // 256
            kmf = k.tile(st, [128, 2, 16], F32, "kmf")
            km = k.tile(st, [128, 2, 16], BF16, "km")
            k.memset(kmf[:], 0.0, [kmf])
            for c in range(2):
                S.op("dve", lambda e: e.tensor_reduce(out=kmf[:, c, 0:nb], in_=kt[:, c, :].rearrange("p (n s) -> p n s", s=256),
                                                       axis=AX.X, op=ALU.add), [kt.res], [kmf.res])
            k.ts(km[:], kmf[:], 1.0 / 256.0, ALU.mult, [kmf], [km])
            pts = [k.tile(st, [128, 512], BF16, "pt") for _ in range(8)]
            o32s = [k.tile(st, [128, 256], F32, "o32") for _ in range(2)]
            rds = [k.tile(st, [128, 4], F32, "rd") for _ in range(2)]
            ostg = [k.tile(st, [128, 2, 512], BF16, "ostg") for _ in range(2)]
            gms = [k.tile(st, [128, 4, 16], F32, "gm") for _ in range(2)]
            m8s = [k.tile(st, [128, 4, 8], F32, "m8") for _ in range(2)]
            bmT = [k.tile(st, [16, 4, 512], BF16, "bmT") for _ in range(2)]
            oi = [0]
            gi = 0
            for g in range(NG):
                BT = bmT[g % 2]
                for r in range(4):
                    i = 4 * g + r
                    cur = i // 2
                    gm, m8 = gms[gi % 2], m8s[gi % 2]
                    gi += 1
                    bA, bB = k.bank(4, 8), k.bank(4, 8)
                    for h in range(4):
                        pb = (h % 2) * 64
                        c = h // 2
                        b = bA if pb == 0 else bB
                        k.mm(b, b[:, (h // 2) * 16:(h // 2) * 16 + 16], qt[pb:pb + 64, c, i * 128:(i + 1) * 128], km[pb:pb + 64, c, :], h < 2, [qt, km])
                    for h in range(4):
                        b = bA if h % 2 == 0 else bB
                        k.tt(gm[:, h, :], b[:, (h // 2) * 16:(h // 2) * 16 + 16], madd[:, cur, h, :], ALU.add, [b, madd], [gm])
                    for h in range(4):
                        S.op("dve", lambda e, h=h: e.max(out=m8[:, h, :], in_=gm[:, h, :]), [gm.res], [m8.res])
                    k.ts(m8[:, :, 2], m8[:, :, 2], -1e29, ALU.max, [m8], [m8])
                    for h in range(4):
                        k.ts(gm[:, h, :], gm[:, h, :], m8[:, h, 2:3], ALU.is_lt, [gm, m8], [gm], s2=NEGM, op1=ALU.mult)
                    k.memset(gm[:, :, cur:cur + 1], 0.0, [gm])
                    b = k.bank(4, 8)
                    for h in range(4):
                        k.tr(b, b[0:16, h * 128:(h + 1) * 128], gm[:, h, :], ident[:], [gm, ident])
                    k.copy("act", BT[:, :, r * 128:(r + 1) * 128], b[0:16, :].rearrange("p (h n) -> p h n", n=128), [b], [BT])
                maps = []
                for h in range(4):
                    pb = (h % 2) * 64
                    c = h // 2
                    hh = 8 + h
                    maps.append(dict(
                        kt=(lambda j, pb=pb, c=c: kt[pb:pb + 64, c, j * 128:(j + 1) * 128]),
                        qt=qt[pb:pb + 64, c, g * 512:(g + 1) * 512], scale=0.125,
                        cb=cb[:, hh:hh + 1], cbr=[cb], bm={0: (Bm, Bm[:, hh, 0, :]), 1: (Bm, Bm[:, hh, 1, :])},
                        mm=(lambda j, h=h, BT=BT: (e256[:, j * 128:(j + 1) * 128], BT[:, h, :], [e256, BT])),
                        v=(lambda j, h=h: va[:, j, h * 65:(h + 1) * 65]), reads=[kt, qt], vreads=[va]))
                OS = ostg[g % 2]

                def out_cb(r, acc, g=g, OS=OS):
                    o32 = o32s[oi[0] % 2]
                    rd = rds[oi[0] % 2]
                    oi[0] += 1
                    av = acc[:, 0:260].rearrange("p (h c) -> p h c", c=65)
                    S.op("dve", lambda e: e.reciprocal(rd[:, :], av[:, :, 64]), [acc.res], [rd.res])
                    for h in range(4):
                        k.ts(o32[:, h * 64:(h + 1) * 64], av[:, h, 0:64], rd[:, h:h + 1], ALU.mult, [acc, rd], [o32])
                    finish_tile_out(None, o32, r, g, OS, 0)
                attn_group(g, maps, causal_jlist(g), 65, pts, out_cb)
                k.store(oT[4:6, :, g * 512:(g + 1) * 512].rearrange("c p n -> p c n"), OS[:], OS)
            S.barrier()

    def phase_diff(l):
        lam_init = 0.8 - 0.6 * math.exp(-0.3 * l)
        with ExitStack() as st:
            qt = k.tile(st, [128, 3, L], BF16, "qt")
            kt = k.tile(st, [128, 3, L], BF16, "kt")
            va = k.tile(st, [128, NT, 260], BF16, "va")
            k.load(qt[:], hT[C_DQ:C_DQ + 3, :, :].rearrange("c p n -> p c n"), qt)
            k.load(kt[:], hT[C_DK:C_DK + 3, :, :].rearrange("c p n -> p c n"), kt)
            k.load(va[:], vaug_d.rearrange("(i p) c -> p i c", p=128), va)
            dl = k.tile(st, [128, 4, 32], F32, "dl")
            gd = k.tile(st, [128, 64], F32, "gd")
            k.load(dl[:], diff_l[l], dl)
            k.load(gd[:], diff_g[l], gd)
            lm = k.tile(st, [128, 8], F32, "lm")
            pr = k.tile(st, [128, 2, 32], F32, "pr")
            k.tt(pr[:, 0, :], dl[:, 0, :], dl[:, 1, :], ALU.mult, [dl], [pr])
            k.tt(pr[:, 1, :], dl[:, 2, :], dl[:, 3, :], ALU.mult, [dl], [pr])
            S.op("dve", lambda e: e.tensor_reduce(out=lm[:, 0:2], in_=pr[:], axis=AX.X, op=ALU.add), [pr.res], [lm.res])
            k.act(lm[:, 2:4], lm[:, 0:2], AF.Exp, [lm], [lm])
            k.tt(lm[:, 4:5], lm[:, 2:3], lm[:, 3:4], ALU.subtract, [lm], [lm])
            k.ts(lm[:, 5:6], lm[:, 4:5], lam_init, ALU.add, [lm], [lm], s2=-1.0, op1=ALU.mult)
            k.ts(gd[:], gd[:], 1.0 - lam_init, ALU.mult, [gd], [gd])
            pts = [k.tile(st, [128, 512], BF16, "pt") for _ in range(8)]
            o32s = [k.tile(st, [128, 4, 256], F32, "o32") for _ in range(2)]
            rds = [k.tile(st, [128, 16], F32, "rd") for _ in range(2)]
            t1s = [k.tile(st, [128, 64], F32, "t1") for _ in range(2)]
            jks = [k.tile(st, [128, 64], F32, "jk") for _ in range(2)]
            ostg = [k.tile(st, [128, 2, 512], BF16, "ostg") for _ in range(2)]
            oi = [0]
            sc = 32 ** -0.5
            for g in range(NG):
                OS = ostg[g % 2]
                O32 = o32s[g % 2]
                for hp in range(2):
                    maps = []
                    for hl in range(2):
                        h = hp * 2 + hl
                        hh = 12 + h
                        for m_ in range(2):
                            mi_ = 2 * h + m_
                            pb = (mi_ % 3) * 32
                            c = mi_ // 3
                            maps.append(dict(
                                kt=(lambda j, pb=pb, c=c: kt[pb:pb + 32, c, j * 128:(j + 1) * 128]),
                                qt=qt[pb:pb + 32, c, g * 512:(g + 1) * 512], scale=sc,
                                cb=cb[:, hh:hh + 1], cbr=[cb], bm={0: (Bm, Bm[:, hh, 0, :]), 1: (Bm, Bm[:, hh, 1, :])},
                                mm=None, v=(lambda j, h=h: va[:, j, h * 65:(h + 1) * 65]), reads=[kt, qt], vreads=[va]))

                    def out_cb(r, acc, hp=hp, g=g, O32=O32, OS=OS):
                        rd = rds[oi[0] % 2]
                        t1 = t1s[oi[0] % 2]
                        jk = jks[oi[0] % 2]
                        oi[0] += 1
                        av = acc[:, 0:260].rearrange("p (h c) -> p h c", c=65)
                        S.op("dve", lambda e: e.reciprocal(rd[:, 0:4], av[:, :, 64]), [acc.res], [rd.res])
                        for hl in range(2):
                            h = hp * 2 + hl
                            od = O32[:, r, h * 64:(h + 1) * 64]
                            k.tt(rd[:, 4 + hl:5 + hl], rd[:, 2 * hl + 1:2 * hl + 2], lm[:, 5:6], ALU.mult, [rd, lm], [rd])
                            k.ts(t1[:], av[:, 2 * hl, 0:64], rd[:, 2 * hl:2 * hl + 1], ALU.mult, [acc, rd], [t1])
                            k.stt(od, av[:, 2 * hl + 1, 0:64], rd[:, 4 + hl:5 + hl], t1[:], ALU.mult, ALU.add, [acc, rd, t1], [O32])
                            k.stt(jk[:], od, 1.0, od, ALU.mult, ALU.mult, [O32], [jk, rd], accum_out=rd[:, 6 + hl:7 + hl])
                            k.act(rd[:, 8 + hl:9 + hl], rd[:, 6 + hl:7 + hl], AF.Ln, [rd, eps6], [rd], bias=eps6[:, 0:1], scale=1.0 / 64.0)
                            k.act(rd[:, 10 + hl:11 + hl], rd[:, 8 + hl:9 + hl], AF.Exp, [rd], [rd], scale=-0.5)
                            k.stt(od, od, rd[:, 10 + hl:11 + hl], gd[:], ALU.mult, ALU.mult, [O32, rd, gd], [O32])
                        if hp == 1:
                            b = k.bank(4, 8)
                            for c in range(2):
                                k.tr(b, b[:, c * 128:(c + 1) * 128], O32[:, r, c * 128:(c + 1) * 128], ident[:], [O32, ident])
                            k.copy("act", OS[:, :, r * 128:(r + 1) * 128], b[:, 0:256].rearrange("p (c n) -> p c n", n=128), [b], [OS])
                    attn_group(g, maps, causal_jlist(g), 65, pts, out_cb)
                k.store(oT[6:8, :, g * 512:(g + 1) * 512].rearrange("c p n -> p c n"), OS[:], OS)
            S.barrier()

    def phase_nsa(l):
        with ExitStack() as st:
            qt = k.tile(st, [128, 2, L], BF16, "qt")
            raw = k.tile(st, [128, L], BF16, "raw")
            ks = k.tile(st, [128, L], BF16, "ks")
            kw = k.tile(st, [128, L], BF16, "kw")
            vb = k.tile(st, [128, NT, 130], BF16, "vb")
            k.load(qt[:], hT[C_BQ:C_BQ + 2, :, :].rearrange("c p n -> p c n"), qt)
            k.load(raw[:], hT[C_KCVC, :, :], raw)
            k.load(ks[:], hT[C_KS, :, :], ks)
            k.load(kw[:], hT[C_KW, :, :], kw)
            k.load(vb[:], vaug_b.rearrange("(i p) c -> p i c", p=128), vb)
            e64 = k.tile(st, [128, L], BF16, "e64")
            k.memset(e64[:], 0.0, [e64])
            k.load(e64[0:64, :], ee64, e64, q="pool")
            cng = k.tile(st, [128, 2, L], BF16, "cng")
            k.load(cng[:], cmpneg.rearrange("(c p) n -> p c n", p=128), cng, q="pool")
            w1t = k.tile(st, [128, 32, 64], BF16, "w1t")
            k.load(w1t[:], nsa_w1[l], w1t, q="pool")
            w2t = k.tile(st, [64, 192], BF16, "w2t")
            k.load(w2t[:], nsa_w2[l], w2t, q="pool")
            posT = k.tile(st, [128, 32], BF16, "posT")
            k.load(posT[:], nsa_posT[l], posT, q="pool")
            vca = k.tile(st, [128, 2, 128], BF16, "vca")
            k.memset(vca[:], 1.0, [vca])
            k.load(vca[:, :, 65:128], ovl.rearrange("(c p) j -> p c j", p=128), vca, q="pool")
            kcT = k.tile(st, [128, 256], BF16, "kcT")
            gates = k.tile(st, [128, NT, 12], F32, "gates")
            k.load(gates[:], smallf[:, 8:20].rearrange("(i p) c -> p i c", p=128), gates)
            k.act(gates[:], gates[:], AF.Exp, [gates], [gates], scale=-1.0)
            k.ts(gates[:], gates[:], 1.0, ALU.add, [gates], [gates])
            S.op("dve", lambda e: e.reciprocal(gates[:], gates[:]), [gates.res], [gates.res])
            nadd = k.tile(st, [128, NT, 63], F32, "nadd")
            k.load(nadd[:], nsa_add.rearrange("(i p) c -> p i c", p=128), nadd)
            ncmp = (L - 32) // 16 + 1
            gT = [k.tile(st, [64, 256], BF16, "gT") for _ in range(2)]
            u = k.tile(st, [64, 256], F32, "u")
            u2 = k.tile(st, [64, 256], F32, "u2")
            cc_ = k.tile(st, [64, 1], F32, "cc")
            for kv in range(2):
                pb = kv * 64
                b = k.bank()
                for j in range(32):
                    rhs = raw[pb:pb + 64, j:j + 16 * (ncmp - 1) + 1:16]
                    k.mm(b, b[0:64, 0:ncmp], w1t[pb:pb + 64, j, :], rhs, j == 0, [w1t, raw])
                for j in range(32):
                    k.mm(b, b[0:64, 256:257], w1t[pb:pb + 64, j, :], posT[pb:pb + 64, j:j + 1], False, [w1t, posT])
                k.copy("dve", cc_[:], b[0:64, 256:257], [b], [cc_])
                k.memset(u[:], 0.0, [u])
                k.act(u[:, 0:ncmp], b[0:64, 0:ncmp], AF.Identity, [b, cc_], [u], bias=cc_[:, 0:1])
                k.tt(u2[:], u[:], u[:], ALU.mult, [u], [u2])
                k.ts(u2[:], u2[:], 0.044715, ALU.mult, [u2], [u2], s2=1.0, op1=ALU.add)
                k.tt(u2[:], u2[:], u[:], ALU.mult, [u2, u], [u2])
                k.act(u2[:], u2[:], AF.Tanh, [u2], [u2], scale=0.7978845608028654)
                k.ts(u2[:], u2[:], 1.0, ALU.add, [u2], [u2], s2=0.5, op1=ALU.mult)
                k.tt(gT[kv][:], u2[:], u[:], ALU.mult, [u2, u], [gT[kv]])
            b = k.bank()
            k.mm(b, b[:, 0:256], w2t[:, 0:128], gT[0][:], True, [w2t, gT[0]])
            k.copy("dve", kcT[:], b[:, 0:256], [b], [kcT])
            for c in range(2):
                b = k.bank()
                k.mm(b, b[:, 0:64], gT[1][:, c * 128:(c + 1) * 128], w2t[:, 128:192], True, [w2t, gT[1]])
                k.copy("dve", vca[:, c, 0:64], b[:, 0:64], [b], [vca])
            import os
            nsadbg = int(os.environ.get("NSADBG", "0"))
            if nsadbg == 1:
                S.barrier()
                return
            pts = [k.tile(st, [128, 512], BF16, "pt") for _ in range(8)]
            oacc = [k.tile(st, [128, 4, 256], F32, "oacc") for _ in range(2)]
            rds = [k.tile(st, [128, 12], F32, "rd") for _ in range(2)]
            imps = [k.tile(st, [128, 64], F32, "imp") for _ in range(2)]
            imp2 = [k.tile(st, [128, 64], F32, "imp2") for _ in range(2)]
            m8s = [k.tile(st, [128, 16], F32, "m8") for _ in range(2)]
            bmn = [k.tile(st, [128, 64], F32, "bmn") for _ in range(2)]
            for t_ in bmn:
                k.memset(t_[:], 1.0, [t_])
            mks = [k.tile(st, [128, 512], BF16, "mk") for _ in range(3)]
            mki = [0]
            bmT = [k.tile(st, [128, 512], BF16, "bmT") for _ in range(2)]
            for t_ in bmT:
                k.memset(t_[:], 0.0, [t_])
            ostg = [k.tile(st, [128, 2, 512], BF16, "ostg") for _ in range(2)]
            oi = [0]
            for g in range(NG):
                OA = oacc[g % 2]
                BT = bmT[g % 2]
                OS = ostg[g % 2]
                maps = []
                for h in range(4):
                    pb = (h % 2) * 64
                    c = h // 2
                    maps.append(dict(
                        kt=(lambda j, pb=pb: kcT[pb:pb + 64, j * 128:(j + 1) * 128]),
                        qt=qt[pb:pb + 64, c, g * 512:(g + 1) * 512], scale=0.125, cb=None, bm={},
                        mm=(lambda j, g=g: (identb[:], cng[:, j, g * 512:(g + 1) * 512], [identb, cng])),
                        v=(lambda j: vca[:, j, :]), reads=[kcT, qt], vreads=[vca]))

                def cb_cmp(r, acc, g=g, OA=OA, BT=BT):
                    i = 4 * g + r
                    rd = rds[oi[0] % 2]
                    imp, i2, m8, bn = imps[oi[0] % 2], imp2[oi[0] % 2], m8s[oi[0] % 2], bmn[oi[0] % 2]
                    oi[0] += 1
                    av = acc[:, :].rearrange("p (h c) -> p h c", c=128)
                    k.ts(rd[:, 0:4], av[:, :, 64], 1e-30, ALU.max, [acc], [rd])
                    S.op("dve", lambda e: e.reciprocal(rd[:, 0:4], rd[:, 0:4]), [rd.res], [rd.res])
                    k.tt(rd[:, 4:8], rd[:, 0:4], gates[:, i, 0:12:3], ALU.mult, [rd, gates], [rd])
                    for h in range(4):
                        k.ts(OA[:, r, h * 64:(h + 1) * 64], av[:, h, 0:64], rd[:, 4 + h:5 + h], ALU.mult, [acc, rd], [OA])
                    k.ts(imp[:, 0:63], av[:, 0, 65:128], rd[:, 0:1], ALU.mult, [acc, rd], [imp])
                    for h in range(1, 4):
                        k.stt(imp[:, 0:63], av[:, h, 65:128], rd[:, h:h + 1], imp[:, 0:63], ALU.mult, ALU.add, [acc, rd, imp], [imp])
                    k.tt(imp[:, 0:63], imp[:, 0:63], nadd[:, i, :], ALU.add, [imp, nadd], [imp])
                    S.op("dve", lambda e: e.max(out=m8[:, 0:8], in_=imp[:, 0:63]), [imp.res], [m8.res])
                    S.op("dve", lambda e: e.match_replace(out=i2[:, 0:63], in_to_replace=m8[:, 0:8], in_values=imp[:, 0:63], imm_value=-3e6),
                         [imp.res, m8.res], [i2.res])
                    S.op("dve", lambda e: e.max(out=m8[:, 8:16], in_=i2[:, 0:63]), [i2.res], [m8.res])
                    k.ts(bn[:, 1:64], imp[:, 0:63], m8[:, 14:15], ALU.is_ge, [imp, m8], [bn])
                    b = k.bank(4, 8)
                    k.tr(b, b[0:64, 0:128], bn[:, :], ident[:], [bn, ident])
                    k.copy("act", BT[0:64, r * 128:(r + 1) * 128], b[0:64, 0:128], [b], [BT])
                jl = [(j, 0, 4) for j in range(2) if 16 * 128 * j + 31 <= 512 * g + 511]
                attn_group(g, maps, jl, 128, pts, cb_cmp)

                mkc = {}

                def get_mk(j, BT=BT, mkc=mkc):
                    if j not in mkc:
                        t = mks[mki[0] % 3]
                        mki[0] += 1
                        b = k.bank(4, 8)
                        k.mm(b, b[:, :], e64[:, j * 128:(j + 1) * 128], BT[:, :], True, [e64, BT])
                        k.copy("dve", t[:, :], b[:, :], [b], [t])
                        mkc[j] = t
                    return mkc[j]

                for br in range(2):
                    if nsadbg == 2 or (nsadbg in (3, 4) and br == 1) or (nsadbg == 5 and br == 0):
                        continue
                    maps = []
                    for h in range(4):
                        pb = (h % 2) * 64
                        c = h // 2
                        hh = 4 + h
                        kk = ks if br == 0 else kw
                        bmd = {0: (Bm, Bm[:, hh, 0, :]), 1: (Bm, Bm[:, hh, 1, :])}
                        if br == 1:
                            bmd[4] = (B4, B4[:])
                        maps.append(dict(
                            kt=(lambda j, pb=pb, kk=kk: kk[pb:pb + 64, j * 128:(j + 1) * 128]),
                            qt=qt[pb:pb + 64, c, g * 512:(g + 1) * 512], scale=0.125,
                            cb=cb[:, hh:hh + 1], cbr=[cb], bm=bmd,
                            mm=None, pre=(get_mk if br == 0 else None),
                            pmul=((lambda j: (get_mk(j)[:, :], [get_mk(j)])) if br == 0 else None),
                            v=(lambda j, br=br: vb[:, j, br * 65:(br + 1) * 65]), reads=[kk, qt], vreads=[vb]))

                    def cb_br(r, acc, g=g, br=br, OA=OA, OS=OS):
                        i = 4 * g + r
                        rd = rds[oi[0] % 2]
                        oi[0] += 1
                        av = acc[:, 0:260].rearrange("p (h c) -> p h c", c=65)
                        S.op("dve", lambda e: e.reciprocal(rd[:, 0:4], av[:, :, 64]), [acc.res], [rd.res])
                        k.tt(rd[:, 4:8], rd[:, 0:4], gates[:, i, 1 + br:12:3], ALU.mult, [rd, gates], [rd])
                        for h in range(4):
                            k.stt(OA[:, r, h * 64:(h + 1) * 64], av[:, h, 0:64], rd[:, 4 + h:5 + h], OA[:, r, h * 64:(h + 1) * 64],
                                  ALU.mult, ALU.add, [acc, rd, OA], [OA])
                        if br == 1:
                            b = k.bank(4, 8)
                            for c in range(2):
                                k.tr(b, b[:, c * 128:(c + 1) * 128], OA[:, r, c * 128:(c + 1) * 128], ident[:], [OA, ident])
                            k.copy("act", OS[:, :, r * 128:(r + 1) * 128], b[:, 0:256].rearrange("p (c n) -> p c n", n=128), [b], [OS])
                    attn_group(g, maps, causal_jlist(g) if br == 0 else win_jlist(g), 65, pts, cb_br)
                k.store(oT[2:4, :, g * 512:(g + 1) * 512].rearrange("c p n -> p c n"), OS[:], OS)
            S.barrier()

    def phase_cross(l):
        with ExitStack() as st:
            Wk = load_w(st, wbf[("xk", l)], 8, D, "Wk")
            Wv = load_w(st, wbf[("xv", l)], 8, D, "Wv", split="row")
            Wq = load_w(st, wbf[("xq", l)], 8, D, "Wq")
            mT = k.tile(st, [128, 8, 256], BF16, "mT")
            k.load(mT[:], memT.rearrange("c p m -> p c m"), mT)
            kxT = k.tile(st, [128, 8, 256], BF16, "kxT")
            vx = k.tile(st, [128, 2, D], BF16, "vx")
            for c in range(8):
                b = k.bank()
                for kc in range(8):
                    k.mm(b, b[:, 0:256], Wk[:, kc, c * 128:(c + 1) * 128], mT[:, kc, :], kc == 0, [Wk.sub[c // 4], mT])
                k.copy("act", kxT[:, c, :], b[:, 0:256], [b], [kxT])
            for mc in range(2):
                for nh in range(2):
                    b = k.bank()
                    for kc in range(8):
                        k.mm(b, b[:, :], mT[:, kc, mc * 128:(mc + 1) * 128], Wv[:, kc, nh * 512:(nh + 1) * 512], kc == 0, [Wv.sub[kc], mT])
                    k.copy("act", vx[:, mc, nh * 512:(nh + 1) * 512], b[:, :], [b], [vx])
            at = [k.tile(st, [128, 8, 512], BF16, "at") for _ in range(2)]
            qx = [k.tile(st, [128, 8, 512], BF16, "qx") for _ in range(2)]
            Es = [k.tile(st, [128, 2, 512], BF16, "E") for _ in range(2)]
            rdn = [k.tile(st, [128, 512], F32, "rdn") for _ in range(2)]
            ostg = [k.tile(st, [128, 8, 512], BF16, "ostg") for _ in range(2)]
            sc = 256 ** -0.5
            ei = 0
            def ld(g):
                k.load(at[g % 2][:], xT[:, :, g * 512:(g + 1) * 512].rearrange("c p n -> p c n"), at[g % 2])
            ld(0)
            for g in range(NG):
                a, Q, OS = at[g % 2], qx[g % 2], ostg[g % 2]
                if g + 1 < NG:
                    ld(g + 1)
                for c in range(8):
                    b = k.bank()
                    for kc in range(8):
                        k.mm(b, b[:, :], Wq[:, kc, c * 128:(c + 1) * 128], a[:, kc, :], kc == 0, [Wq.sub[c // 4], a])
                    k.copy("act" if c % 2 == 0 else "dve", Q[:, c, :], b[:, :], [b], [Q])
                for h in range(4):
                    E, rd = Es[ei % 2], rdn[ei % 2]
                    ei += 1
                    for mc in range(2):
                        b = k.bank()
                        for dc in range(2):
                            k.mm(b, b[:, :], kxT[:, 2 * h + dc, mc * 128:(mc + 1) * 128], Q[:, 2 * h + dc, :], dc == 0, [kxT, Q])
                        k.act(E[:, mc, :], b[:, :], AF.Exp, [b], [E], scale=sc)
                    b = k.bank()
                    for mc in range(2):
                        k.mm(b, b[:, :], onesb[:], E[:, mc, :], mc == 0, [onesb, E])
                    S.op("dve", lambda e: e.reciprocal(rd[:], b[:, :]), [b.res], [rd.res])
                    for dc in range(2):
                        b = k.bank()
                        for mc in range(2):
                            k.mm(b, b[:, :], vx[:, mc, (2 * h + dc) * 128:(2 * h + dc + 1) * 128], E[:, mc, :], mc == 0, [vx, E])
                        k.tt(OS[:, 2 * h + dc, :], b[:, :], rd[:], ALU.mult, [b, rd], [OS])
                k.store(oT[:, :, g * 512:(g + 1) * 512].rearrange("c p n -> p c n"), OS[:], OS)
            S.barrier()

    with ExitStack() as st:
        xr = [k.tile(st, [128, 2, D], F32, "xr") for _ in range(2)]
        xts = [k.tile(st, [128, 8, 256], BF16, "xts") for _ in range(2)]
        for g in range(L // 256):
            X, XT = xr[g % 2], xts[g % 2]
            k.load(X[:], x_in[g * 256:(g + 1) * 256, :].rearrange("(r p) d -> p r d", p=128), X)
            for r in range(2):
                for c0 in range(0, 8, 4):
                    b = k.bank()
                    for cc in range(4):
                        c = c0 + cc
                        k.tr(b, b[:, cc * 128:(cc + 1) * 128], X[:, r, c * 128:(c + 1) * 128], ident[:], [X, ident])
                    k.copy("act" if c0 == 0 else "dve", XT[:, c0:c0 + 4, r * 128:(r + 1) * 128],
                           b[:, :].rearrange("p (c n) -> p c n", n=128), [b], [XT])
            k.store(xT[:, :, g * 256:(g + 1) * 256].rearrange("c p n -> p c n"), XT[:], XT)
        S.barrier()

    if stop_after == "xt0":
        return k
    cur_x = x_in
    pp = 0
    done = False
    for l in range(depth):
        last = (l == depth - 1)
        phase_fm(xT, 8, L, wbf[("w_in_fm", l)], NFM, hT)
        if stop_after == "fm":
            break
        phase_tm(l)
        if stop_after == "proj":
            break
        import os
        only = os.environ.get("ONLY", "")
        if l + 1 < depth:
            convert_layer(l + 1)
        if only in ("", "dsa_sel", "dsa"):
            phase_dsa_select(l)
        if only in ("", "dsa"):
            phase_dsa_attn(l)
        if only in ("", "nsa"):
            phase_nsa(l)
        if only in ("", "moba"):
            phase_moba(l)
        if only in ("", "diff"):
            phase_diff(l)
        if stop_after == "mix":
            break
        nxt = xres[pp]
        phase_outln(oT, 8, wbf[("w_out", l)], lngb[l], 0, cur_x, nxt)
        cur_x = nxt
        pp ^= 1
        phase_cross(l)
        nxt = xres[pp]
        phase_outln(oT, 8, wbf[("xo", l)], lngb[l], 2, cur_x, nxt)
        cur_x = nxt
        pp ^= 1
        phase_fm(xT, 8, L, wbf[("w1", l)], 32, hmlp, relu2=True)
        nxt = out_d if last else xres[pp]
        phase_outln(hmlp, 32, wbf[("w2", l)], lngb[l], 4, cur_x, nxt)
        cur_x = nxt
        pp ^= 1
    if stop_after is not None:
        pass
    S.barrier()
    k.ninst = S.ninst
    return k


def host_consts(L, rel_bias):
    c = {}
    s = np.arange(128)[:, None]
    t = np.arange(128)[None, :]
    G = np.zeros((128, 16, 2, 128), np.float32)
    for d in range(2):
        bk = rel_bucket_np(t - s + 128 * d)
        G[:, :, d, :] = np.transpose(rel_bias[bk], (0, 2, 1))
    c["biasG"] = G
    M = np.zeros((128, 3, 128), np.float32)
    M[:, 0, :] = np.where(t - s >= 0, 0.0, NEGM)
    M[:, 1, :] = np.where(s > t, 0.0, NEGM)
    c["maskM"] = M
    c["cbias"] = np.ascontiguousarray(np.broadcast_to(rel_bias[31][None, :], (128, 16))).astype(np.float32)
    c["identf"] = np.eye(128, dtype=np.float32)
    pos = np.arange(L)
    c["ee64"] = (pos[None, :] // 64 == np.arange(64)[:, None]).astype(np.float32)
    c["ee256"] = (pos[None, :] // 256 == np.arange(16)[:, None]).astype(np.float32)
    n = np.arange(256)[:, None]
    cm = np.where((16 * n + 31 <= pos[None, :]) & (n < (L - 32) // 16 + 1), 0.0, NEGM).astype(np.float32)
    c["cmpneg"] = cm
    cs = np.arange(256) * 16
    ss = np.arange(1, 64) * 64
    ov = ((cs[:, None] <= ss[None, :] + 63) & (cs[:, None] + 31 >= ss[None, :])).astype(np.float32)
    ov[(L - 32) // 16 + 1:, :] = 0.0
    c["ovl"] = ov
    j = np.arange(1, 64)[None, :]
    cur = (pos // 64)[:, None]
    add = np.zeros((L, 63), np.float32)
    add = np.where((j == cur) | (j == cur - 1), 1e6 + 16.0 * j, add)
    add = np.where(j > cur, -1e6 - 16.0 * j, add)
    c["nsa_add"] = add.astype(np.float32)
    ma = np.zeros((128, 16, 4, 16), np.float32)
    for cu in range(16):
        ma[:, cu, :, cu:] = -1e30
    c["moba_add"] = ma
    tt_ = np.arange(128)[:, None]
    s_ = np.arange(128)[None, :]
    c["causneg"] = np.where(s_ <= tt_, 0.0, -1e30).astype(np.float32)
    return c


def host_weights(inp, depth):
    w = {}
    w["w_in_p"] = np.ascontiguousarray(inp["w_in"][:depth][:, :, FM_COLS + TM_COLS])
    for nme in ("w_out", "xq", "xk", "xv", "xo", "mlp_w1", "mlp_w2"):
        w[nme] = np.ascontiguousarray(inp[nme][:depth])
    gb = np.stack([inp["ln1_g"], inp["ln1_b"], inp["ln2_g"], inp["ln2_b"], inp["ln3_g"], inp["ln3_b"]], axis=1)[:depth]
    w["lngb"] = np.ascontiguousarray(np.broadcast_to(gb[:, None], (depth, 128, 6, D))).astype(np.float32)
    w1k = inp["nsa_w1_k"][:depth].reshape(depth, 32, 64, 64).transpose(0, 2, 1, 3)
    w1v = inp["nsa_w1_v"][:depth].reshape(depth, 32, 64, 64).transpose(0, 2, 1, 3)
    w["nsa_w1"] = np.ascontiguousarray(np.concatenate([w1k, w1v], axis=1))
    w["nsa_w2"] = np.ascontiguousarray(np.concatenate([inp["nsa_w2_k"][:depth], inp["nsa_w2_k"][:depth], inp["nsa_w2_v"][:depth]], axis=2))
    w["nsa_posT"] = np.ascontiguousarray(np.concatenate([inp["nsa_pos_k"][:depth].transpose(0, 2, 1),
                                                         inp["nsa_pos_v"][:depth].transpose(0, 2, 1)], axis=1))
    dl = np.stack([inp["diff_lq1"], inp["diff_lk1"], inp["diff_lq2"], inp["diff_lk2"]], axis=1)[:depth]
    w["diff_l"] = np.ascontiguousarray(np.broadcast_to(dl[:, None], (depth, 128, 4, 32))).astype(np.float32)
    w["diff_g"] = np.ascontiguousarray(np.broadcast_to(inp["diff_g"][:depth][:, None], (depth, 128, 64))).astype(np.float32)
    return w


_CACHE = {}


def kernel(**inputs):
    inp = {k_: np.asarray(v, dtype=np.float32) for k_, v in inputs.items()}
    L, depth = 4096, 4
    if "prog" not in _CACHE:
        _CACHE["prog"] = build(L, depth)
    kb = _CACHE["prog"]
    shared = {}
    shared.update(host_consts(L, inp["rel_bias"]))
    shared.update(host_weights(inp, depth))
    in_maps = []
    for b in range(8):
        m = dict(shared)
        m["x"] = np.ascontiguousarray(inp["x"][b])
        m["mem"] = np.ascontiguousarray(inp["mem"][b])
        in_maps.append(m)
    res = run_bass_kernel_spmd(kb.nc, in_maps, core_ids=list(range(8)))
    return np.stack([np.asarray(r["out"], dtype=np.float32) for r in res.results], axis=0)
```
